# Optimizing a Trainium2 kernel written in Bass

```python
import math
import jax, jax.numpy as jnp
from jax import lax
import numpy as np

D_MODEL = 4096
BATCH = 4
SEQ = 2048
DEPTH = 1
DEC_BATCH = 128
DEC_SEQ = 8
PAST_LEN = 16384
PAGE_SIZE = 128

N_META = 16
SSM_WIDTH = D_MODEL // 2
SSM_GROUP = 16
SSM_GROUPS = SSM_WIDTH // SSM_GROUP
SSM_STATE = 64
CONV_WIDTH = D_MODEL // 2
CONV_K = 3
D_FF = ((8 * D_MODEL + 3 * 256 - 1) // (3 * 256)) * 256
IN_COLS = SSM_WIDTH + 3 * CONV_WIDTH + 2 * D_MODEL
DT_MIN = 0.001
DT_MAX = 0.1
EPS = 1e-6

kernel_name = 'hybrid_s5_shortconv_decode_step'


def rmsnorm(x, g):
    x32 = x.astype(jnp.float32)
    y = x32 * lax.rsqrt(jnp.mean(x32 * x32, axis=-1, keepdims=True) + EPS)
    return (y * g.astype(jnp.float32)).astype(x.dtype)


def _scan_op(left, right):
    a1, b1 = left
    a2, b2 = right
    return a1 * a2, a2 * b1 + b2


def s5_mixer(u, s0_re, s0_im, lam_re, lam_im, log_dt, b_re, b_im, c_re, c_im, d_skip):
    f32 = jnp.float32
    n, l, _ = u.shape
    u32 = u.astype(f32).reshape(n, l, SSM_GROUPS, SSM_GROUP)
    lam = lax.complex(lam_re.astype(f32), lam_im.astype(f32))
    dt = jnp.exp(log_dt.astype(f32))[:, None]
    lam_bar = jnp.exp(lam * dt)
    b_bar = ((lam_bar - 1.0) / lam)[:, :, None] * lax.complex(b_re.astype(f32), b_im.astype(f32))
    bu = jnp.einsum('nlgh,gph->nlgp', u32.astype(jnp.complex64), b_bar)
    s0 = lax.complex(s0_re.astype(f32), s0_im.astype(f32))
    bu = bu.at[:, 0].add(lam_bar * s0)
    a = jnp.broadcast_to(lam_bar, bu.shape)
    _, s = lax.associative_scan(_scan_op, (a, bu), axis=1)
    c = lax.complex(c_re.astype(f32), c_im.astype(f32))
    y = jnp.real(jnp.einsum('nlgp,ghp->nlgh', s, c)) + d_skip.astype(f32).reshape(SSM_GROUPS, SSM_GROUP) * u32
    s_last = s[:, -1]
    return y.reshape(n, l, SSM_WIDTH), jnp.real(s_last), jnp.imag(s_last)


def short_conv_mixer(h, gate_b, gate_c, conv_state, conv_w):
    l = h.shape[1]
    z = gate_c * h
    zp = jnp.concatenate([conv_state.astype(z.dtype), z], axis=1)
    y = conv_w[0] * zp[:, 0:l]
    for k in range(1, CONV_K):
        y = y + conv_w[k] * zp[:, k:k + l]
    return gate_b * y, zp[:, -(CONV_K - 1):]


def trunk_layer(x, s_re, s_im, conv_state, g_pre_mix, w_in, lam_re, lam_im, log_dt, b_re, b_im,
                c_re, c_im, ssm_d, w_glu_v, w_glu_g, conv_w, w_conv_out, w_o, g_post_mix,
                g_pre_ffn, w_ffn_gate, w_ffn_up, w_ffn_down, g_post_ffn):
    hn = rmsnorm(x, g_pre_mix)
    proj = hn @ w_in
    o1 = SSM_WIDTH
    o2 = o1 + CONV_WIDTH
    o3 = o2 + CONV_WIDTH
    o4 = o3 + CONV_WIDTH
    o5 = o4 + D_MODEL
    u = proj[..., :o1]
    ch = proj[..., o1:o2]
    cb = proj[..., o2:o3]
    cc = proj[..., o3:o4]
    ga = proj[..., o4:o5]
    gb = proj[..., o5:]
    y_ssm, ns_re, ns_im = s5_mixer(u, s_re, s_im, lam_re, lam_im, log_dt, b_re, b_im, c_re, c_im, ssm_d)
    y_ssm = jax.nn.gelu(y_ssm).astype(x.dtype)
    y_a = (y_ssm @ w_glu_v) * jax.nn.sigmoid(y_ssm @ w_glu_g)
    y_conv, n_conv = short_conv_mixer(ch, cb, cc, conv_state, conv_w)
    y_b = y_conv @ w_conv_out
    merged = jax.nn.sigmoid(ga) * y_a + jax.nn.sigmoid(gb) * y_b
    x = x + rmsnorm(merged @ w_o, g_post_mix)
    hf = rmsnorm(x, g_pre_ffn)
    f = (jax.nn.silu(hf @ w_ffn_gate) * (hf @ w_ffn_up)) @ w_ffn_down
    x = x + rmsnorm(f, g_post_ffn)
    return x, ns_re, ns_im, n_conv


def setup_inputs(seed: int = 0) -> dict:
    key = jax.random.key(seed)
    ks = jax.random.split(key, 32)
    f32 = jnp.float32

    def nrm(k, shape, scale):
        return jax.random.normal(k, shape, f32) * scale

    def gain(k, shape):
        return 1.0 + 0.02 * jax.random.normal(k, shape, f32)

    n_idx = jnp.arange(SSM_STATE, dtype=f32)
    lam_im = jnp.broadcast_to(math.pi * n_idx, (DEPTH, SSM_GROUPS, SSM_STATE)) + nrm(ks[8], (DEPTH, SSM_GROUPS, SSM_STATE), 0.01)
    return {
        'x_prompt': nrm(ks[0], (BATCH, SEQ, D_MODEL), 1.0),
        'x_sample': nrm(ks[1], (DEC_BATCH, DEC_SEQ, D_MODEL), 1.0),
        'state_ssm_re': nrm(ks[2], (DEPTH, DEC_BATCH, SSM_GROUPS, SSM_STATE), 0.1),
        'state_ssm_im': nrm(ks[3], (DEPTH, DEC_BATCH, SSM_GROUPS, SSM_STATE), 0.1),
        'state_conv': nrm(ks[4], (DEPTH, DEC_BATCH, CONV_K - 1, CONV_WIDTH), 1.0),
        'meta_tokens': nrm(ks[5], (N_META, D_MODEL), 1.0),
        'g_pre_mix': gain(ks[6], (DEPTH, D_MODEL)),
        'w_in': nrm(ks[7], (DEPTH, D_MODEL, IN_COLS), D_MODEL ** -0.5),
        'ssm_lambda_re': -0.5 + nrm(ks[9], (DEPTH, SSM_GROUPS, SSM_STATE), 0.01),
        'ssm_lambda_im': lam_im,
        'ssm_log_dt': jax.random.uniform(ks[10], (DEPTH, SSM_GROUPS), f32, math.log(DT_MIN), math.log(DT_MAX)),
        'ssm_b_re': nrm(ks[11], (DEPTH, SSM_GROUPS, SSM_STATE, SSM_GROUP), (2 * SSM_GROUP) ** -0.5),
        'ssm_b_im': nrm(ks[12], (DEPTH, SSM_GROUPS, SSM_STATE, SSM_GROUP), (2 * SSM_GROUP) ** -0.5),
        'ssm_c_re': nrm(ks[13], (DEPTH, SSM_GROUPS, SSM_GROUP, SSM_STATE), (2 * SSM_STATE) ** -0.5),
        'ssm_c_im': nrm(ks[14], (DEPTH, SSM_GROUPS, SSM_GROUP, SSM_STATE), (2 * SSM_STATE) ** -0.5),
        'ssm_d': nrm(ks[15], (DEPTH, SSM_WIDTH), 1.0),
        'w_glu_v': nrm(ks[16], (DEPTH, SSM_WIDTH, D_MODEL), SSM_WIDTH ** -0.5),
        'w_glu_g': nrm(ks[17], (DEPTH, SSM_WIDTH, D_MODEL), SSM_WIDTH ** -0.5),
        'conv_w': nrm(ks[18], (DEPTH, CONV_K, CONV_WIDTH), CONV_K ** -0.5),
        'w_conv_out': nrm(ks[19], (DEPTH, CONV_WIDTH, D_MODEL), CONV_WIDTH ** -0.5),
        'w_o': nrm(ks[20], (DEPTH, D_MODEL, D_MODEL), D_MODEL ** -0.5),
        'g_post_mix': gain(ks[21], (DEPTH, D_MODEL)),
        'g_pre_ffn': gain(ks[22], (DEPTH, D_MODEL)),
        'w_ffn_gate': nrm(ks[23], (DEPTH, D_MODEL, D_FF), D_MODEL ** -0.5),
        'w_ffn_up': nrm(ks[24], (DEPTH, D_MODEL, D_FF), D_MODEL ** -0.5),
        'w_ffn_down': nrm(ks[25], (DEPTH, D_FF, D_MODEL), D_FF ** -0.5),
        'g_post_ffn': gain(ks[26], (DEPTH, D_MODEL)),
    }


def reference(x_prompt, x_sample, state_ssm_re, state_ssm_im, state_conv, meta_tokens, g_pre_mix, w_in,
              ssm_lambda_re, ssm_lambda_im, ssm_log_dt, ssm_b_re, ssm_b_im, ssm_c_re, ssm_c_im, ssm_d,
              w_glu_v, w_glu_g, conv_w, w_conv_out, w_o, g_post_mix, g_pre_ffn, w_ffn_gate, w_ffn_up,
              w_ffn_down, g_post_ffn):
    nb = x_prompt.shape[0]
    meta = jnp.broadcast_to(meta_tokens.astype(x_prompt.dtype)[None], (nb, N_META, D_MODEL))
    xp = jnp.concatenate([meta, x_prompt], axis=1)
    xs = x_sample
    zero_s = jnp.zeros((nb, SSM_GROUPS, SSM_STATE), jnp.float32)
    zero_c = jnp.zeros((nb, CONV_K - 1, CONV_WIDTH), xp.dtype)
    p_re, p_im, p_cv, s_re, s_im, s_cv = [], [], [], [], [], []
    for i in range(DEPTH):
        lw = (g_pre_mix[i], w_in[i], ssm_lambda_re[i], ssm_lambda_im[i], ssm_log_dt[i], ssm_b_re[i],
              ssm_b_im[i], ssm_c_re[i], ssm_c_im[i], ssm_d[i], w_glu_v[i], w_glu_g[i], conv_w[i],
              w_conv_out[i], w_o[i], g_post_mix[i], g_pre_ffn[i], w_ffn_gate[i], w_ffn_up[i],
              w_ffn_down[i], g_post_ffn[i])
        xp, a_re, a_im, a_cv = trunk_layer(xp, zero_s, zero_s, zero_c, *lw)
        xs, b_re, b_im, b_cv = trunk_layer(xs, state_ssm_re[i], state_ssm_im[i], state_conv[i], *lw)
        p_re.append(a_re)
        p_im.append(a_im)
        p_cv.append(a_cv)
        s_re.append(b_re)
        s_im.append(b_im)
        s_cv.append(b_cv)
    y_prompt = xp[:, N_META:]
    return (y_prompt, xs, jnp.stack(p_re), jnp.stack(p_im), jnp.stack(p_cv),
            jnp.stack(s_re), jnp.stack(s_im), jnp.stack(s_cv))
```

```python
import math
from contextlib import ExitStack

import numpy as np
import concourse.bass as bass
import concourse.mybir as mybir
from concourse.bass_utils import run_bass_kernel_spmd

F32 = mybir.dt.float32
BF16 = mybir.dt.bfloat16
AF = mybir.ActivationFunctionType
ALU = mybir.AluOpType

D = 4096
KT = 32
SW = 2048
DFF = 11008
FT = 86
NT = 580
NP = 516
H = 290
HP = 258
EPS = 1e-6
TB = 16
FCH = 24
PI = math.pi
_DBG_STAGE = 4
_DBG_SUB = 0


class _Sem:
    def __init__(self, h):
        self.h = h
        self.val = 0


class _Eng:
    def __init__(self, name, h, sem):
        self.name = name
        self.h = h
        self.sem = sem
        self.waited = {}


class Region:
    __slots__ = ("w", "r")

    def __init__(self):
        self.w = None
        self.r = {}


class Sync:
    def __init__(self, nc, es):
        self.nc = nc
        mk = lambda n: _Sem(es.enter_context(nc.semaphore(n)))
        self.E = {
            "pe": _Eng("pe", nc.tensor, mk("s_pe")),
            "act": _Eng("act", nc.scalar, mk("s_act")),
            "dve": _Eng("dve", nc.vector, mk("s_dve")),
            "pool": _Eng("pool", nc.gpsimd, mk("s_pool")),
            "sp": _Eng("sp", nc.sync, mk("s_sp")),
        }
        self.sp_slots = [mk("s_d%d" % i) for i in range(12)]
        self.sp_i = 0

    def _wait(self, eng, st, self_sync=True):
        s, v = st
        if eng.waited.get(s, 0) >= v:
            return
        if s is eng.sem and (eng.name == "pe" or not self_sync or v > s.val):
            return
        eng.h.wait_ge(s.h, v)
        eng.waited[s] = v

    def _deps(self, eng, reads, writes, self_sync=True):
        for r in reads:
            if r.w:
                self._wait(eng, r.w, self_sync)
        for r in writes:
            if r.w:
                self._wait(eng, r.w, self_sync)
            for st in r.r.values():
                self._wait(eng, st, self_sync)

    def _mark(self, st, reads, writes):
        for r in writes:
            r.w = st
            r.r = {}
        for r in reads:
            r.r[st[0]] = st

    def op(self, en, fn, reads=(), writes=(), signal=True, self_sync=True):
        eng = self.E[en]
        self._deps(eng, reads, writes, self_sync)
        ins = fn()
        if signal:
            eng.sem.val += 1
            ins.then_inc(eng.sem.h, 1)
            st = (eng.sem, eng.sem.val)
        else:
            st = (eng.sem, eng.sem.val + 1)
        self._mark(st, reads, writes)
        return ins

    def dma(self, qn, out, in_, reads=(), writes=(), slot=None, **kw):
        q = self.E[qn]
        if slot is None:
            slot = self.sp_slots[self.sp_i]
            self.sp_i = (self.sp_i + 1) % len(self.sp_slots)
        if slot.val:
            self._wait(q, (slot, slot.val))
        self._deps(q, reads, writes)
        ins = q.h.dma_start(out=out, in_=in_, **kw)
        slot.val += 16
        ins.then_inc(slot.h, 16)
        st = (slot, slot.val)
        self._mark(st, reads, writes)
        return ins

    def barrier(self, extra_slots=()):
        names = ["pe", "act", "dve", "sp"]
        for a in names:
            ea = self.E[a]
            for b in names:
                if a == b:
                    continue
                eb = self.E[b]
                if eb.sem.val:
                    self._wait(ea, (eb.sem, eb.sem.val))
            for s in list(self.sp_slots) + list(extra_slots):
                if s.val:
                    self._wait(ea, (s, s.val))


def build_nc():
    nc = bass.Bass("TRN2", target_bir_lowering=False)
    dt_in = lambda n, s: nc.dram_tensor(n, s, F32, kind="ExternalInput").ap()
    dt_out = lambda n, s: nc.dram_tensor(n, s, F32, kind="ExternalOutput").ap()

    xm = dt_in("xm", [2, NT, D])
    xp = dt_in("xp", [2, NP, D])
    sst_in = [dt_in("sst_re_in", [16 * 64, 128]), dt_in("sst_im_in", [16 * 64, 128])]
    scv_in = dt_in("scv_in", [32, SW])
    ident_d = dt_in("ident", [128, 128])
    gcols_d = dt_in("gcols", [128, 4 * 32])
    convw_d = dt_in("convw", [128, 48])
    dcol_d = dt_in("dcol", [128, 16])
    l1_d = dt_in("l1", [3, 128, 64])
    l2_d = dt_in("l2", [5, 128, 2048])
    cm_d = dt_in("cm", [128, 4096])
    win = dt_in("win", [128, 128, 4096])
    wgv = dt_in("wgv", [32, 128, 2048])
    wgg = dt_in("wgg", [32, 128, 2048])
    wco = dt_in("wco", [32, 128, 2048])
    wo = dt_in("wo", [32, 128, 4096])
    wfg = dt_in("wfg", [FT, 128, 4096])
    wfu = dt_in("wfu", [FT, 128, 4096])
    wfd = dt_in("wfd", [32, 128, FT * 128])

    y_out = dt_out("y", [2, NT, D])
    pst_out = [dt_out("pst_re", [64, 128]), dt_out("pst_im", [64, 128])]
    pcv_out = dt_out("pcv", [2, SW])
    sst_out = [dt_out("sst_re", [16 * 64, 128]), dt_out("sst_im", [16 * 64, 128])]
    scv_out = dt_out("scv", [32, SW])
    x1s = nc.dram_tensor("x1s", [NT, D], F32).ap()

    with ExitStack() as es:
        sb = lambda n, s, d: es.enter_context(nc.sbuf_tensor("sb_" + n, s, d))
        S = Sync(nc, es)
        op, dma = S.op, S.dma
        w_slots = [_Sem(es.enter_context(nc.semaphore("s_w%d" % i))) for i in range(3)]

        ident = sb("ident", [128, 128], F32)
        ones = sb("ones", [128, 128], F32)
        gcols = sb("gcols", [128, 4, 32], F32)
        convw = sb("convw", [128, 3, 16], F32)
        dcol = sb("dcol", [128, 16], F32)
        CA = sb("CA", [128, 2, 64], F32)
        CB = sb("CB", [128, 2, 64], F32)
        Bt = sb("Bt", [128, 2, 16, 128], BF16)
        Xp = sb("Xp", [128, 2, 2, 64], F32)
        CA2 = sb("CA2", [128, 2, 64], F32)
        CB2 = sb("CB2", [128, 2, 64], F32)
        Bt2 = sb("Bt2", [128, 2, 16, 128], BF16)
        ulast = sb("ulast", [128, 2, 16], BF16)
        ypend = sb("ypend", [128, 16, 1], BF16)
        Xinit = sb("Xinit", [128, 2, 2, 64], F32)
        Zhist = sb("Zhist", [128, 16, 2], F32)
        ZsInit = sb("ZsInit", [128, 16, 16, 2], F32)
        ZsOut = sb("ZsOut", [128, 16, 16, 2], F32)
        rstdb = sb("rstdb", [128, NT], F32)
        tA = sb("tA", [128, NT], F32)
        tB = sb("tB", [128, NT], F32)
        tC = sb("tC", [128, NT], F32)
        Zp = sb("Zp", [128, NP + 2], F32)
        Zsp = sb("Zsp", [128, 8, 10], F32)
        ss = sb("ss", [128, 4], F32)
        stg = sb("stg", [128, 2, 128], F32)
        tY = sb("tY", [128, 16, 32], F32)
        t1r = sb("t1r", [128, 256], F32)
        t2r = sb("t2r", [128, 256], F32)
        R1f = sb("R1", [128, 18560], F32)
        R2b = sb("R2", [128, 18560], BF16)
        Wr = [sb("wr%d" % i, [128, 4096], BF16) for i in range(3)]
        Df = sb("Dd", [128, 8192], F32)
        PS = es.enter_context(nc.psum_tensor("PS", [128, 8, 512], F32))

        R1b = R1f[:].bitcast(BF16)
        hnT = R1b[:, 0:18560].rearrange("p (k n) -> p k n", k=32)
        ygT = R1b[:, 18560:27840].rearrange("p (k n) -> p k n", k=16)
        ycT = R1b[:, 27840:37120].rearrange("p (k n) -> p k n", k=16)
        OG = R1f[:].rearrange("p (k n) -> p k n", k=32)
        mgT = R2b[:].rearrange("p (k n) -> p k n", k=32)
        xtok = Df[:, 0:4096]
        xtok2 = Df[:, 4096:8192]
        BU = [Df[:, i * 2048:(i + 1) * 2048].rearrange("p (t c e) -> p t c e", t=TB, c=2) for i in range(2)]
        BUP = [Df[:, i * 4096:(i + 1) * 4096].rearrange("p (t c e) -> p t c e", t=32, c=2) for i in range(2)]
        Cm = Df[:, 4096:8192].rearrange("p (e c m) -> p e c m", e=64, c=2)
        hbuf = Df[:].bitcast(BF16)[:, 0:FCH * NT].rearrange("p (k n) -> p k n", k=FCH)

        class RG:
            pass
        rg = RG()
        for n in ("ident ones gcols convw dcol CA CB CA2 CB2 Bt2 ulast ypend Bt Xp Xinit Zhist ZsInit ZsOut rstdb tA tB tC Zp Zsp ss stg tY t1r t2r "
                  "hnT yg yc OG mg xtok xtok2 Cm hbuf x1s").split():
            setattr(rg, n, Region())
        rg.wr = [Region() for _ in range(3)]
        rg.BU = [Region() for _ in range(2)]
        rg.pm = [Region() for _ in range(2)]
        rg.pb = [Region() for _ in range(2)]
        rg.pc = Region()
        rg.pt = Region()

        pm = lambda s, w: PS[:, 2 * s:2 * s + 2, 0:w]
        st_ = {"w": 0, "p": 0, "pt": 0}

        def V2(ap, w):
            return ap.rearrange("p (h n) -> p h n", h=2) if w else ap

        def mm_group(wsrc, nkt, rhs_fn, rhs_regs, halves):
            slot = st_["w"]
            st_["w"] = (slot + 1) % 3
            ps = st_["p"]
            st_["p"] = (ps + 1) % 2
            nel = nkt * 128
            bsz = max(b_ for b_ in range(128, 2049, 128) if nel % b_ == 0)
            dma("pool", Wr[slot][:, 0:nel].rearrange("p (a b) -> p a b", b=bsz),
                wsrc.rearrange("p (a b) -> p a b", b=bsz),
                reads=(), writes=(rg.wr[slot],), slot=w_slots[slot])
            nh = len(halves)
            for kt in range(nkt):
                for h, (c0, c1) in enumerate(halves):
                    last = (kt == nkt - 1) and (h == nh - 1)
                    op("pe", lambda: nc.tensor.matmul(PS[:, 2 * ps + h, 0:c1 - c0],
                                                      lhsT=Wr[slot][:, kt * 128:(kt + 1) * 128],
                                                      rhs=rhs_fn(kt)[:, c0:c1],
                                                      start=(kt == 0), stop=(kt == nkt - 1)),
                       reads=[rg.wr[slot]] + list(rhs_regs), writes=[rg.pm[ps]], signal=last)
            return ps

        MH = [(0, H), (H, NT)]
        PH = [(0, HP), (HP, NP)]

        def setup():
            dma("sp", ident[:], ident_d, writes=[rg.ident])
            dma("sp", gcols[:], gcols_d.rearrange("p (a b) -> p a b", a=4), writes=[rg.gcols])
            dma("sp", convw[:], convw_d.rearrange("p (a b) -> p a b", a=3), writes=[rg.convw])
            dma("sp", dcol[:], dcol_d, writes=[rg.dcol])
            op("dve", lambda: nc.vector.memset(ones[:], 1.0), writes=[rg.ones])
            op("dve", lambda: nc.vector.memset(Xp[:], 0.0), writes=[rg.Xp])
            op("dve", lambda: nc.vector.memset(ulast[:], 0.0), writes=[rg.ulast])
            op("dve", lambda: nc.vector.memset(Zhist[:], 0.0), writes=[rg.Zhist])

            rtmp = Region()

            def lam_bar(lre, lim, ldt, tmp, n):
                dtv, ldr, ldi, er, a1, a2 = tmp[:6]
                R = [rtmp]
                op("act", lambda: nc.scalar.activation(out=dtv, in_=ldt, func=AF.Exp), reads=R, writes=R)
                op("dve", lambda: nc.vector.tensor_tensor(out=ldr, in0=lre, in1=dtv, op=ALU.mult), reads=R, writes=R)
                op("dve", lambda: nc.vector.tensor_tensor(out=ldi, in0=lim, in1=dtv, op=ALU.mult), reads=R, writes=R)
                op("act", lambda: nc.scalar.activation(out=er, in_=ldr, func=AF.Exp), reads=R, writes=R)
                TS = lambda o, i, s1, s2, o0, o1=None: op("dve", lambda: (nc.vector.tensor_scalar(out=o, in0=i, scalar1=s1, scalar2=s2, op0=o0, op1=o1)
                                                                         if o1 is not None else
                                                                         nc.vector.tensor_scalar(out=o, in0=i, scalar1=s1, scalar2=None, op0=o0)),
                                                         reads=R, writes=R)
                TT_ = lambda o, a, b, o_: op("dve", lambda: nc.vector.tensor_tensor(out=o, in0=a, in1=b, op=o_), reads=R, writes=R)
                TS(a1, ldi, 1.0 / (2 * PI), None, ALU.mult)
                op("dve", lambda: nc.vector.tensor_copy(out=a2.bitcast(mybir.dt.int32), in_=a1), reads=R, writes=R)
                op("dve", lambda: nc.vector.tensor_copy(out=a1, in_=a2.bitcast(mybir.dt.int32)), reads=R, writes=R)
                op("dve", lambda: nc.vector.scalar_tensor_tensor(out=a1, in0=a1, scalar=-2 * PI, in1=ldi, op0=ALU.mult,
                                                                 op1=ALU.add), reads=R, writes=R)
                TS(dtv, a1, PI, 2 * PI, ALU.is_gt, ALU.mult)
                TT_(a1, a1, dtv, ALU.subtract)
                TS(dtv, a1, -PI, 2 * PI, ALU.is_lt, ALU.mult)
                TT_(a1, a1, dtv, ALU.add)
                TS(a2, a1, PI / 2, None, ALU.add)
                TS(dtv, a2, PI, 2 * PI, ALU.is_gt, ALU.mult)
                TT_(a2, a2, dtv, ALU.subtract)
                op("act", lambda: nc.scalar.activation(out=a1, in_=a1, func=AF.Sin), reads=R, writes=R)
                op("act", lambda: nc.scalar.activation(out=a2, in_=a2, func=AF.Sin), reads=R, writes=R)
                op("dve", lambda: nc.vector.tensor_tensor(out=a2, in0=a2, in1=er, op=ALU.mult), reads=R, writes=R)
                op("dve", lambda: nc.vector.tensor_tensor(out=a1, in0=a1, in1=er, op=ALU.mult), reads=R, writes=R)
                return a2, a1

            t1 = [Df[:, i * 64:(i + 1) * 64] for i in range(12)]
            for i in range(3):
                dma("sp", t1[i], l1_d[i], writes=[rtmp])
            lbr, lbi = lam_bar(t1[0], t1[1], t1[2], t1[3:9], 64)
            R = [rtmp]
            for c in range(2):
                op("dve", lambda: nc.vector.tensor_copy(out=CA[:, c, :], in_=lbr), reads=R, writes=[rg.CA])
            op("dve", lambda: nc.vector.tensor_copy(out=CB[:, 0, :], in_=lbi), reads=R, writes=[rg.CB])
            op("dve", lambda: nc.vector.tensor_scalar(out=CB[:, 1, :], in0=lbi, scalar1=-1.0, scalar2=None,
                                                      op0=ALU.mult), reads=R, writes=[rg.CB])
            q1, q2 = t1[9], t1[10]
            TT1 = lambda o, a, b, o_, w=(rtmp,): op("dve", lambda: nc.vector.tensor_tensor(out=o, in0=a, in1=b, op=o_), reads=R, writes=list(w))
            TT1(q1, lbr, lbr, ALU.mult)
            TT1(q2, lbi, lbi, ALU.mult)
            TT1(q1, q1, q2, ALU.subtract)
            TT1(q2, lbr, lbi, ALU.mult)
            for c in range(2):
                op("dve", lambda: nc.vector.tensor_copy(out=CA2[:, c, :], in_=q1), reads=R, writes=[rg.CA2])
            op("dve", lambda: nc.vector.tensor_scalar(out=CB2[:, 0, :], in0=q2, scalar1=2.0, scalar2=None, op0=ALU.mult),
               reads=R, writes=[rg.CB2])
            op("dve", lambda: nc.vector.tensor_scalar(out=CB2[:, 1, :], in0=q2, scalar1=-2.0, scalar2=None, op0=ALU.mult),
               reads=R, writes=[rg.CB2])
            S.barrier()
            t2 = [R1f[:, i * 2048:(i + 1) * 2048] for i in range(9)] + [Df[:, i * 2048:(i + 1) * 2048] for i in range(4)]
            for i in range(5):
                dma("sp", t2[i], l2_d[i], writes=[rtmp])
            lre, lim, ldt, bre, bim = t2[0:5]
            lbr, lbi = lam_bar(lre, lim, ldt, t2[5:11], 2048)
            nr, m2 = t2[11], t2[12]
            f1, f2 = t2[5], t2[6]
            TT = lambda o, a, b, o_: op("dve", lambda: nc.vector.tensor_tensor(out=o, in0=a, in1=b, op=o_), reads=R, writes=R)
            op("dve", lambda: nc.vector.tensor_scalar(out=nr, in0=lbr, scalar1=-1.0, scalar2=None, op0=ALU.add),
               reads=R, writes=R)
            TT(m2, lre, lre, ALU.mult)
            TT(f1, lim, lim, ALU.mult)
            TT(m2, m2, f1, ALU.add)
            op("dve", lambda: nc.vector.reciprocal(out=m2, in_=m2), reads=R, writes=R)
            TT(f1, nr, lre, ALU.mult)
            TT(f2, lbi, lim, ALU.mult)
            TT(f1, f1, f2, ALU.add)
            TT(f1, f1, m2, ALU.mult)
            TT(f2, lbi, lre, ALU.mult)
            TT(nr, nr, lim, ALU.mult)
            TT(f2, f2, nr, ALU.subtract)
            TT(f2, f2, m2, ALU.mult)
            X1, X2, sc1 = t2[0], t2[1], t2[2]
            TT(nr, f1, bre, ALU.mult)
            TT(m2, f2, bim, ALU.mult)
            TT(X1, nr, m2, ALU.subtract)
            TT(nr, f1, bim, ALU.mult)
            TT(m2, f2, bre, ALU.mult)
            TT(nr, nr, m2, ALU.add)
            op("dve", lambda: nc.vector.tensor_scalar(out=X2, in0=nr, scalar1=-1.0, scalar2=None, op0=ALU.mult),
               reads=R, writes=R)
            flat = lambda t: t.rearrange("p a b -> p (a b)")
            op("dve", lambda: nc.vector.tensor_copy(out=flat(Bt[:, 0]), in_=X1), reads=R, writes=[rg.Bt])
            op("dve", lambda: nc.vector.tensor_copy(out=flat(Bt[:, 1]), in_=X2), reads=R, writes=[rg.Bt])
            TT(nr, lbr, X1, ALU.mult)
            TT(m2, lbi, X2, ALU.mult)
            op("dve", lambda: nc.vector.tensor_tensor(out=flat(Bt2[:, 0]), in0=nr, in1=m2, op=ALU.add), reads=R, writes=[rg.Bt2])
            TT(nr, lbr, X2, ALU.mult)
            TT(m2, lbi, X1, ALU.mult)
            op("dve", lambda: nc.vector.tensor_tensor(out=flat(Bt2[:, 1]), in0=nr, in1=m2, op=ALU.subtract), reads=R,
               writes=[rg.Bt2])
            S.barrier()
            dma("sp", Df[0:32, 0:2048], scv_in, writes=[rg.xtok])
            for ct in range(16):
                op("pe", lambda: nc.tensor.transpose(PS[:, 7, 0:32], Df[0:32, ct * 128:(ct + 1) * 128], ident[0:32, 0:32]),
                   reads=[rg.xtok, rg.ident], writes=[rg.pt])
                op("act", lambda: nc.scalar.activation(out=ZsInit[:, ct].rearrange("p b k -> p (b k)"), in_=PS[:, 7, 0:32],
                                                       func=AF.Copy), reads=[rg.pt], writes=[rg.ZsInit])
            S.barrier()

        def tiles_of(n):
            return [(i, min(128, n - i)) for i in range(0, n, 128)]

        def rstd_from_ss(col):
            op("dve", lambda: nc.vector.tensor_scalar(out=ss[:, col:col + 1], in0=ss[:, col:col + 1], scalar1=1.0 / D,
                                                      scalar2=EPS, op0=ALU.mult, op1=ALU.add), reads=[rg.ss], writes=[rg.ss])
            op("act", lambda: nc.scalar.activation(out=ss[:, col:col + 1], in_=ss[:, col:col + 1], func=AF.Sqrt),
               reads=[rg.ss], writes=[rg.ss])
            op("dve", lambda: nc.vector.reciprocal(out=ss[:, col:col + 1], in_=ss[:, col:col + 1]),
               reads=[rg.ss], writes=[rg.ss])

        def transposes_to_T(src, sz, gi, dstT, dreg, c0):
            for g4 in range(8):
                for j in range(4):
                    kt = g4 * 4 + j
                    op("pe", lambda: nc.tensor.transpose(PS[:, 7, j * 128:j * 128 + sz], src[0:sz, kt * 128:(kt + 1) * 128],
                                                         ident[0:sz, 0:sz]),
                       reads=[rg.xtok, rg.xtok2, rg.ident], writes=[rg.pt])
                op("dve", lambda: nc.vector.tensor_tensor(
                    out=dstT[:, g4 * 4:g4 * 4 + 4, c0:c0 + sz],
                    in0=PS[:, 7, :].rearrange("p (j n) -> p j n", j=4)[:, :, 0:sz],
                    in1=gcols[:, gi, g4 * 4:g4 * 4 + 4].unsqueeze(2).to_broadcast([128, 4, sz]), op=ALU.mult),
                   reads=[rg.pt, rg.gcols], writes=[dreg])

        def prep(xsrc, ntok):
            for (r0, sz) in tiles_of(ntok):
                dma("sp", xtok[0:sz, :], xsrc[r0:r0 + sz, :], writes=[rg.xtok])
                op("dve", lambda: nc.vector.memset(ss[:, 0:1], 0.0), writes=[rg.ss])
                op("act", lambda: nc.scalar.activation(out=xtok2[0:sz, :], in_=xtok[0:sz, :], func=AF.Square,
                                                       accum_out=ss[0:sz, 0:1]),
                   reads=[rg.xtok], writes=[rg.xtok2, rg.ss])
                rstd_from_ss(0)
                op("act", lambda: nc.scalar.activation(out=xtok2[0:sz, :], in_=xtok[0:sz, :], func=AF.Copy,
                                                       scale=ss[0:sz, 0:1]),
                   reads=[rg.xtok, rg.ss], writes=[rg.xtok2])
                transposes_to_T(xtok2, sz, 0, hnT, rg.hnT, r0)

        def ssm_B(uT, ureg, cols, n, sample, bufi, first=False, ubuf=0):
            BUv = st_["BU"][bufi]
            rB = st_["BUreg"][bufi]
            tb = st_["TB"]
            ncl = 128 // tb
            for pas in range(16 // ncl):
                for q in range(4):
                    par = q % 2
                    qh = q // 2
                    pbv = PS[:, 4 + par, :].rearrange("p (j c t) -> p j c t", j=2 * ncl, c=2)
                    for cl in range(ncl):
                        ct = ncl * pas + cl
                        j = 2 * cl + qh
                        for c in range(2):
                            rows = slice(32 * q, 32 * q + 32)
                            if not sample:
                                items = [(Bt, uT[rows, ct, cols:cols + n], pbv[:, j, c, 0:n])]
                                if first:
                                    items.append((Bt2, ulast[rows, ubuf, ct:ct + 1], pbv[:, j, c, 0:1]))
                                    items.append((Bt2, uT[rows, ct, cols:cols + n - 1], pbv[:, j, c, 1:n]))
                                else:
                                    items.append((Bt2, uT[rows, ct, cols - 1:cols + n - 1], pbv[:, j, c, 0:n]))
                            else:
                                items = [(Bt, uT[rows, ct, cols + 8 * b_:cols + 8 * b_ + 8],
                                          pbv[:, j, c, b_:16:2]) for b_ in range(2)]
                            for ii, (bm, rhs, o_) in enumerate(items):
                                st_flag = True if sample else (ii == 0)
                                sp_flag = True if sample else (ii == len(items) - 1)
                                op("pe", lambda: nc.tensor.matmul(o_, lhsT=bm[rows, c, ct, :], rhs=rhs,
                                                                  start=st_flag, stop=sp_flag, tile_position=(32 * q, 0)),
                                   reads=[rg.Bt, rg.Bt2, rg.ulast, ureg], writes=[rg.pb[par]],
                                   signal=(cl == ncl - 1 and c == 1 and ii == len(items) - 1))
                for par in range(2):
                    pbv = PS[:, 4 + par, :].rearrange("p (j c t) -> p j c t", j=2 * ncl, c=2)
                    e0 = 4 * ncl * pas + par
                    op("act", lambda: nc.scalar.activation(
                        out=BUv[:, 0:n, :, e0:e0 + 4 * ncl - 1:2].rearrange("p t c e -> p e c t"),
                        in_=pbv[:, :, :, 0:n], func=AF.Copy), reads=[rg.pb[par]], writes=[rB])

        def ssm_R(n, sample, bufi, prev_ap, inter=None, inter_at=()):
            BUv = st_["BU"][bufi]
            rB = st_["BUreg"][bufi]
            if not sample:
                assert n % 2 == 0
                nsteps = n // 2
                halves = [lambda a, h=h: a[:, h] for h in range(2)]
                get_prev = lambda j: Xp[:] if j == 0 else BUv[:, 2 * (j - 1):2 * j]
                get_cur = lambda j: BUv[:, 2 * j:2 * j + 2]
                ca = CA2[:].unsqueeze(1).to_broadcast([128, 2, 2, 64])
                cb = CB2[:].unsqueeze(1).to_broadcast([128, 2, 2, 64])
                t1 = t1r[:].rearrange("p (b c e) -> p b c e", b=2, c=2)
                t2 = t2r[:].rearrange("p (b c e) -> p b c e", b=2, c=2)
                sw = lambda a: a[:, ::-1, :]
            else:
                nsteps = 8
                halves = [lambda a, h=h: a[:, h] for h in range(2)]
                get_prev = lambda j: Xinit[:] if j == 0 else BUv[:, 2 * (j - 1):2 * j]
                get_cur = lambda j: BUv[:, 2 * j:2 * j + 2]
                ca = CA[:].unsqueeze(1).to_broadcast([128, 2, 2, 64])
                cb = CB[:].unsqueeze(1).to_broadcast([128, 2, 2, 64])
                t1 = t1r[:].rearrange("p (b c e) -> p b c e", b=2, c=2)
                t2 = t2r[:].rearrange("p (b c e) -> p b c e", b=2, c=2)
                sw = lambda a: a[:, ::-1, :]
            RR = [rB, rg.BU[0], rg.BU[1], rg.Xp, rg.Xinit, rg.CA, rg.CB, rg.CA2, rg.CB2]
            for j in range(nsteps):
                prev, cur = get_prev(j), get_cur(j)
                ss_ = (j == 0)
                for hf in halves:
                    op("dve", lambda: nc.vector.tensor_tensor(out=hf(t1), in0=hf(prev), in1=hf(ca), op=ALU.mult),
                       reads=RR, writes=[rg.t1r], self_sync=ss_, signal=False)
                for hf in halves:
                    op("dve", lambda: nc.vector.tensor_tensor(out=hf(t2), in0=sw(hf(prev)), in1=hf(cb), op=ALU.mult),
                       reads=RR, writes=[rg.t2r], self_sync=ss_, signal=False)
                for hf in halves:
                    op("dve", lambda: nc.vector.tensor_tensor(out=hf(t1), in0=hf(t1), in1=hf(t2), op=ALU.add),
                       reads=[rg.t1r, rg.t2r], writes=[rg.t1r], self_sync=ss_, signal=False)
                for hi, hf in enumerate(halves):
                    op("dve", lambda: nc.vector.tensor_tensor(out=hf(cur), in0=hf(cur), in1=hf(t1), op=ALU.add),
                       reads=[rg.t1r, rB], writes=[rB], self_sync=ss_, signal=(hi == 1))
                if inter and j in inter_at:
                    inter.pop(0)()

        def ssm_Y(uT, ureg, cols, n, sample, bufi, hold_last=False):
            BUv = st_["BU"][bufi]
            rB = st_["BUreg"][bufi]
            pcv = PS[:, 6, 0:16 * st_["TB"]].rearrange("p (k t) -> p k t", k=16)
            for ct in range(16):
                for q in range(4):
                    e = 4 * ct + q
                    for c in range(2):
                        op("pe", lambda: nc.tensor.matmul(pcv[32 * q:32 * q + 32, ct, 0:n], lhsT=Cm[:, e, c, :],
                                                          rhs=BUv[:, 0:n, c, e], start=(c == 0), stop=(c == 1),
                                                          tile_position=(0, 32 * q)),
                           reads=[rg.Cm, rB], writes=[rg.pc], signal=(ct == 15 and q == 3 and c == 1))
            if not sample:
                uv = uT[:, :, cols:cols + n]
                tyv = tY[:, :, 0:n]
                pv = pcv[:, :, 0:n]
                dbc = dcol[:].unsqueeze(2).to_broadcast([128, 16, n])
            else:
                uv = uT[:, :, cols:cols + 16].rearrange("p k (b t) -> p k t b", b=2)
                tyv = tY[:, :, 0:16].rearrange("p k (t b) -> p k t b", b=2)
                pv = pcv[:, :, 0:16].rearrange("p k (t b) -> p k t b", b=2)
                dbc = dcol[:].unsqueeze(2).unsqueeze(3).to_broadcast([128, 16, 8, 2])
            op("dve", lambda: nc.vector.tensor_tensor(out=tyv, in0=uv, in1=dbc, op=ALU.mult),
               reads=[ureg, rg.dcol], writes=[rg.tY])
            op("dve", lambda: nc.vector.tensor_tensor(out=tyv, in0=tyv, in1=pv, op=ALU.add),
               reads=[rg.tY, rg.pc], writes=[rg.tY])
            if sample or not hold_last:
                op("act", lambda: nc.scalar.activation(out=uv, in_=tyv, func=AF.Gelu_apprx_tanh),
                   reads=[rg.tY], writes=[ureg])
                return lambda: None
            op("act", lambda: nc.scalar.activation(out=uT[:, :, cols:cols + n - 1], in_=tY[:, :, 0:n - 1],
                                                   func=AF.Gelu_apprx_tanh), reads=[rg.tY], writes=[ureg])
            op("act", lambda: nc.scalar.activation(out=ypend[:], in_=tY[:, :, n - 1:n], func=AF.Gelu_apprx_tanh),
               reads=[rg.tY], writes=[rg.ypend])

            def flush():
                op("act", lambda: nc.scalar.activation(out=uT[:, :, cols + n - 1:cols + n], in_=ypend[:], func=AF.Copy),
                   reads=[rg.ypend], writes=[ureg])
            return flush

        def store_state_T(src_c_aps, dst_drams, row0, nrows):
            for c in range(2):
                op("pe", lambda: nc.tensor.transpose(PS[0:nrows, 7, 0:128], src_c_aps[c], ident[:]),
                   reads=[rg.BU[0], rg.BU[1], rg.Xp, rg.t2r, rg.ident], writes=[rg.pt])
                op("act", lambda: nc.scalar.activation(out=stg[0:nrows, c, :], in_=PS[0:nrows, 7, 0:128], func=AF.Copy,
                                                       scale=(1.0 if c == 0 else -1.0)),
                   reads=[rg.pt], writes=[rg.stg])
                dma("sp", dst_drams[c][row0:row0 + nrows, :], stg[0:nrows, c, :], reads=[rg.stg])

        def ssm_block(uT, ureg, ntok_prompt, blk, main, units):
            tb_ = 32
            st_["TB"] = tb_
            st_["BU"] = [BUP[0], BUP[0]] if main else BUP
            st_["BUreg"] = [rg.BU[0], rg.BU[0]] if main else rg.BU
            subs = [(t0, min(tb_, ntok_prompt - t0)) for t0 in range(0, ntok_prompt, tb_)]
            b0 = st_.get("bufi", 0)
            kb = st_.get("kblk", 0)
            st_["kblk"] = kb + 1
            ub = kb % 2
            op("dve", lambda: nc.vector.tensor_copy(out=ulast[:, 1 - ub, :], in_=uT[:, :, ntok_prompt - 1]),
               reads=[ureg], writes=[rg.ulast])
            nsub = len(subs)
            bf = lambda i: (b0 + i) % 2

            def do_R(i, inter):
                n_ = subs[i][1]
                ssm_R(n_, False, bf(i), None, inter=inter, inter_at=(0, 2, 4, 6, 8, 10))
                op("dve", lambda: nc.vector.tensor_copy(out=Xp[:], in_=st_["BU"][bf(i)][:, n_ - 2:n_]), reads=[st_["BUreg"][bf(i)]],
                   writes=[rg.Xp])

            ssm_B(uT, ureg, subs[0][0], subs[0][1], False, bf(0), first=True, ubuf=ub)
            if main:
                do_R(0, None)
                for i, (t0, n) in enumerate(subs):
                    fl = ssm_Y(uT, ureg, t0, n, False, bf(i), hold_last=True)
                    if i + 1 < nsub:
                        ssm_B(uT, ureg, subs[i + 1][0], subs[i + 1][1], False, bf(i + 1))
                    fl()
                    if i + 1 < nsub:
                        do_R(i + 1, units)
            else:
                if nsub > 1:
                    ssm_B(uT, ureg, subs[1][0], subs[1][1], False, bf(1))
                do_R(0, None)
                for i, (t0, n) in enumerate(subs):
                    if i + 2 < nsub:
                        ssm_B(uT, ureg, subs[i + 2][0], subs[i + 2][1], False, bf(i + 2))
                    if i + 1 < nsub:
                        do_R(i + 1, units)
            bufi = (b0 + len(subs)) % 2
            if main:
                st_["TB"] = 16
                st_["BU"] = [BU[0], BU[0]]
                for r in range(4):
                    seq0 = 8 * blk + 2 * r
                    for c in range(2):
                        dma("sp", stg[:, c, :], sst_in[c][seq0 * 64:(seq0 + 2) * 64, :], writes=[rg.stg])
                    for c in range(2):
                        op("pe", lambda: nc.tensor.transpose(PS[:, 7, c * 128:(c + 1) * 128], stg[:, c, :], ident[:]),
                           reads=[rg.stg, rg.ident], writes=[rg.pt])
                    op("act", lambda: nc.scalar.activation(
                        out=Xinit[:, :, 0, :], in_=PS[:, 7, 0:128].rearrange("p (b e) -> p b e", b=2), func=AF.Copy),
                       reads=[rg.pt], writes=[rg.Xinit])
                    op("act", lambda: nc.scalar.activation(
                        out=Xinit[:, :, 1, :], in_=PS[:, 7, 128:256].rearrange("p (b e) -> p b e", b=2), func=AF.Copy,
                        scale=-1.0), reads=[rg.pt], writes=[rg.Xinit])
                    ssm_B(uT, ureg, NP + 16 * r, 16, True, bufi)
                    ssm_R(16, True, bufi, None)
                    ssm_Y(uT, ureg, NP + 16 * r, 16, True, bufi)
                    op("dve", lambda: nc.vector.tensor_copy(
                        out=t2r[:].rearrange("p (c b e) -> p c b e", c=2, b=2),
                        in_=st_["BU"][bufi][:, 14:16, :, :].rearrange("p b c e -> p c b e")), reads=[st_["BUreg"][bufi]], writes=[rg.t2r])
                    fin = [t2r[:, c * 128:(c + 1) * 128] for c in range(2)]
                    store_state_T(fin, sst_out, seq0 * 64, 128)
                    bufi = 1 - bufi
                    if units:
                        units.pop(0)()
            st_["bufi"] = bufi
            while units:
                units.pop(0)()

        def conv_unit(ct, blk):
            hn = lambda kt: hnT[:, kt, :]

            def m1():
                ps = mm_group(win[16 + ct], KT, hn, [rg.hnT], MH)
                op("act", lambda: nc.scalar.activation(out=V2(tA[:], 1), in_=pm(ps, H), func=AF.Copy),
                   reads=[rg.pm[ps]], writes=[rg.tA])

            def m2():
                ps = mm_group(win[48 + ct], KT, hn, [rg.hnT], MH)
                op("dve", lambda: nc.vector.tensor_copy(out=Zp[:, 0:2], in_=Zhist[:, ct, :]), reads=[rg.Zhist], writes=[rg.Zp])
                op("dve", lambda: nc.vector.tensor_copy(out=Zsp[:, :, 0:2], in_=ZsInit[:, ct, 8 * blk:8 * blk + 8, :]),
                   reads=[rg.ZsInit], writes=[rg.Zsp])
                op("dve", lambda: nc.vector.tensor_tensor(out=Zp[:, 2:2 + H], in0=PS[:, 2 * ps, 0:H], in1=tA[:, 0:H],
                                                          op=ALU.mult), reads=[rg.pm[ps], rg.tA], writes=[rg.Zp])
                op("dve", lambda: nc.vector.tensor_tensor(out=Zp[:, 2 + H:2 + NP], in0=PS[:, 2 * ps + 1, 0:NP - H],
                                                          in1=tA[:, H:NP], op=ALU.mult),
                   reads=[rg.pm[ps], rg.tA], writes=[rg.Zp])
                op("dve", lambda: nc.vector.tensor_tensor(
                    out=Zsp[:, :, 2:10], in0=PS[:, 2 * ps + 1, NP - H:H].rearrange("p (b t) -> p b t", b=8),
                    in1=tA[:, NP:NT].rearrange("p (b t) -> p b t", b=8), op=ALU.mult),
                   reads=[rg.pm[ps], rg.tA], writes=[rg.Zsp])
                w = lambda k: convw[:, k, ct:ct + 1]
                accp = tB[:, 0:NP]
                accs = tB[:, NP:NT].rearrange("p (b t) -> p b t", b=8)
                RZ = [rg.Zp, rg.Zsp, rg.convw, rg.tB]
                op("dve", lambda: nc.vector.tensor_scalar(out=accp, in0=Zp[:, 2:2 + NP], scalar1=w(2), scalar2=None,
                                                          op0=ALU.mult), reads=RZ, writes=[rg.tB])
                op("dve", lambda: nc.vector.tensor_scalar(out=accs, in0=Zsp[:, :, 2:10], scalar1=w(2), scalar2=None,
                                                          op0=ALU.mult), reads=RZ, writes=[rg.tB])
                for k in (1, 0):
                    sh = 2 - k
                    op("dve", lambda: nc.vector.scalar_tensor_tensor(out=accp, in0=Zp[:, 2 - sh:2 - sh + NP], scalar=w(k),
                                                                     in1=accp, op0=ALU.mult, op1=ALU.add),
                       reads=RZ, writes=[rg.tB])
                    op("dve", lambda: nc.vector.scalar_tensor_tensor(out=accs, in0=Zsp[:, :, 2 - sh:10 - sh], scalar=w(k),
                                                                     in1=accs, op0=ALU.mult, op1=ALU.add),
                       reads=RZ, writes=[rg.tB])
                op("dve", lambda: nc.vector.tensor_copy(out=Zhist[:, ct, :], in_=Zp[:, NP:NP + 2]), reads=[rg.Zp],
                   writes=[rg.Zhist])
                op("dve", lambda: nc.vector.tensor_copy(out=ZsOut[:, ct, 8 * blk:8 * blk + 8, :], in_=Zsp[:, :, 8:10]),
                   reads=[rg.Zsp], writes=[rg.ZsOut])

            def m3():
                ps = mm_group(win[32 + ct], KT, hn, [rg.hnT], MH)
                op("dve", lambda: nc.vector.tensor_tensor(out=V2(ycT[:, ct, :], 1), in0=pm(ps, H), in1=V2(tB[:], 1),
                                                          op=ALU.mult), reads=[rg.pm[ps], rg.tB], writes=[rg.yc])
            return [m1, m2, m3]

        def sumsq_acc(ps, mt, nmt):
            op("act", lambda: nc.scalar.activation(out=V2(tC[:], 1), in_=pm(ps, H), func=AF.Square),
               reads=[rg.pm[ps]], writes=[rg.tC])
            for h in range(2):
                op("pe", lambda: nc.tensor.matmul(PS[:, 4 + h, 0:H], lhsT=ones[:], rhs=tC[:, h * H:(h + 1) * H],
                                                  start=(mt == 0), stop=(mt == nmt - 1)),
                   reads=[rg.ones, rg.tC], writes=[rg.pb[h]], signal=True)

        def rstdb_from_ps():
            op("dve", lambda: nc.vector.tensor_scalar(out=V2(rstdb[:], 1), in0=PS[:, 4:6, 0:H], scalar1=1.0 / D, scalar2=EPS,
                                                      op0=ALU.mult, op1=ALU.add), reads=[rg.pb[0], rg.pb[1]], writes=[rg.rstdb])
            op("act", lambda: nc.scalar.activation(out=rstdb[:], in_=rstdb[:], func=AF.Sqrt),
               reads=[rg.rstdb], writes=[rg.rstdb])
            op("dve", lambda: nc.vector.reciprocal(out=rstdb[:], in_=rstdb[:]), reads=[rg.rstdb], writes=[rg.rstdb])

        def main_block(blk):
            xsrc = xm[blk]
            S.barrier()
            prep(xsrc, NT)
            hn = lambda kt: hnT[:, kt, :]
            for mt in range(16):
                ps = mm_group(win[mt], KT, hn, [rg.hnT], MH)
                op("act", lambda: nc.scalar.activation(out=V2(ygT[:, mt, :], 1), in_=pm(ps, H), func=AF.Copy),
                   reads=[rg.pm[ps]], writes=[rg.yg])
            S.barrier()
            dma("sp", Df[:, 4096:8192], cm_d, writes=[rg.Cm])
            hn = lambda kt: hnT[:, kt, :]
            yg = lambda kt: ygT[:, kt, :]
            yc = lambda kt: ycT[:, kt, :]

            def gbyb_unit(mt):
                def g1():
                    ps = mm_group(win[96 + mt], KT, hn, [rg.hnT], MH)
                    op("act", lambda: nc.scalar.activation(out=V2(tC[:], 1), in_=pm(ps, H), func=AF.Sigmoid),
                       reads=[rg.pm[ps]], writes=[rg.tC])

                def g2():
                    ps = mm_group(wco[mt], 16, yc, [rg.yc], MH)
                    op("dve", lambda: nc.vector.tensor_tensor(out=V2(mgT[:, mt, :], 1), in0=pm(ps, H), in1=V2(tC[:], 1),
                                                              op=ALU.mult), reads=[rg.pm[ps], rg.tC], writes=[rg.mg])
                return [g1, g2]
            units = [m for ct in range(16) for m in conv_unit(ct, blk)] + [m for mt in range(32) for m in gbyb_unit(mt)]
            ssm_block(ygT, rg.yg, NP, blk, True, units)
            for mt in range(32):
                ps = mm_group(win[64 + mt], KT, hn, [rg.hnT], MH)
                op("act", lambda: nc.scalar.activation(out=V2(tA[:], 1), in_=pm(ps, H), func=AF.Sigmoid),
                   reads=[rg.pm[ps]], writes=[rg.tA])
                ps = mm_group(wgv[mt], 16, yg, [rg.yg], MH)
                op("dve", lambda: nc.vector.tensor_tensor(out=V2(tA[:], 1), in0=pm(ps, H), in1=V2(tA[:], 1), op=ALU.mult),
                   reads=[rg.pm[ps], rg.tA], writes=[rg.tA])
                ps = mm_group(wgg[mt], 16, yg, [rg.yg], MH)
                op("act", lambda: nc.scalar.activation(out=V2(tC[:], 1), in_=pm(ps, H), func=AF.Sigmoid),
                   reads=[rg.pm[ps]], writes=[rg.tC])
                op("dve", lambda: nc.vector.tensor_tensor(out=tA[:], in0=tA[:], in1=tC[:], op=ALU.mult),
                   reads=[rg.tA, rg.tC], writes=[rg.tA])
                op("dve", lambda: nc.vector.tensor_tensor(out=mgT[:, mt, :], in0=tA[:], in1=mgT[:, mt, :], op=ALU.add),
                   reads=[rg.tA, rg.mg], writes=[rg.mg])
            S.barrier()
            mg = lambda kt: mgT[:, kt, :]
            for mt in range(32):
                ps = mm_group(wo[mt], KT, mg, [rg.mg], MH)
                op("act", lambda: nc.scalar.activation(out=V2(OG[:, mt, :], 1), in_=pm(ps, H), func=AF.Copy,
                                                       scale=gcols[:, 1, mt:mt + 1]),
                   reads=[rg.pm[ps], rg.gcols], writes=[rg.OG])
                sumsq_acc(ps, mt, 32)
            rstdb_from_ps()
            for mt in range(32):
                op("dve", lambda: nc.vector.tensor_tensor(out=OG[:, mt, :], in0=OG[:, mt, :], in1=rstdb[:], op=ALU.mult),
                   reads=[rg.OG, rg.rstdb], writes=[rg.OG])
            for (r0, sz) in tiles_of(NT):
                dma("sp", xtok[0:sz, :], xsrc[r0:r0 + sz, :], writes=[rg.xtok])
                for g4 in range(8):
                    for j in range(4):
                        kt = g4 * 4 + j
                        op("pe", lambda: nc.tensor.transpose(PS[0:sz, 7, j * 128:(j + 1) * 128], OG[:, kt, r0:r0 + sz], ident[:]),
                           reads=[rg.OG, rg.ident], writes=[rg.pt])
                    op("dve", lambda: nc.vector.tensor_tensor(out=xtok[0:sz, g4 * 512:(g4 + 1) * 512],
                                                              in0=xtok[0:sz, g4 * 512:(g4 + 1) * 512], in1=PS[0:sz, 7, :],
                                                              op=ALU.add), reads=[rg.xtok, rg.pt], writes=[rg.xtok])
                dma("sp", x1s[r0:r0 + sz, :], xtok[0:sz, :], reads=[rg.xtok], writes=[rg.x1s])
                op("dve", lambda: nc.vector.memset(ss[:, 1:2], 0.0), writes=[rg.ss])
                op("act", lambda: nc.scalar.activation(out=xtok2[0:sz, :], in_=xtok[0:sz, :], func=AF.Square,
                                                       accum_out=ss[0:sz, 1:2]),
                   reads=[rg.xtok], writes=[rg.xtok2, rg.ss])
                rstd_from_ss(1)
                op("act", lambda: nc.scalar.activation(out=xtok2[0:sz, :], in_=xtok[0:sz, :], func=AF.Copy,
                                                       scale=ss[0:sz, 1:2]),
                   reads=[rg.xtok, rg.ss], writes=[rg.xtok2])
                transposes_to_T(xtok2, sz, 2, mgT, rg.mg, r0)
            S.barrier()
            hf = lambda kt: mgT[:, kt, :]
            chunks = [(f0, min(FCH, FT - f0)) for f0 in range(0, FT, FCH)]
            for ci, (f0, nf) in enumerate(chunks):
                for j in range(nf):
                    ps = mm_group(wfg[f0 + j], KT, hf, [rg.mg], MH)
                    op("act", lambda: nc.scalar.activation(out=V2(tA[:], 1), in_=pm(ps, H), func=AF.Silu),
                       reads=[rg.pm[ps]], writes=[rg.tA])
                    ps = mm_group(wfu[f0 + j], KT, hf, [rg.mg], MH)
                    op("dve", lambda: nc.vector.tensor_tensor(out=V2(hbuf[:, j, :], 1), in0=pm(ps, H), in1=V2(tA[:], 1),
                                                              op=ALU.mult), reads=[rg.pm[ps], rg.tA], writes=[rg.hbuf])
                hb = lambda kt: hbuf[:, kt, :]
                lastc = ci == len(chunks) - 1
                for nt_ in range(32):
                    ps = mm_group(wfd[nt_][:, f0 * 128:(f0 + nf) * 128], nf, hb, [rg.hbuf], MH)
                    if ci == 0:
                        op("act", lambda: nc.scalar.activation(out=V2(OG[:, nt_, :], 1), in_=pm(ps, H), func=AF.Copy),
                           reads=[rg.pm[ps]], writes=[rg.OG])
                    else:
                        op("dve", lambda: nc.vector.tensor_tensor(out=V2(OG[:, nt_, :], 1), in0=pm(ps, H),
                                                                  in1=V2(OG[:, nt_, :], 1), op=ALU.add),
                           reads=[rg.pm[ps], rg.OG], writes=[rg.OG])
                    if lastc:
                        op("act", lambda: nc.scalar.activation(out=tC[:], in_=OG[:, nt_, :], func=AF.Square),
                           reads=[rg.OG], writes=[rg.tC])
                        for h in range(2):
                            op("pe", lambda: nc.tensor.matmul(PS[:, 4 + h, 0:H], lhsT=ones[:], rhs=tC[:, h * H:(h + 1) * H],
                                                              start=(nt_ == 0), stop=(nt_ == 31)),
                               reads=[rg.ones, rg.tC], writes=[rg.pb[h]], signal=True)
                        op("act", lambda: nc.scalar.activation(out=OG[:, nt_, :], in_=OG[:, nt_, :], func=AF.Copy,
                                                               scale=gcols[:, 3, nt_:nt_ + 1]),
                           reads=[rg.OG, rg.gcols], writes=[rg.OG])
            rstdb_from_ps()
            for mt in range(32):
                op("dve", lambda: nc.vector.tensor_tensor(out=OG[:, mt, :], in0=OG[:, mt, :], in1=rstdb[:], op=ALU.mult),
                   reads=[rg.OG, rg.rstdb], writes=[rg.OG])
            S.barrier()
            for (r0, sz) in tiles_of(NT):
                dma("sp", xtok[0:sz, :], x1s[r0:r0 + sz, :], reads=[rg.x1s], writes=[rg.xtok])
                for g4 in range(8):
                    for j in range(4):
                        kt = g4 * 4 + j
                        op("pe", lambda: nc.tensor.transpose(PS[0:sz, 7, j * 128:(j + 1) * 128], OG[:, kt, r0:r0 + sz], ident[:]),
                           reads=[rg.OG, rg.ident], writes=[rg.pt])
                    op("dve", lambda: nc.vector.tensor_tensor(out=xtok[0:sz, g4 * 512:(g4 + 1) * 512],
                                                              in0=xtok[0:sz, g4 * 512:(g4 + 1) * 512], in1=PS[0:sz, 7, :],
                                                              op=ALU.add), reads=[rg.xtok, rg.pt], writes=[rg.xtok])
                dma("sp", y_out[blk][r0:r0 + sz, :], xtok[0:sz, :], reads=[rg.xtok])

        def prefix_block(blk):
            S.barrier()
            prep(xp[blk], NP)
            if _DBG_SUB == 1:
                return
            hn = lambda kt: hnT[:, kt, :]
            for mt in range(16 if _DBG_SUB != 2 else 1):
                ps = mm_group(win[mt], KT, hn, [rg.hnT], PH)
                op("act", lambda: nc.scalar.activation(out=ygT[:, mt, 0:NP].rearrange("p (h n) -> p h n", h=2),
                                                       in_=pm(ps, HP), func=AF.Copy), reads=[rg.pm[ps]], writes=[rg.yg])
            if blk == 1:
                for ct in range(16):
                    ps = mm_group(win[16 + ct], KT, hn, [rg.hnT], [(NP - 2, NP)])
                    op("act", lambda: nc.scalar.activation(out=tA[:, 0:2], in_=PS[:, 2 * ps, 0:2], func=AF.Copy),
                       reads=[rg.pm[ps]], writes=[rg.tA])
                    ps = mm_group(win[48 + ct], KT, hn, [rg.hnT], [(NP - 2, NP)])
                    op("dve", lambda: nc.vector.tensor_tensor(out=Zhist[:, ct, :], in0=PS[:, 2 * ps, 0:2], in1=tA[:, 0:2],
                                                              op=ALU.mult), reads=[rg.pm[ps], rg.tA], writes=[rg.Zhist])
            if _DBG_SUB in (2, 3):
                return
            S.barrier()
            ssm_block(ygT, rg.yg, NP, blk, False, [])

        setup()
        if _DBG_STAGE >= 1:
            prefix_block(0)
        if _DBG_STAGE >= 2:
            prefix_block(1)
        if _DBG_STAGE >= 3:
            main_block(0)
        if _DBG_STAGE >= 4:
            main_block(1)
        S.barrier()
        store_state_T([Xp[:, 1, c, :] for c in range(2)], pst_out, 0, 64)
        for k in range(2):
            dma("sp", pcv_out[k].rearrange("(c p) -> p c", p=128), Zhist[:, :, k], reads=[rg.Zhist],
                allow_slow_non_contiguous=True)
        for ct in range(16):
            op("pe", lambda: nc.tensor.transpose(PS[0:32, 7, ct * 128 % 512:ct * 128 % 512 + 128],
                                                 ZsOut[:, ct].rearrange("p b k -> p (b k)"), ident[:]),
               reads=[rg.ZsOut, rg.ident], writes=[rg.pt])
            if ct % 4 == 3:
                g = ct // 4
                op("act", lambda: nc.scalar.activation(out=xtok[0:32, g * 512:(g + 1) * 512], in_=PS[0:32, 7, :], func=AF.Copy),
                   reads=[rg.pt], writes=[rg.xtok])
        dma("sp", scv_out, xtok[0:32, 0:2048], reads=[rg.xtok])
        sp = S.E["sp"]
        for s in S.sp_slots:
            if s.val:
                S._wait(sp, (s, s.val))
        for n in ("pe", "act", "dve"):
            e = S.E[n]
            if e.sem.val:
                S._wait(sp, (e.sem, e.sem.val))
    return nc


def _tile_w(w, nkt, nmt):
    return np.ascontiguousarray(w.reshape(nkt, 128, nmt, 128).transpose(2, 1, 0, 3)).reshape(nmt, 128, nkt * 128)


_NC_CACHE = {}


def kernel(x_prompt, x_sample, state_ssm_re, state_ssm_im, state_conv, meta_tokens, g_pre_mix, w_in,
           ssm_lambda_re, ssm_lambda_im, ssm_log_dt, ssm_b_re, ssm_b_im, ssm_c_re, ssm_c_im, ssm_d,
           w_glu_v, w_glu_g, conv_w, w_conv_out, w_o, g_post_mix, g_pre_ffn, w_ffn_gate, w_ffn_up,
           w_ffn_down, g_post_ffn):
    f32 = np.float32
    A = lambda a: np.asarray(a, dtype=f32)
    x_prompt, x_sample = A(x_prompt), A(x_sample)
    meta = A(meta_tokens)
    shared = {}
    shared["ident"] = np.eye(128, dtype=f32)
    gc = np.stack([A(g)[0].reshape(32, 128).T for g in (g_pre_mix, g_post_mix, g_pre_ffn, g_post_ffn)], axis=1)
    shared["gcols"] = np.ascontiguousarray(gc).reshape(128, 128)
    shared["convw"] = np.ascontiguousarray(A(conv_w)[0].reshape(3, 16, 128).transpose(2, 0, 1)).reshape(128, 48)
    shared["dcol"] = np.ascontiguousarray(A(ssm_d)[0].reshape(16, 128).T)
    lre, lim, ldt = A(ssm_lambda_re)[0], A(ssm_lambda_im)[0], A(ssm_log_dt)[0]
    l1f = lambda a: np.ascontiguousarray(a.reshape(64, 2, 64).transpose(1, 2, 0)).reshape(128, 64)
    ldt_gp = np.broadcast_to(ldt[:, None], (128, 64))
    shared["l1"] = np.stack([l1f(lre), l1f(lim), l1f(ldt_gp)])
    def l2_bcast(a):
        v = a.reshape(16, 4, 2, 64)
        o = np.broadcast_to(v[:, :, None, None, :, :], (16, 4, 2, 16, 2, 64))
        return np.ascontiguousarray(o.transpose(1, 2, 3, 0, 4, 5)).reshape(128, 2048)
    def l2_b(b):
        v = b.reshape(16, 4, 2, 64, 16)
        o = np.zeros((16, 4, 2, 16, 2, 64), f32)
        for g2 in range(2):
            o[:, :, g2, :, g2, :] = v[:, :, g2].transpose(0, 1, 3, 2)
        return np.ascontiguousarray(o.transpose(1, 2, 3, 0, 4, 5)).reshape(128, 2048)
    shared["l2"] = np.stack([l2_bcast(lre), l2_bcast(lim), l2_bcast(ldt_gp), l2_b(A(ssm_b_re)[0]), l2_b(A(ssm_b_im)[0])])
    cm = np.zeros((2, 64, 64, 2, 2, 16), f32)
    for c, carr in enumerate((A(ssm_c_re)[0], A(ssm_c_im)[0])):
        v = carr.reshape(64, 2, 16, 64)
        for g2 in range(2):
            cm[g2, :, :, c, g2, :] = v[:, g2].transpose(2, 0, 1)
    shared["cm"] = cm.reshape(128, 4096)
    shared["win"] = _tile_w(A(w_in)[0], 32, 128)
    shared["wgv"] = _tile_w(A(w_glu_v)[0], 16, 32)
    shared["wgg"] = _tile_w(A(w_glu_g)[0], 16, 32)
    shared["wco"] = _tile_w(A(w_conv_out)[0], 16, 32)
    shared["wo"] = _tile_w(A(w_o)[0], 32, 32)
    shared["wfg"] = _tile_w(A(w_ffn_gate)[0], 32, FT)
    shared["wfu"] = _tile_w(A(w_ffn_up)[0], 32, FT)
    shared["wfd"] = _tile_w(A(w_ffn_down)[0], FT, 32)

    sre, sim, scv = A(state_ssm_re)[0], A(state_ssm_im)[0], A(state_conv)[0]
    in_maps = []
    for c in range(8):
        bq, half = c // 2, c % 2
        full = np.concatenate([meta, x_prompt[bq]], axis=0)
        if half == 0:
            main_p = full[0:1032]
            pre = np.zeros((1032, D), f32)
        else:
            main_p = full[1032:2064]
            pre = full[0:1032]
        xs = x_sample[16 * c:16 * c + 16].reshape(2, 64, D)
        xm_ = np.concatenate([main_p.reshape(2, NP, D), xs], axis=1)
        m = dict(shared)
        m["xm"] = np.ascontiguousarray(xm_)
        m["xp"] = np.ascontiguousarray(pre.reshape(2, NP, D))
        m["sst_re_in"] = np.ascontiguousarray(sre[16 * c:16 * c + 16]).reshape(16 * 64, 128)
        m["sst_im_in"] = np.ascontiguousarray(sim[16 * c:16 * c + 16]).reshape(16 * 64, 128)
        m["scv_in"] = np.ascontiguousarray(scv[16 * c:16 * c + 16]).reshape(32, SW)
        in_maps.append(m)

    if "nc" not in _NC_CACHE:
        _NC_CACHE["nc"] = build_nc()
    nc = _NC_CACHE["nc"]
    res = run_bass_kernel_spmd(nc, in_maps, core_ids=list(range(8)))
    R = res.results

    y_prompt = np.zeros((4, 2048, D), f32)
    y_sample = np.zeros((128, 8, D), f32)
    p_re = np.zeros((1, 4, 128, 64), f32)
    p_im = np.zeros((1, 4, 128, 64), f32)
    p_cv = np.zeros((1, 4, 2, SW), f32)
    s_re = np.zeros((1, 128, 128, 64), f32)
    s_im = np.zeros((1, 128, 128, 64), f32)
    s_cv = np.zeros((1, 128, 2, SW), f32)
    for c in range(8):
        bq, half = c // 2, c % 2
        y = R[c]["y"]
        yp = y[:, 0:NP].reshape(1032, D)
        if half == 0:
            y_prompt[bq, 0:1016] = yp[16:]
        else:
            y_prompt[bq, 1016:2048] = yp
            p_re[0, bq] = R[c]["pst_re"].reshape(128, 64)
            p_im[0, bq] = R[c]["pst_im"].reshape(128, 64)
            p_cv[0, bq] = R[c]["pcv"]
        y_sample[16 * c:16 * c + 16] = y[:, NP:NT].reshape(16, 8, D)
        s_re[0, 16 * c:16 * c + 16] = R[c]["sst_re"].reshape(16, 128, 64)
        s_im[0, 16 * c:16 * c + 16] = R[c]["sst_im"].reshape(16, 128, 64)
        s_cv[0, 16 * c:16 * c + 16] = R[c]["scv"].reshape(16, 2, SW)
    return (y_prompt, y_sample, p_re, p_im, p_cv, s_re, s_im, s_cv)
```

```python
import math
from contextlib import ExitStack

import numpy as np
import concourse.bass as bass
import concourse.mybir as mybir
from concourse.bass_utils import run_bass_kernel_spmd

F32 = mybir.dt.float32
BF16 = mybir.dt.bfloat16
AF = mybir.ActivationFunctionType
ALU = mybir.AluOpType

D = 4096
KT = 32
SW = 2048
DFF = 11008
FT = 86
NT = 580
NP = 516
H = 290
HP = 258
EPS = 1e-6
TB = 16
FCH = 24
PI = math.pi
_DBG_STAGE = 4
_DBG_SUB = 0


class _Sem:
    def __init__(self, h):
        self.h = h
        self.val = 0


class _Eng:
    def __init__(self, name, h, sem):
        self.name = name
        self.h = h
        self.sem = sem
        self.waited = {}


class Region:
    __slots__ = ("w", "r")

    def __init__(self):
        self.w = None
        self.r = {}


class Sync:
    def __init__(self, nc, es):
        self.nc = nc
        mk = lambda n: _Sem(es.enter_context(nc.semaphore(n)))
        self.E = {
            "pe": _Eng("pe", nc.tensor, mk("s_pe")),
            "act": _Eng("act", nc.scalar, mk("s_act")),
            "dve": _Eng("dve", nc.vector, mk("s_dve")),
            "pool": _Eng("pool", nc.gpsimd, mk("s_pool")),
            "sp": _Eng("sp", nc.sync, mk("s_sp")),
        }
        self.sp_slots = [mk("s_d%d" % i) for i in range(12)]
        self.sp_i = 0

    def _wait(self, eng, st, self_sync=True):
        s, v = st
        if eng.waited.get(s, 0) >= v:
            return
        if s is eng.sem and (eng.name == "pe" or not self_sync or v > s.val):
            return
        eng.h.wait_ge(s.h, v)
        eng.waited[s] = v

    def _deps(self, eng, reads, writes, self_sync=True):
        for r in reads:
            if r.w:
                self._wait(eng, r.w, self_sync)
        for r in writes:
            if r.w:
                self._wait(eng, r.w, self_sync)
            for st in r.r.values():
                self._wait(eng, st, self_sync)

    def _mark(self, st, reads, writes):
        for r in writes:
            r.w = st
            r.r = {}
        for r in reads:
            r.r[st[0]] = st

    def op(self, en, fn, reads=(), writes=(), signal=True, self_sync=True):
        eng = self.E[en]
        self._deps(eng, reads, writes, self_sync)
        ins = fn()
        if signal:
            eng.sem.val += 1
            ins.then_inc(eng.sem.h, 1)
            st = (eng.sem, eng.sem.val)
        else:
            st = (eng.sem, eng.sem.val + 1)
        self._mark(st, reads, writes)
        return ins

    def dma(self, qn, out, in_, reads=(), writes=(), slot=None, **kw):
        q = self.E[qn]
        if slot is None:
            slot = self.sp_slots[self.sp_i]
            self.sp_i = (self.sp_i + 1) % len(self.sp_slots)
        if slot.val:
            self._wait(q, (slot, slot.val))
        self._deps(q, reads, writes)
        ins = q.h.dma_start(out=out, in_=in_, **kw)
        slot.val += 16
        ins.then_inc(slot.h, 16)
        st = (slot, slot.val)
        self._mark(st, reads, writes)
        return ins

    def barrier(self, extra_slots=()):
        names = ["pe", "act", "dve", "sp"]
        for a in names:
            ea = self.E[a]
            for b in names:
                if a == b:
                    continue
                eb = self.E[b]
                if eb.sem.val:
                    self._wait(ea, (eb.sem, eb.sem.val))
            for s in list(self.sp_slots) + list(extra_slots):
                if s.val:
                    self._wait(ea, (s, s.val))


def build_nc():
    nc = bass.Bass("TRN2", target_bir_lowering=False)
    dt_in = lambda n, s: nc.dram_tensor(n, s, F32, kind="ExternalInput").ap()
    dt_out = lambda n, s: nc.dram_tensor(n, s, F32, kind="ExternalOutput").ap()

    xm = dt_in("xm", [2, NT, D])
    xp = dt_in("xp", [2, NP, D])
    sst_in = [dt_in("sst_re_in", [16 * 64, 128]), dt_in("sst_im_in", [16 * 64, 128])]
    scv_in = dt_in("scv_in", [32, SW])
    ident_d = dt_in("ident", [128, 128])
    gcols_d = dt_in("gcols", [128, 4 * 32])
    convw_d = dt_in("convw", [128, 48])
    dcol_d = dt_in("dcol", [128, 16])
    l1_d = dt_in("l1", [3, 128, 64])
    l2_d = dt_in("l2", [5, 128, 2048])
    cm_d = dt_in("cm", [128, 4096])
    win = dt_in("win", [128, 128, 4096])
    wgv = dt_in("wgv", [32, 128, 2048])
    wgg = dt_in("wgg", [32, 128, 2048])
    wco = dt_in("wco", [32, 128, 2048])
    wo = dt_in("wo", [32, 128, 4096])
    wfg = dt_in("wfg", [FT, 128, 4096])
    wfu = dt_in("wfu", [FT, 128, 4096])
    wfd = dt_in("wfd", [32, 128, FT * 128])

    y_out = dt_out("y", [2, NT, D])
    pst_out = [dt_out("pst_re", [64, 128]), dt_out("pst_im", [64, 128])]
    pcv_out = dt_out("pcv", [2, SW])
    sst_out = [dt_out("sst_re", [16 * 64, 128]), dt_out("sst_im", [16 * 64, 128])]
    scv_out = dt_out("scv", [32, SW])
    x1s = nc.dram_tensor("x1s", [NT, D], F32).ap()

    with ExitStack() as es:
        sb = lambda n, s, d: es.enter_context(nc.sbuf_tensor("sb_" + n, s, d))
        S = Sync(nc, es)
        op, dma = S.op, S.dma
        w_slots = [_Sem(es.enter_context(nc.semaphore("s_w%d" % i))) for i in range(3)]

        ident = sb("ident", [128, 128], F32)
        ones = sb("ones", [128, 128], F32)
        gcols = sb("gcols", [128, 4, 32], F32)
        convw = sb("convw", [128, 3, 16], F32)
        dcol = sb("dcol", [128, 16], F32)
        CA = sb("CA", [128, 2, 64], F32)
        CB = sb("CB", [128, 2, 64], F32)
        Bt = sb("Bt", [128, 2, 16, 128], BF16)
        Xp = sb("Xp", [128, 2, 2, 64], F32)
        CA2 = sb("CA2", [128, 2, 64], F32)
        CB2 = sb("CB2", [128, 2, 64], F32)
        Bt2 = sb("Bt2", [128, 2, 16, 128], BF16)
        ulast = sb("ulast", [128, 2, 16], BF16)
        ypend = sb("ypend", [128, 16, 1], BF16)
        Xinit = sb("Xinit", [128, 2, 2, 64], F32)
        Zhist = sb("Zhist", [128, 16, 2], F32)
        ZsInit = sb("ZsInit", [128, 16, 16, 2], F32)
        ZsOut = sb("ZsOut", [128, 16, 16, 2], F32)
        rstdb = sb("rstdb", [128, NT], F32)
        tA = sb("tA", [128, NT], F32)
        tB = sb("tB", [128, NT], F32)
        tC = sb("tC", [128, NT], F32)
        Zp = sb("Zp", [128, NP + 2], F32)
        Zsp = sb("Zsp", [128, 8, 10], F32)
        ss = sb("ss", [128, 4], F32)
        junk = sb("junk", [128, 4], F32)
        stg = sb("stg", [128, 2, 128], F32)
        tY = sb("tY", [128, 16, 32], F32)
        t1r = sb("t1r", [128, 256], F32)
        t2r = sb("t2r", [128, 256], F32)
        R1f = sb("R1", [128, 18560], F32)
        R2b = sb("R2", [128, 18560], BF16)
        Wr = [sb("wr%d" % i, [128, 4096], BF16) for i in range(3)]
        Df = sb("Dd", [128, 8192], F32)
        PS = es.enter_context(nc.psum_tensor("PS", [128, 8, 512], F32))

        R1b = R1f[:].bitcast(BF16)
        hnT = R1b[:, 0:18560].rearrange("p (k n) -> p k n", k=32)
        ygT = R1b[:, 18560:27840].rearrange("p (k n) -> p k n", k=16)
        ycT = R1b[:, 27840:37120].rearrange("p (k n) -> p k n", k=16)
        OG = R1f[:].rearrange("p (k n) -> p k n", k=32)
        mgT = R2b[:].rearrange("p (k n) -> p k n", k=32)
        xtok = Df[:, 0:4096]
        xtok2 = Df[:, 4096:8192]
        BU = [Df[:, i * 2048:(i + 1) * 2048].rearrange("p (t c e) -> p t c e", t=TB, c=2) for i in range(2)]
        BUP = [Df[:, i * 4096:(i + 1) * 4096].rearrange("p (t c e) -> p t c e", t=32, c=2) for i in range(2)]
        Cm = Df[:, 4096:8192].rearrange("p (e c m) -> p e c m", e=64, c=2)
        hbuf = Df[:].bitcast(BF16)[:, 0:FCH * NT].rearrange("p (k n) -> p k n", k=FCH)

        class RG:
            pass
        rg = RG()
        for n in ("ident ones gcols convw dcol CA CB CA2 CB2 Bt2 ulast ypend Bt Xp Xinit Zhist ZsInit ZsOut rstdb tA tB tC Zp Zsp ss junk stg tY t1r t2r "
                  "hnT yg yc OG mg xtok xtok2 Cm hbuf x1s").split():
            setattr(rg, n, Region())
        rg.wr = [Region() for _ in range(3)]
        rg.BU = [Region() for _ in range(2)]
        rg.pm = [Region() for _ in range(2)]
        rg.pb = [Region() for _ in range(2)]
        rg.pc = Region()
        rg.pt = Region()

        pm = lambda s, w: PS[:, 2 * s:2 * s + 2, 0:w]
        st_ = {"w": 0, "p": 0, "pt": 0}

        def V2(ap, w):
            return ap.rearrange("p (h n) -> p h n", h=2) if w else ap

        def mm_group(wsrc, nkt, rhs_fn, rhs_regs, halves):
            slot = st_["w"]
            st_["w"] = (slot + 1) % 3
            ps = st_["p"]
            st_["p"] = (ps + 1) % 2
            nel = nkt * 128
            bsz = max(b_ for b_ in range(128, 2049, 128) if nel % b_ == 0)
            dma("pool", Wr[slot][:, 0:nel].rearrange("p (a b) -> p a b", b=bsz),
                wsrc.rearrange("p (a b) -> p a b", b=bsz),
                reads=(), writes=(rg.wr[slot],), slot=w_slots[slot])
            nh = len(halves)
            for kt in range(nkt):
                for h, (c0, c1) in enumerate(halves):
                    last = (kt == nkt - 1) and (h == nh - 1)
                    op("pe", lambda: nc.tensor.matmul(PS[:, 2 * ps + h, 0:c1 - c0],
                                                      lhsT=Wr[slot][:, kt * 128:(kt + 1) * 128],
                                                      rhs=rhs_fn(kt)[:, c0:c1],
                                                      start=(kt == 0), stop=(kt == nkt - 1)),
                       reads=[rg.wr[slot]] + list(rhs_regs), writes=[rg.pm[ps]], signal=last)
            return ps

        MH = [(0, H), (H, NT)]
        PH = [(0, HP), (HP, NP)]

        def setup():
            dma("sp", ident[:], ident_d, writes=[rg.ident])
            dma("sp", gcols[:], gcols_d.rearrange("p (a b) -> p a b", a=4), writes=[rg.gcols])
            dma("sp", convw[:], convw_d.rearrange("p (a b) -> p a b", a=3), writes=[rg.convw])
            dma("sp", dcol[:], dcol_d, writes=[rg.dcol])
            op("dve", lambda: nc.vector.memset(ones[:], 1.0), writes=[rg.ones])
            op("dve", lambda: nc.vector.memset(Xp[:], 0.0), writes=[rg.Xp])
            op("dve", lambda: nc.vector.memset(ulast[:], 0.0), writes=[rg.ulast])
            op("dve", lambda: nc.vector.memset(Zhist[:], 0.0), writes=[rg.Zhist])

            rtmp = Region()

            def lam_bar(lre, lim, ldt, tmp, n):
                dtv, ldr, ldi, er, a1, a2 = tmp[:6]
                R = [rtmp]
                op("act", lambda: nc.scalar.activation(out=dtv, in_=ldt, func=AF.Exp), reads=R, writes=R)
                op("dve", lambda: nc.vector.tensor_tensor(out=ldr, in0=lre, in1=dtv, op=ALU.mult), reads=R, writes=R)
                op("dve", lambda: nc.vector.tensor_tensor(out=ldi, in0=lim, in1=dtv, op=ALU.mult), reads=R, writes=R)
                op("act", lambda: nc.scalar.activation(out=er, in_=ldr, func=AF.Exp), reads=R, writes=R)
                TS = lambda o, i, s1, s2, o0, o1=None: op("dve", lambda: (nc.vector.tensor_scalar(out=o, in0=i, scalar1=s1, scalar2=s2, op0=o0, op1=o1)
                                                                         if o1 is not None else
                                                                         nc.vector.tensor_scalar(out=o, in0=i, scalar1=s1, scalar2=None, op0=o0)),
                                                         reads=R, writes=R)
                TT_ = lambda o, a, b, o_: op("dve", lambda: nc.vector.tensor_tensor(out=o, in0=a, in1=b, op=o_), reads=R, writes=R)
                TS(a1, ldi, 1.0 / (2 * PI), None, ALU.mult)
                op("dve", lambda: nc.vector.tensor_copy(out=a2.bitcast(mybir.dt.int32), in_=a1), reads=R, writes=R)
                op("dve", lambda: nc.vector.tensor_copy(out=a1, in_=a2.bitcast(mybir.dt.int32)), reads=R, writes=R)
                op("dve", lambda: nc.vector.scalar_tensor_tensor(out=a1, in0=a1, scalar=-2 * PI, in1=ldi, op0=ALU.mult,
                                                                 op1=ALU.add), reads=R, writes=R)
                TS(dtv, a1, PI, 2 * PI, ALU.is_gt, ALU.mult)
                TT_(a1, a1, dtv, ALU.subtract)
                TS(dtv, a1, -PI, 2 * PI, ALU.is_lt, ALU.mult)
                TT_(a1, a1, dtv, ALU.add)
                TS(a2, a1, PI / 2, None, ALU.add)
                TS(dtv, a2, PI, 2 * PI, ALU.is_gt, ALU.mult)
                TT_(a2, a2, dtv, ALU.subtract)
                op("act", lambda: nc.scalar.activation(out=a1, in_=a1, func=AF.Sin), reads=R, writes=R)
                op("act", lambda: nc.scalar.activation(out=a2, in_=a2, func=AF.Sin), reads=R, writes=R)
                op("dve", lambda: nc.vector.tensor_tensor(out=a2, in0=a2, in1=er, op=ALU.mult), reads=R, writes=R)
                op("dve", lambda: nc.vector.tensor_tensor(out=a1, in0=a1, in1=er, op=ALU.mult), reads=R, writes=R)
                return a2, a1

            t1 = [Df[:, i * 64:(i + 1) * 64] for i in range(12)]
            for i in range(3):
                dma("sp", t1[i], l1_d[i], writes=[rtmp])
            lbr, lbi = lam_bar(t1[0], t1[1], t1[2], t1[3:9], 64)
            R = [rtmp]
            for c in range(2):
                op("dve", lambda: nc.vector.tensor_copy(out=CA[:, c, :], in_=lbr), reads=R, writes=[rg.CA])
            op("dve", lambda: nc.vector.tensor_copy(out=CB[:, 0, :], in_=lbi), reads=R, writes=[rg.CB])
            op("dve", lambda: nc.vector.tensor_scalar(out=CB[:, 1, :], in0=lbi, scalar1=-1.0, scalar2=None,
                                                      op0=ALU.mult), reads=R, writes=[rg.CB])
            q1, q2 = t1[9], t1[10]
            TT1 = lambda o, a, b, o_, w=(rtmp,): op("dve", lambda: nc.vector.tensor_tensor(out=o, in0=a, in1=b, op=o_), reads=R, writes=list(w))
            TT1(q1, lbr, lbr, ALU.mult)
            TT1(q2, lbi, lbi, ALU.mult)
            TT1(q1, q1, q2, ALU.subtract)
            TT1(q2, lbr, lbi, ALU.mult)
            for c in range(2):
                op("dve", lambda: nc.vector.tensor_copy(out=CA2[:, c, :], in_=q1), reads=R, writes=[rg.CA2])
            op("dve", lambda: nc.vector.tensor_scalar(out=CB2[:, 0, :], in0=q2, scalar1=2.0, scalar2=None, op0=ALU.mult),
               reads=R, writes=[rg.CB2])
            op("dve", lambda: nc.vector.tensor_scalar(out=CB2[:, 1, :], in0=q2, scalar1=-2.0, scalar2=None, op0=ALU.mult),
               reads=R, writes=[rg.CB2])
            S.barrier()
            t2 = [R1f[:, i * 2048:(i + 1) * 2048] for i in range(9)] + [Df[:, i * 2048:(i + 1) * 2048] for i in range(4)]
            for i in range(5):
                dma("sp", t2[i], l2_d[i], writes=[rtmp])
            lre, lim, ldt, bre, bim = t2[0:5]
            lbr, lbi = lam_bar(lre, lim, ldt, t2[5:11], 2048)
            nr, m2 = t2[11], t2[12]
            f1, f2 = t2[5], t2[6]
            TT = lambda o, a, b, o_: op("dve", lambda: nc.vector.tensor_tensor(out=o, in0=a, in1=b, op=o_), reads=R, writes=R)
            op("dve", lambda: nc.vector.tensor_scalar(out=nr, in0=lbr, scalar1=-1.0, scalar2=None, op0=ALU.add),
               reads=R, writes=R)
            TT(m2, lre, lre, ALU.mult)
            TT(f1, lim, lim, ALU.mult)
            TT(m2, m2, f1, ALU.add)
            op("dve", lambda: nc.vector.reciprocal(out=m2, in_=m2), reads=R, writes=R)
            TT(f1, nr, lre, ALU.mult)
            TT(f2, lbi, lim, ALU.mult)
            TT(f1, f1, f2, ALU.add)
            TT(f1, f1, m2, ALU.mult)
            TT(f2, lbi, lre, ALU.mult)
            TT(nr, nr, lim, ALU.mult)
            TT(f2, f2, nr, ALU.subtract)
            TT(f2, f2, m2, ALU.mult)
            X1, X2, sc1 = t2[0], t2[1], t2[2]
            TT(nr, f1, bre, ALU.mult)
            TT(m2, f2, bim, ALU.mult)
            TT(X1, nr, m2, ALU.subtract)
            TT(nr, f1, bim, ALU.mult)
            TT(m2, f2, bre, ALU.mult)
            TT(nr, nr, m2, ALU.add)
            op("dve", lambda: nc.vector.tensor_scalar(out=X2, in0=nr, scalar1=-1.0, scalar2=None, op0=ALU.mult),
               reads=R, writes=R)
            flat = lambda t: t.rearrange("p a b -> p (a b)")
            op("dve", lambda: nc.vector.tensor_copy(out=flat(Bt[:, 0]), in_=X1), reads=R, writes=[rg.Bt])
            op("dve", lambda: nc.vector.tensor_copy(out=flat(Bt[:, 1]), in_=X2), reads=R, writes=[rg.Bt])
            TT(nr, lbr, X1, ALU.mult)
            TT(m2, lbi, X2, ALU.mult)
            op("dve", lambda: nc.vector.tensor_tensor(out=flat(Bt2[:, 0]), in0=nr, in1=m2, op=ALU.add), reads=R, writes=[rg.Bt2])
            TT(nr, lbr, X2, ALU.mult)
            TT(m2, lbi, X1, ALU.mult)
            op("dve", lambda: nc.vector.tensor_tensor(out=flat(Bt2[:, 1]), in0=nr, in1=m2, op=ALU.subtract), reads=R,
               writes=[rg.Bt2])
            S.barrier()
            dma("sp", Df[0:32, 0:2048], scv_in, writes=[rg.xtok])
            for ct in range(16):
                op("pe", lambda: nc.tensor.transpose(PS[:, 7, 0:32], Df[0:32, ct * 128:(ct + 1) * 128], ident[0:32, 0:32]),
                   reads=[rg.xtok, rg.ident], writes=[rg.pt])
                op("act", lambda: nc.scalar.activation(out=ZsInit[:, ct].rearrange("p b k -> p (b k)"), in_=PS[:, 7, 0:32],
                                                       func=AF.Copy), reads=[rg.pt], writes=[rg.ZsInit])
            S.barrier()

        def tiles_of(n):
            return [(i, min(128, n - i)) for i in range(0, n, 128)]

        def rstd_from_ss(col):
            op("dve", lambda: nc.vector.tensor_scalar(out=ss[:, col:col + 1], in0=ss[:, col:col + 1], scalar1=1.0 / D,
                                                      scalar2=EPS, op0=ALU.mult, op1=ALU.add), reads=[rg.ss], writes=[rg.ss])
            op("act", lambda: nc.scalar.activation(out=ss[:, col:col + 1], in_=ss[:, col:col + 1], func=AF.Sqrt),
               reads=[rg.ss], writes=[rg.ss])
            op("dve", lambda: nc.vector.reciprocal(out=ss[:, col:col + 1], in_=ss[:, col:col + 1]),
               reads=[rg.ss], writes=[rg.ss])

        def transposes_to_T(src, sz, gi, dstT, dreg, c0, sreg=None):
            for g4 in range(8):
                for j in range(4):
                    kt = g4 * 4 + j
                    op("pe", lambda: nc.tensor.transpose(PS[:, 7, j * 128:j * 128 + sz], src[0:sz, kt * 128:(kt + 1) * 128],
                                                         ident[0:sz, 0:sz]),
                       reads=([sreg] if sreg is not None else [rg.xtok, rg.xtok2]) + [rg.ident], writes=[rg.pt])
                op("dve", lambda: nc.vector.tensor_tensor(
                    out=dstT[:, g4 * 4:g4 * 4 + 4, c0:c0 + sz],
                    in0=PS[:, 7, :].rearrange("p (j n) -> p j n", j=4)[:, :, 0:sz],
                    in1=gcols[:, gi, g4 * 4:g4 * 4 + 4].unsqueeze(2).to_broadcast([128, 4, sz]), op=ALU.mult),
                   reads=[rg.pt, rg.gcols], writes=[dreg])

        XT = [xtok, xtok2]
        rg.XT = [rg.xtok, rg.xtok2]

        def sumsq_rstd(buf, breg, sz, col):
            op("dve", lambda: nc.vector.memset(ss[:, col:col + 1], 0.0), writes=[rg.ss])
            op("act", lambda: nc.scalar.activation(out=junk[0:sz, 0:1].to_broadcast([sz, D]), in_=buf[0:sz, :], func=AF.Square,
                                                   accum_out=ss[0:sz, col:col + 1]),
               reads=[breg], writes=[rg.junk, rg.ss])
            rstd_from_ss(col)

        def prep(xsrc, ntok):
            for ti, (r0, sz) in enumerate(tiles_of(ntok)):
                buf, breg = XT[ti % 2], rg.XT[ti % 2]
                dma("sp", buf[0:sz, :], xsrc[r0:r0 + sz, :], writes=[breg])
                sumsq_rstd(buf, breg, sz, ti % 2)
                op("act", lambda: nc.scalar.activation(out=buf[0:sz, :], in_=buf[0:sz, :], func=AF.Copy,
                                                       scale=ss[0:sz, ti % 2:ti % 2 + 1]),
                   reads=[breg, rg.ss], writes=[breg])
                transposes_to_T(buf, sz, 0, hnT, rg.hnT, r0, breg)

        def ssm_B(uT, ureg, cols, n, sample, bufi, first=False, ubuf=0):
            BUv = st_["BU"][bufi]
            rB = st_["BUreg"][bufi]
            tb = st_["TB"]
            ncl = 128 // tb
            for pas in range(16 // ncl):
                for q in range(4):
                    par = q % 2
                    qh = q // 2
                    pbv = PS[:, 4 + par, :].rearrange("p (j c t) -> p j c t", j=2 * ncl, c=2)
                    for cl in range(ncl):
                        ct = ncl * pas + cl
                        j = 2 * cl + qh
                        for c in range(2):
                            rows = slice(32 * q, 32 * q + 32)
                            if not sample:
                                items = [(Bt, uT[rows, ct, cols:cols + n], pbv[:, j, c, 0:n])]
                                if first:
                                    items.append((Bt2, ulast[rows, ubuf, ct:ct + 1], pbv[:, j, c, 0:1]))
                                    items.append((Bt2, uT[rows, ct, cols:cols + n - 1], pbv[:, j, c, 1:n]))
                                else:
                                    items.append((Bt2, uT[rows, ct, cols - 1:cols + n - 1], pbv[:, j, c, 0:n]))
                            else:
                                items = [(Bt, uT[rows, ct, cols + 8 * b_:cols + 8 * b_ + 8],
                                          pbv[:, j, c, b_:16:2]) for b_ in range(2)]
                            for ii, (bm, rhs, o_) in enumerate(items):
                                st_flag = True if sample else (ii == 0)
                                sp_flag = True if sample else (ii == len(items) - 1)
                                op("pe", lambda: nc.tensor.matmul(o_, lhsT=bm[rows, c, ct, :], rhs=rhs,
                                                                  start=st_flag, stop=sp_flag, tile_position=(32 * q, 0)),
                                   reads=[rg.Bt, rg.Bt2, rg.ulast, ureg], writes=[rg.pb[par]],
                                   signal=(cl == ncl - 1 and c == 1 and ii == len(items) - 1))
                for par in range(2):
                    pbv = PS[:, 4 + par, :].rearrange("p (j c t) -> p j c t", j=2 * ncl, c=2)
                    e0 = 4 * ncl * pas + par
                    op("act", lambda: nc.scalar.activation(
                        out=BUv[:, 0:n, :, e0:e0 + 4 * ncl - 1:2].rearrange("p t c e -> p e c t"),
                        in_=pbv[:, :, :, 0:n], func=AF.Copy), reads=[rg.pb[par]], writes=[rB])

        def ssm_R(n, sample, bufi, prev_ap, inter=None, inter_at=()):
            BUv = st_["BU"][bufi]
            rB = st_["BUreg"][bufi]
            if not sample:
                assert n % 2 == 0
                nsteps = n // 2
                halves = [lambda a, h=h: a[:, h] for h in range(2)]
                get_prev = lambda j: Xp[:] if j == 0 else BUv[:, 2 * (j - 1):2 * j]
                get_cur = lambda j: BUv[:, 2 * j:2 * j + 2]
                ca = CA2[:].unsqueeze(1).to_broadcast([128, 2, 2, 64])
                cb = CB2[:].unsqueeze(1).to_broadcast([128, 2, 2, 64])
                t1 = t1r[:].rearrange("p (b c e) -> p b c e", b=2, c=2)
                t2 = t2r[:].rearrange("p (b c e) -> p b c e", b=2, c=2)
                sw = lambda a: a[:, ::-1, :]
            else:
                nsteps = 8
                halves = [lambda a, h=h: a[:, h] for h in range(2)]
                get_prev = lambda j: Xinit[:] if j == 0 else BUv[:, 2 * (j - 1):2 * j]
                get_cur = lambda j: BUv[:, 2 * j:2 * j + 2]
                ca = CA[:].unsqueeze(1).to_broadcast([128, 2, 2, 64])
                cb = CB[:].unsqueeze(1).to_broadcast([128, 2, 2, 64])
                t1 = t1r[:].rearrange("p (b c e) -> p b c e", b=2, c=2)
                t2 = t2r[:].rearrange("p (b c e) -> p b c e", b=2, c=2)
                sw = lambda a: a[:, ::-1, :]
            RR = [rB, rg.BU[0], rg.BU[1], rg.Xp, rg.Xinit, rg.CA, rg.CB, rg.CA2, rg.CB2]
            for j in range(nsteps):
                prev, cur = get_prev(j), get_cur(j)
                ss_ = (j == 0)
                for hf in halves:
                    op("dve", lambda: nc.vector.tensor_tensor(out=hf(t1), in0=hf(prev), in1=hf(ca), op=ALU.mult),
                       reads=RR, writes=[rg.t1r], self_sync=ss_, signal=False)
                for hf in halves:
                    op("dve", lambda: nc.vector.tensor_tensor(out=hf(t2), in0=sw(hf(prev)), in1=hf(cb), op=ALU.mult),
                       reads=RR, writes=[rg.t2r], self_sync=ss_, signal=False)
                for hf in halves:
                    op("dve", lambda: nc.vector.tensor_tensor(out=hf(t1), in0=hf(t1), in1=hf(t2), op=ALU.add),
                       reads=[rg.t1r, rg.t2r], writes=[rg.t1r], self_sync=ss_, signal=False)
                for hi, hf in enumerate(halves):
                    op("dve", lambda: nc.vector.tensor_tensor(out=hf(cur), in0=hf(cur), in1=hf(t1), op=ALU.add),
                       reads=[rg.t1r, rB], writes=[rB], self_sync=ss_, signal=(hi == 1))
                if inter and j in inter_at:
                    inter.pop(0)()

        def ssm_Y(uT, ureg, cols, n, sample, bufi, hold_last=False):
            BUv = st_["BU"][bufi]
            rB = st_["BUreg"][bufi]
            pcv = PS[:, 6, 0:16 * st_["TB"]].rearrange("p (k t) -> p k t", k=16)
            for ct in range(16):
                for q in range(4):
                    e = 4 * ct + q
                    for c in range(2):
                        op("pe", lambda: nc.tensor.matmul(pcv[32 * q:32 * q + 32, ct, 0:n], lhsT=Cm[:, e, c, :],
                                                          rhs=BUv[:, 0:n, c, e], start=(c == 0), stop=(c == 1),
                                                          tile_position=(0, 32 * q)),
                           reads=[rg.Cm, rB], writes=[rg.pc], signal=(ct == 15 and q == 3 and c == 1))
            if not sample:
                uv = uT[:, :, cols:cols + n]
                tyv = tY[:, :, 0:n]
                pv = pcv[:, :, 0:n]
                dbc = dcol[:].unsqueeze(2).to_broadcast([128, 16, n])
            else:
                uv = uT[:, :, cols:cols + 16].rearrange("p k (b t) -> p k t b", b=2)
                tyv = tY[:, :, 0:16].rearrange("p k (t b) -> p k t b", b=2)
                pv = pcv[:, :, 0:16].rearrange("p k (t b) -> p k t b", b=2)
                dbc = dcol[:].unsqueeze(2).unsqueeze(3).to_broadcast([128, 16, 8, 2])
            op("dve", lambda: nc.vector.tensor_tensor(out=tyv, in0=uv, in1=dbc, op=ALU.mult),
               reads=[ureg, rg.dcol], writes=[rg.tY])
            op("dve", lambda: nc.vector.tensor_tensor(out=tyv, in0=tyv, in1=pv, op=ALU.add),
               reads=[rg.tY, rg.pc], writes=[rg.tY])
            if sample or not hold_last:
                op("act", lambda: nc.scalar.activation(out=uv, in_=tyv, func=AF.Gelu_apprx_tanh),
                   reads=[rg.tY], writes=[ureg])
                return lambda: None
            op("act", lambda: nc.scalar.activation(out=uT[:, :, cols:cols + n - 1], in_=tY[:, :, 0:n - 1],
                                                   func=AF.Gelu_apprx_tanh), reads=[rg.tY], writes=[ureg])
            op("act", lambda: nc.scalar.activation(out=ypend[:], in_=tY[:, :, n - 1:n], func=AF.Gelu_apprx_tanh),
               reads=[rg.tY], writes=[rg.ypend])

            def flush():
                op("act", lambda: nc.scalar.activation(out=uT[:, :, cols + n - 1:cols + n], in_=ypend[:], func=AF.Copy),
                   reads=[rg.ypend], writes=[ureg])
            return flush

        def store_state_T(src_c_aps, dst_drams, row0, nrows):
            for c in range(2):
                op("pe", lambda: nc.tensor.transpose(PS[0:nrows, 7, 0:128], src_c_aps[c], ident[:]),
                   reads=[rg.BU[0], rg.BU[1], rg.Xp, rg.t2r, rg.ident], writes=[rg.pt])
                op("act", lambda: nc.scalar.activation(out=stg[0:nrows, c, :], in_=PS[0:nrows, 7, 0:128], func=AF.Copy,
                                                       scale=(1.0 if c == 0 else -1.0)),
                   reads=[rg.pt], writes=[rg.stg])
                dma("sp", dst_drams[c][row0:row0 + nrows, :], stg[0:nrows, c, :], reads=[rg.stg])

        def ssm_block(uT, ureg, ntok_prompt, blk, main, units):
            tb_ = 32
            st_["TB"] = tb_
            st_["BU"] = [BUP[0], BUP[0]] if main else BUP
            st_["BUreg"] = [rg.BU[0], rg.BU[0]] if main else rg.BU
            subs = [(t0, min(tb_, ntok_prompt - t0)) for t0 in range(0, ntok_prompt, tb_)]
            b0 = st_.get("bufi", 0)
            kb = st_.get("kblk", 0)
            st_["kblk"] = kb + 1
            ub = kb % 2
            op("dve", lambda: nc.vector.tensor_copy(out=ulast[:, 1 - ub, :], in_=uT[:, :, ntok_prompt - 1]),
               reads=[ureg], writes=[rg.ulast])
            nsub = len(subs)
            bf = lambda i: (b0 + i) % 2

            def do_R(i, inter):
                n_ = subs[i][1]
                ssm_R(n_, False, bf(i), None, inter=inter, inter_at=(0, 2, 4, 6, 8, 10))
                op("dve", lambda: nc.vector.tensor_copy(out=Xp[:], in_=st_["BU"][bf(i)][:, n_ - 2:n_]), reads=[st_["BUreg"][bf(i)]],
                   writes=[rg.Xp])

            ssm_B(uT, ureg, subs[0][0], subs[0][1], False, bf(0), first=True, ubuf=ub)
            if main:
                do_R(0, None)
                for i, (t0, n) in enumerate(subs):
                    fl = ssm_Y(uT, ureg, t0, n, False, bf(i), hold_last=True)
                    if i + 1 < nsub:
                        ssm_B(uT, ureg, subs[i + 1][0], subs[i + 1][1], False, bf(i + 1))
                    fl()
                    if i + 1 < nsub:
                        do_R(i + 1, units)
            else:
                if nsub > 1:
                    ssm_B(uT, ureg, subs[1][0], subs[1][1], False, bf(1))
                do_R(0, None)
                for i, (t0, n) in enumerate(subs):
                    if i + 2 < nsub:
                        ssm_B(uT, ureg, subs[i + 2][0], subs[i + 2][1], False, bf(i + 2))
                    if i + 1 < nsub:
                        do_R(i + 1, units)
            bufi = (b0 + len(subs)) % 2
            if main:
                st_["TB"] = 16
                st_["BU"] = [BU[0], BU[0]]
                for r in range(4):
                    seq0 = 8 * blk + 2 * r
                    for c in range(2):
                        dma("sp", stg[:, c, :], sst_in[c][seq0 * 64:(seq0 + 2) * 64, :], writes=[rg.stg])
                    for c in range(2):
                        op("pe", lambda: nc.tensor.transpose(PS[:, 7, c * 128:(c + 1) * 128], stg[:, c, :], ident[:]),
                           reads=[rg.stg, rg.ident], writes=[rg.pt])
                    op("act", lambda: nc.scalar.activation(
                        out=Xinit[:, :, 0, :], in_=PS[:, 7, 0:128].rearrange("p (b e) -> p b e", b=2), func=AF.Copy),
                       reads=[rg.pt], writes=[rg.Xinit])
                    op("act", lambda: nc.scalar.activation(
                        out=Xinit[:, :, 1, :], in_=PS[:, 7, 128:256].rearrange("p (b e) -> p b e", b=2), func=AF.Copy,
                        scale=-1.0), reads=[rg.pt], writes=[rg.Xinit])
                    ssm_B(uT, ureg, NP + 16 * r, 16, True, bufi)
                    ssm_R(16, True, bufi, None)
                    ssm_Y(uT, ureg, NP + 16 * r, 16, True, bufi)
                    op("dve", lambda: nc.vector.tensor_copy(
                        out=t2r[:].rearrange("p (c b e) -> p c b e", c=2, b=2),
                        in_=st_["BU"][bufi][:, 14:16, :, :].rearrange("p b c e -> p c b e")), reads=[st_["BUreg"][bufi]], writes=[rg.t2r])
                    fin = [t2r[:, c * 128:(c + 1) * 128] for c in range(2)]
                    store_state_T(fin, sst_out, seq0 * 64, 128)
                    bufi = 1 - bufi
                    if units:
                        units.pop(0)()
            st_["bufi"] = bufi
            while units:
                units.pop(0)()

        def conv_unit(ct, blk):
            hn = lambda kt: hnT[:, kt, :]

            def m1():
                ps = mm_group(win[16 + ct], KT, hn, [rg.hnT], MH)
                op("act", lambda: nc.scalar.activation(out=V2(tA[:], 1), in_=pm(ps, H), func=AF.Copy),
                   reads=[rg.pm[ps]], writes=[rg.tA])

            def m2():
                ps = mm_group(win[48 + ct], KT, hn, [rg.hnT], MH)
                op("dve", lambda: nc.vector.tensor_copy(out=Zp[:, 0:2], in_=Zhist[:, ct, :]), reads=[rg.Zhist], writes=[rg.Zp])
                op("dve", lambda: nc.vector.tensor_copy(out=Zsp[:, :, 0:2], in_=ZsInit[:, ct, 8 * blk:8 * blk + 8, :]),
                   reads=[rg.ZsInit], writes=[rg.Zsp])
                op("dve", lambda: nc.vector.tensor_tensor(out=Zp[:, 2:2 + H], in0=PS[:, 2 * ps, 0:H], in1=tA[:, 0:H],
                                                          op=ALU.mult), reads=[rg.pm[ps], rg.tA], writes=[rg.Zp])
                op("dve", lambda: nc.vector.tensor_tensor(out=Zp[:, 2 + H:2 + NP], in0=PS[:, 2 * ps + 1, 0:NP - H],
                                                          in1=tA[:, H:NP], op=ALU.mult),
                   reads=[rg.pm[ps], rg.tA], writes=[rg.Zp])
                op("dve", lambda: nc.vector.tensor_tensor(
                    out=Zsp[:, :, 2:10], in0=PS[:, 2 * ps + 1, NP - H:H].rearrange("p (b t) -> p b t", b=8),
                    in1=tA[:, NP:NT].rearrange("p (b t) -> p b t", b=8), op=ALU.mult),
                   reads=[rg.pm[ps], rg.tA], writes=[rg.Zsp])
                w = lambda k: convw[:, k, ct:ct + 1]
                accp = tB[:, 0:NP]
                accs = tB[:, NP:NT].rearrange("p (b t) -> p b t", b=8)
                RZ = [rg.Zp, rg.Zsp, rg.convw, rg.tB]
                op("dve", lambda: nc.vector.tensor_scalar(out=accp, in0=Zp[:, 2:2 + NP], scalar1=w(2), scalar2=None,
                                                          op0=ALU.mult), reads=RZ, writes=[rg.tB])
                op("dve", lambda: nc.vector.tensor_scalar(out=accs, in0=Zsp[:, :, 2:10], scalar1=w(2), scalar2=None,
                                                          op0=ALU.mult), reads=RZ, writes=[rg.tB])
                for k in (1, 0):
                    sh = 2 - k
                    op("dve", lambda: nc.vector.scalar_tensor_tensor(out=accp, in0=Zp[:, 2 - sh:2 - sh + NP], scalar=w(k),
                                                                     in1=accp, op0=ALU.mult, op1=ALU.add),
                       reads=RZ, writes=[rg.tB])
                    op("dve", lambda: nc.vector.scalar_tensor_tensor(out=accs, in0=Zsp[:, :, 2 - sh:10 - sh], scalar=w(k),
                                                                     in1=accs, op0=ALU.mult, op1=ALU.add),
                       reads=RZ, writes=[rg.tB])
                op("dve", lambda: nc.vector.tensor_copy(out=Zhist[:, ct, :], in_=Zp[:, NP:NP + 2]), reads=[rg.Zp],
                   writes=[rg.Zhist])
                op("dve", lambda: nc.vector.tensor_copy(out=ZsOut[:, ct, 8 * blk:8 * blk + 8, :], in_=Zsp[:, :, 8:10]),
                   reads=[rg.Zsp], writes=[rg.ZsOut])

            def m3():
                ps = mm_group(win[32 + ct], KT, hn, [rg.hnT], MH)
                op("dve", lambda: nc.vector.tensor_tensor(out=V2(ycT[:, ct, :], 1), in0=pm(ps, H), in1=V2(tB[:], 1),
                                                          op=ALU.mult), reads=[rg.pm[ps], rg.tB], writes=[rg.yc])
            return [m1, m2, m3]

        def sumsq_acc(ps, mt, nmt):
            op("act", lambda: nc.scalar.activation(out=V2(tC[:], 1), in_=pm(ps, H), func=AF.Square),
               reads=[rg.pm[ps]], writes=[rg.tC])
            for h in range(2):
                op("pe", lambda: nc.tensor.matmul(PS[:, 4 + h, 0:H], lhsT=ones[:], rhs=tC[:, h * H:(h + 1) * H],
                                                  start=(mt == 0), stop=(mt == nmt - 1)),
                   reads=[rg.ones, rg.tC], writes=[rg.pb[h]], signal=True)

        def rstdb_from_ps():
            op("dve", lambda: nc.vector.tensor_scalar(out=V2(rstdb[:], 1), in0=PS[:, 4:6, 0:H], scalar1=1.0 / D, scalar2=EPS,
                                                      op0=ALU.mult, op1=ALU.add), reads=[rg.pb[0], rg.pb[1]], writes=[rg.rstdb])
            op("act", lambda: nc.scalar.activation(out=rstdb[:], in_=rstdb[:], func=AF.Sqrt),
               reads=[rg.rstdb], writes=[rg.rstdb])
            op("dve", lambda: nc.vector.reciprocal(out=rstdb[:], in_=rstdb[:]), reads=[rg.rstdb], writes=[rg.rstdb])

        def main_block(blk):
            xsrc = xm[blk]
            S.barrier()
            prep(xsrc, NT)
            hn = lambda kt: hnT[:, kt, :]
            for mt in range(16):
                ps = mm_group(win[mt], KT, hn, [rg.hnT], MH)
                op("act", lambda: nc.scalar.activation(out=V2(ygT[:, mt, :], 1), in_=pm(ps, H), func=AF.Copy),
                   reads=[rg.pm[ps]], writes=[rg.yg])
            S.barrier()
            dma("sp", Df[:, 4096:8192], cm_d, writes=[rg.Cm])
            hn = lambda kt: hnT[:, kt, :]
            yg = lambda kt: ygT[:, kt, :]
            yc = lambda kt: ycT[:, kt, :]

            def gbyb_unit(mt):
                def g1():
                    ps = mm_group(win[96 + mt], KT, hn, [rg.hnT], MH)
                    op("act", lambda: nc.scalar.activation(out=V2(tC[:], 1), in_=pm(ps, H), func=AF.Sigmoid),
                       reads=[rg.pm[ps]], writes=[rg.tC])

                def g2():
                    ps = mm_group(wco[mt], 16, yc, [rg.yc], MH)
                    op("dve", lambda: nc.vector.tensor_tensor(out=V2(mgT[:, mt, :], 1), in0=pm(ps, H), in1=V2(tC[:], 1),
                                                              op=ALU.mult), reads=[rg.pm[ps], rg.tC], writes=[rg.mg])
                return [g1, g2]
            units = [m for ct in range(16) for m in conv_unit(ct, blk)] + [m for mt in range(32) for m in gbyb_unit(mt)]
            ssm_block(ygT, rg.yg, NP, blk, True, units)
            for mt in range(32):
                ps = mm_group(win[64 + mt], KT, hn, [rg.hnT], MH)
                op("act", lambda: nc.scalar.activation(out=V2(tA[:], 1), in_=pm(ps, H), func=AF.Sigmoid),
                   reads=[rg.pm[ps]], writes=[rg.tA])
                ps = mm_group(wgv[mt], 16, yg, [rg.yg], MH)
                op("dve", lambda: nc.vector.tensor_tensor(out=V2(tA[:], 1), in0=pm(ps, H), in1=V2(tA[:], 1), op=ALU.mult),
                   reads=[rg.pm[ps], rg.tA], writes=[rg.tA])
                ps = mm_group(wgg[mt], 16, yg, [rg.yg], MH)
                op("act", lambda: nc.scalar.activation(out=V2(tC[:], 1), in_=pm(ps, H), func=AF.Sigmoid),
                   reads=[rg.pm[ps]], writes=[rg.tC])
                op("dve", lambda: nc.vector.tensor_tensor(out=tA[:], in0=tA[:], in1=tC[:], op=ALU.mult),
                   reads=[rg.tA, rg.tC], writes=[rg.tA])
                op("dve", lambda: nc.vector.tensor_tensor(out=mgT[:, mt, :], in0=tA[:], in1=mgT[:, mt, :], op=ALU.add),
                   reads=[rg.tA, rg.mg], writes=[rg.mg])
            S.barrier()
            mg = lambda kt: mgT[:, kt, :]
            for mt in range(32):
                ps = mm_group(wo[mt], KT, mg, [rg.mg], MH)
                op("act", lambda: nc.scalar.activation(out=V2(OG[:, mt, :], 1), in_=pm(ps, H), func=AF.Copy,
                                                       scale=gcols[:, 1, mt:mt + 1]),
                   reads=[rg.pm[ps], rg.gcols], writes=[rg.OG])
                sumsq_acc(ps, mt, 32)
            rstdb_from_ps()
            for mt in range(32):
                op("dve", lambda: nc.vector.tensor_tensor(out=OG[:, mt, :], in0=OG[:, mt, :], in1=rstdb[:], op=ALU.mult),
                   reads=[rg.OG, rg.rstdb], writes=[rg.OG])
            for ti, (r0, sz) in enumerate(tiles_of(NT)):
                buf, breg = XT[ti % 2], rg.XT[ti % 2]
                dma("sp", buf[0:sz, :], xsrc[r0:r0 + sz, :], writes=[breg])
                for g4 in range(8):
                    for j in range(4):
                        kt = g4 * 4 + j
                        op("pe", lambda: nc.tensor.transpose(PS[0:sz, 7, j * 128:(j + 1) * 128], OG[:, kt, r0:r0 + sz], ident[:]),
                           reads=[rg.OG, rg.ident], writes=[rg.pt])
                    op("dve", lambda: nc.vector.tensor_tensor(out=buf[0:sz, g4 * 512:(g4 + 1) * 512],
                                                              in0=buf[0:sz, g4 * 512:(g4 + 1) * 512], in1=PS[0:sz, 7, :],
                                                              op=ALU.add), reads=[breg, rg.pt], writes=[breg])
                dma("sp", x1s[r0:r0 + sz, :], buf[0:sz, :], reads=[breg], writes=[rg.x1s])
                sumsq_rstd(buf, breg, sz, 2 + ti % 2)
                op("act", lambda: nc.scalar.activation(out=buf[0:sz, :], in_=buf[0:sz, :], func=AF.Copy,
                                                       scale=ss[0:sz, 2 + ti % 2:3 + ti % 2]),
                   reads=[breg, rg.ss], writes=[breg])
                transposes_to_T(buf, sz, 2, mgT, rg.mg, r0, breg)
            S.barrier()
            hf = lambda kt: mgT[:, kt, :]
            chunks = [(f0, min(FCH, FT - f0)) for f0 in range(0, FT, FCH)]
            for ci, (f0, nf) in enumerate(chunks):
                for j in range(nf):
                    ps = mm_group(wfg[f0 + j], KT, hf, [rg.mg], MH)
                    op("act", lambda: nc.scalar.activation(out=V2(tA[:], 1), in_=pm(ps, H), func=AF.Silu),
                       reads=[rg.pm[ps]], writes=[rg.tA])
                    ps = mm_group(wfu[f0 + j], KT, hf, [rg.mg], MH)
                    op("dve", lambda: nc.vector.tensor_tensor(out=V2(hbuf[:, j, :], 1), in0=pm(ps, H), in1=V2(tA[:], 1),
                                                              op=ALU.mult), reads=[rg.pm[ps], rg.tA], writes=[rg.hbuf])
                hb = lambda kt: hbuf[:, kt, :]
                lastc = ci == len(chunks) - 1
                for nt_ in range(32):
                    ps = mm_group(wfd[nt_][:, f0 * 128:(f0 + nf) * 128], nf, hb, [rg.hbuf], MH)
                    if ci == 0:
                        op("act", lambda: nc.scalar.activation(out=V2(OG[:, nt_, :], 1), in_=pm(ps, H), func=AF.Copy),
                           reads=[rg.pm[ps]], writes=[rg.OG])
                    else:
                        op("dve", lambda: nc.vector.tensor_tensor(out=V2(OG[:, nt_, :], 1), in0=pm(ps, H),
                                                                  in1=V2(OG[:, nt_, :], 1), op=ALU.add),
                           reads=[rg.pm[ps], rg.OG], writes=[rg.OG])
                    if lastc:
                        op("act", lambda: nc.scalar.activation(out=tC[:], in_=OG[:, nt_, :], func=AF.Square),
                           reads=[rg.OG], writes=[rg.tC])
                        for h in range(2):
                            op("pe", lambda: nc.tensor.matmul(PS[:, 4 + h, 0:H], lhsT=ones[:], rhs=tC[:, h * H:(h + 1) * H],
                                                              start=(nt_ == 0), stop=(nt_ == 31)),
                               reads=[rg.ones, rg.tC], writes=[rg.pb[h]], signal=True)
                        op("act", lambda: nc.scalar.activation(out=OG[:, nt_, :], in_=OG[:, nt_, :], func=AF.Copy,
                                                               scale=gcols[:, 3, nt_:nt_ + 1]),
                           reads=[rg.OG, rg.gcols], writes=[rg.OG])
            rstdb_from_ps()
            for mt in range(32):
                op("dve", lambda: nc.vector.tensor_tensor(out=OG[:, mt, :], in0=OG[:, mt, :], in1=rstdb[:], op=ALU.mult),
                   reads=[rg.OG, rg.rstdb], writes=[rg.OG])
            S.barrier()
            for ti, (r0, sz) in enumerate(tiles_of(NT)):
                buf, breg = XT[ti % 2], rg.XT[ti % 2]
                dma("sp", buf[0:sz, :], x1s[r0:r0 + sz, :], reads=[rg.x1s], writes=[breg])
                for g4 in range(8):
                    for j in range(4):
                        kt = g4 * 4 + j
                        op("pe", lambda: nc.tensor.transpose(PS[0:sz, 7, j * 128:(j + 1) * 128], OG[:, kt, r0:r0 + sz], ident[:]),
                           reads=[rg.OG, rg.ident], writes=[rg.pt])
                    op("dve", lambda: nc.vector.tensor_tensor(out=buf[0:sz, g4 * 512:(g4 + 1) * 512],
                                                              in0=buf[0:sz, g4 * 512:(g4 + 1) * 512], in1=PS[0:sz, 7, :],
                                                              op=ALU.add), reads=[breg, rg.pt], writes=[breg])
                dma("sp", y_out[blk][r0:r0 + sz, :], buf[0:sz, :], reads=[breg])

        def prefix_block(blk):
            S.barrier()
            prep(xp[blk], NP)
            if _DBG_SUB == 1:
                return
            hn = lambda kt: hnT[:, kt, :]
            for mt in range(16 if _DBG_SUB != 2 else 1):
                ps = mm_group(win[mt], KT, hn, [rg.hnT], PH)
                op("act", lambda: nc.scalar.activation(out=ygT[:, mt, 0:NP].rearrange("p (h n) -> p h n", h=2),
                                                       in_=pm(ps, HP), func=AF.Copy), reads=[rg.pm[ps]], writes=[rg.yg])
            if blk == 1:
                for ct in range(16):
                    ps = mm_group(win[16 + ct], KT, hn, [rg.hnT], [(NP - 2, NP)])
                    op("act", lambda: nc.scalar.activation(out=tA[:, 0:2], in_=PS[:, 2 * ps, 0:2], func=AF.Copy),
                       reads=[rg.pm[ps]], writes=[rg.tA])
                    ps = mm_group(win[48 + ct], KT, hn, [rg.hnT], [(NP - 2, NP)])
                    op("dve", lambda: nc.vector.tensor_tensor(out=Zhist[:, ct, :], in0=PS[:, 2 * ps, 0:2], in1=tA[:, 0:2],
                                                              op=ALU.mult), reads=[rg.pm[ps], rg.tA], writes=[rg.Zhist])
            if _DBG_SUB in (2, 3):
                return
            S.barrier()
            ssm_block(ygT, rg.yg, NP, blk, False, [])

        setup()
        if _DBG_STAGE >= 1:
            prefix_block(0)
        if _DBG_STAGE >= 2:
            prefix_block(1)
        if _DBG_STAGE >= 3:
            main_block(0)
        if _DBG_STAGE >= 4:
            main_block(1)
        S.barrier()
        store_state_T([Xp[:, 1, c, :] for c in range(2)], pst_out, 0, 64)
        for k in range(2):
            dma("sp", pcv_out[k].rearrange("(c p) -> p c", p=128), Zhist[:, :, k], reads=[rg.Zhist],
                allow_slow_non_contiguous=True)
        for ct in range(16):
            op("pe", lambda: nc.tensor.transpose(PS[0:32, 7, ct * 128 % 512:ct * 128 % 512 + 128],
                                                 ZsOut[:, ct].rearrange("p b k -> p (b k)"), ident[:]),
               reads=[rg.ZsOut, rg.ident], writes=[rg.pt])
            if ct % 4 == 3:
                g = ct // 4
                op("act", lambda: nc.scalar.activation(out=xtok[0:32, g * 512:(g + 1) * 512], in_=PS[0:32, 7, :], func=AF.Copy),
                   reads=[rg.pt], writes=[rg.xtok])
        dma("sp", scv_out, xtok[0:32, 0:2048], reads=[rg.xtok])
        sp = S.E["sp"]
        for s in S.sp_slots:
            if s.val:
                S._wait(sp, (s, s.val))
        for n in ("pe", "act", "dve"):
            e = S.E[n]
            if e.sem.val:
                S._wait(sp, (e.sem, e.sem.val))
    return nc


def _tile_w(w, nkt, nmt):
    return np.ascontiguousarray(w.reshape(nkt, 128, nmt, 128).transpose(2, 1, 0, 3)).reshape(nmt, 128, nkt * 128)


_NC_CACHE = {}


def kernel(x_prompt, x_sample, state_ssm_re, state_ssm_im, state_conv, meta_tokens, g_pre_mix, w_in,
           ssm_lambda_re, ssm_lambda_im, ssm_log_dt, ssm_b_re, ssm_b_im, ssm_c_re, ssm_c_im, ssm_d,
           w_glu_v, w_glu_g, conv_w, w_conv_out, w_o, g_post_mix, g_pre_ffn, w_ffn_gate, w_ffn_up,
           w_ffn_down, g_post_ffn):
    f32 = np.float32
    A = lambda a: np.asarray(a, dtype=f32)
    x_prompt, x_sample = A(x_prompt), A(x_sample)
    meta = A(meta_tokens)
    shared = {}
    shared["ident"] = np.eye(128, dtype=f32)
    gc = np.stack([A(g)[0].reshape(32, 128).T for g in (g_pre_mix, g_post_mix, g_pre_ffn, g_post_ffn)], axis=1)
    shared["gcols"] = np.ascontiguousarray(gc).reshape(128, 128)
    shared["convw"] = np.ascontiguousarray(A(conv_w)[0].reshape(3, 16, 128).transpose(2, 0, 1)).reshape(128, 48)
    shared["dcol"] = np.ascontiguousarray(A(ssm_d)[0].reshape(16, 128).T)
    lre, lim, ldt = A(ssm_lambda_re)[0], A(ssm_lambda_im)[0], A(ssm_log_dt)[0]
    l1f = lambda a: np.ascontiguousarray(a.reshape(64, 2, 64).transpose(1, 2, 0)).reshape(128, 64)
    ldt_gp = np.broadcast_to(ldt[:, None], (128, 64))
    shared["l1"] = np.stack([l1f(lre), l1f(lim), l1f(ldt_gp)])
    def l2_bcast(a):
        v = a.reshape(16, 4, 2, 64)
        o = np.broadcast_to(v[:, :, None, None, :, :], (16, 4, 2, 16, 2, 64))
        return np.ascontiguousarray(o.transpose(1, 2, 3, 0, 4, 5)).reshape(128, 2048)
    def l2_b(b):
        v = b.reshape(16, 4, 2, 64, 16)
        o = np.zeros((16, 4, 2, 16, 2, 64), f32)
        for g2 in range(2):
            o[:, :, g2, :, g2, :] = v[:, :, g2].transpose(0, 1, 3, 2)
        return np.ascontiguousarray(o.transpose(1, 2, 3, 0, 4, 5)).reshape(128, 2048)
    shared["l2"] = np.stack([l2_bcast(lre), l2_bcast(lim), l2_bcast(ldt_gp), l2_b(A(ssm_b_re)[0]), l2_b(A(ssm_b_im)[0])])
    cm = np.zeros((2, 64, 64, 2, 2, 16), f32)
    for c, carr in enumerate((A(ssm_c_re)[0], A(ssm_c_im)[0])):
        v = carr.reshape(64, 2, 16, 64)
        for g2 in range(2):
            cm[g2, :, :, c, g2, :] = v[:, g2].transpose(2, 0, 1)
    shared["cm"] = cm.reshape(128, 4096)
    shared["win"] = _tile_w(A(w_in)[0], 32, 128)
    shared["wgv"] = _tile_w(A(w_glu_v)[0], 16, 32)
    shared["wgg"] = _tile_w(A(w_glu_g)[0], 16, 32)
    shared["wco"] = _tile_w(A(w_conv_out)[0], 16, 32)
    shared["wo"] = _tile_w(A(w_o)[0], 32, 32)
    shared["wfg"] = _tile_w(A(w_ffn_gate)[0], 32, FT)
    shared["wfu"] = _tile_w(A(w_ffn_up)[0], 32, FT)
    shared["wfd"] = _tile_w(A(w_ffn_down)[0], FT, 32)

    sre, sim, scv = A(state_ssm_re)[0], A(state_ssm_im)[0], A(state_conv)[0]
    in_maps = []
    for c in range(8):
        bq, half = c // 2, c % 2
        full = np.concatenate([meta, x_prompt[bq]], axis=0)
        if half == 0:
            main_p = full[0:1032]
            pre = np.zeros((1032, D), f32)
        else:
            main_p = full[1032:2064]
            pre = full[0:1032]
        xs = x_sample[16 * c:16 * c + 16].reshape(2, 64, D)
        xm_ = np.concatenate([main_p.reshape(2, NP, D), xs], axis=1)
        m = dict(shared)
        m["xm"] = np.ascontiguousarray(xm_)
        m["xp"] = np.ascontiguousarray(pre.reshape(2, NP, D))
        m["sst_re_in"] = np.ascontiguousarray(sre[16 * c:16 * c + 16]).reshape(16 * 64, 128)
        m["sst_im_in"] = np.ascontiguousarray(sim[16 * c:16 * c + 16]).reshape(16 * 64, 128)
        m["scv_in"] = np.ascontiguousarray(scv[16 * c:16 * c + 16]).reshape(32, SW)
        in_maps.append(m)

    if "nc" not in _NC_CACHE:
        _NC_CACHE["nc"] = build_nc()
    nc = _NC_CACHE["nc"]
    res = run_bass_kernel_spmd(nc, in_maps, core_ids=list(range(8)))
    R = res.results

    y_prompt = np.zeros((4, 2048, D), f32)
    y_sample = np.zeros((128, 8, D), f32)
    p_re = np.zeros((1, 4, 128, 64), f32)
    p_im = np.zeros((1, 4, 128, 64), f32)
    p_cv = np.zeros((1, 4, 2, SW), f32)
    s_re = np.zeros((1, 128, 128, 64), f32)
    s_im = np.zeros((1, 128, 128, 64), f32)
    s_cv = np.zeros((1, 128, 2, SW), f32)
    for c in range(8):
        bq, half = c // 2, c % 2
        y = R[c]["y"]
        yp = y[:, 0:NP].reshape(1032, D)
        if half == 0:
            y_prompt[bq, 0:1016] = yp[16:]
        else:
            y_prompt[bq, 1016:2048] = yp
            p_re[0, bq] = R[c]["pst_re"].reshape(128, 64)
            p_im[0, bq] = R[c]["pst_im"].reshape(128, 64)
            p_cv[0, bq] = R[c]["pcv"]
        y_sample[16 * c:16 * c + 16] = y[:, NP:NT].reshape(16, 8, D)
        s_re[0, 16 * c:16 * c + 16] = R[c]["sst_re"].reshape(16, 128, 64)
        s_im[0, 16 * c:16 * c + 16] = R[c]["sst_im"].reshape(16, 128, 64)
        s_cv[0, 16 * c:16 * c + 16] = R[c]["scv"].reshape(16, 2, SW)
    return (y_prompt, y_sample, p_re, p_im, p_cv, s_re, s_im, s_cv)
```

```python
import math
from contextlib import ExitStack

import numpy as np
import concourse.bass as bass
import concourse.mybir as mybir
from concourse.bass_utils import run_bass_kernel_spmd

F32 = mybir.dt.float32
BF16 = mybir.dt.bfloat16
AF = mybir.ActivationFunctionType
ALU = mybir.AluOpType

D = 4096
KT = 32
SW = 2048
DFF = 11008
FT = 86
NT = 580
NP = 516
H = 290
HP = 258
EPS = 1e-6
TB = 16
FCH = 24
PI = math.pi
_DBG_STAGE = 4
_DBG_SUB = 0


class _Sem:
    def __init__(self, h):
        self.h = h
        self.val = 0


class _Eng:
    def __init__(self, name, h, sem):
        self.name = name
        self.h = h
        self.sem = sem
        self.waited = {}


class Region:
    __slots__ = ("w", "r")

    def __init__(self):
        self.w = None
        self.r = {}


class Sync:
    def __init__(self, nc, es):
        self.nc = nc
        mk = lambda n: _Sem(es.enter_context(nc.semaphore(n)))
        self.E = {
            "pe": _Eng("pe", nc.tensor, mk("s_pe")),
            "act": _Eng("act", nc.scalar, mk("s_act")),
            "dve": _Eng("dve", nc.vector, mk("s_dve")),
            "pool": _Eng("pool", nc.gpsimd, mk("s_pool")),
            "sp": _Eng("sp", nc.sync, mk("s_sp")),
        }
        self.sp_slots = [mk("s_d%d" % i) for i in range(12)]
        self.sp_i = 0

    def _wait(self, eng, st, self_sync=True):
        s, v = st
        if eng.waited.get(s, 0) >= v:
            return
        if s is eng.sem and (eng.name == "pe" or not self_sync or v > s.val):
            return
        eng.h.wait_ge(s.h, v)
        eng.waited[s] = v

    def _deps(self, eng, reads, writes, self_sync=True):
        for r in reads:
            if r.w:
                self._wait(eng, r.w, self_sync)
        for r in writes:
            if r.w:
                self._wait(eng, r.w, self_sync)
            for st in r.r.values():
                self._wait(eng, st, self_sync)

    def _mark(self, st, reads, writes):
        for r in writes:
            r.w = st
            r.r = {}
        for r in reads:
            r.r[st[0]] = st

    def op(self, en, fn, reads=(), writes=(), signal=True, self_sync=True):
        eng = self.E[en]
        self._deps(eng, reads, writes, self_sync)
        ins = fn()
        if signal:
            eng.sem.val += 1
            ins.then_inc(eng.sem.h, 1)
            st = (eng.sem, eng.sem.val)
        else:
            st = (eng.sem, eng.sem.val + 1)
        self._mark(st, reads, writes)
        return ins

    def dma(self, qn, out, in_, reads=(), writes=(), slot=None, **kw):
        q = self.E[qn]
        if slot is None:
            slot = self.sp_slots[self.sp_i]
            self.sp_i = (self.sp_i + 1) % len(self.sp_slots)
        if slot.val:
            self._wait(q, (slot, slot.val))
        self._deps(q, reads, writes)
        ins = q.h.dma_start(out=out, in_=in_, **kw)
        slot.val += 16
        ins.then_inc(slot.h, 16)
        st = (slot, slot.val)
        self._mark(st, reads, writes)
        return ins

    def barrier(self, extra_slots=()):
        names = ["pe", "act", "dve", "sp"]
        for a in names:
            ea = self.E[a]
            for b in names:
                if a == b:
                    continue
                eb = self.E[b]
                if eb.sem.val:
                    self._wait(ea, (eb.sem, eb.sem.val))
            for s in list(self.sp_slots) + list(extra_slots):
                if s.val:
                    self._wait(ea, (s, s.val))


def build_nc():
    nc = bass.Bass("TRN2", target_bir_lowering=False)
    dt_in = lambda n, s: nc.dram_tensor(n, s, F32, kind="ExternalInput").ap()
    dt_out = lambda n, s: nc.dram_tensor(n, s, F32, kind="ExternalOutput").ap()

    xm = dt_in("xm", [2, NT, D])
    xp = dt_in("xp", [2, NP, D])
    sst_in = [dt_in("sst_re_in", [16 * 64, 128]), dt_in("sst_im_in", [16 * 64, 128])]
    scv_in = dt_in("scv_in", [32, SW])
    ident_d = dt_in("ident", [128, 128])
    gcols_d = dt_in("gcols", [128, 4 * 32])
    convw_d = dt_in("convw", [128, 48])
    dcol_d = dt_in("dcol", [128, 16])
    l1_d = dt_in("l1", [3, 128, 64])
    l2_d = dt_in("l2", [5, 128, 2048])
    cm_d = dt_in("cm", [128, 4096])
    win = dt_in("win", [128, 128, 4096])
    wgv = dt_in("wgv", [32, 128, 2048])
    wgg = dt_in("wgg", [32, 128, 2048])
    wco = dt_in("wco", [32, 128, 2048])
    wo = dt_in("wo", [32, 128, 4096])
    wfg = dt_in("wfg", [FT, 128, 4096])
    wfu = dt_in("wfu", [FT, 128, 4096])
    wfd = dt_in("wfd", [32, 128, FT * 128])

    y_out = dt_out("y", [2, NT, D])
    pst_out = [dt_out("pst_re", [64, 128]), dt_out("pst_im", [64, 128])]
    pcv_out = dt_out("pcv", [2, SW])
    sst_out = [dt_out("sst_re", [16 * 64, 128]), dt_out("sst_im", [16 * 64, 128])]
    scv_out = dt_out("scv", [32, SW])
    x1s = nc.dram_tensor("x1s", [NT, D], F32).ap()

    with ExitStack() as es:
        sb = lambda n, s, d: es.enter_context(nc.sbuf_tensor("sb_" + n, s, d))
        S = Sync(nc, es)
        op, dma = S.op, S.dma
        w_slots = [_Sem(es.enter_context(nc.semaphore("s_w%d" % i))) for i in range(3)]

        ident = sb("ident", [128, 128], F32)
        ones = sb("ones", [128, 128], F32)
        gcols = sb("gcols", [128, 4, 32], F32)
        convw = sb("convw", [128, 3, 16], F32)
        dcol = sb("dcol", [128, 16], F32)
        CA = sb("CA", [128, 2, 64], F32)
        CB = sb("CB", [128, 2, 64], F32)
        Bt = sb("Bt", [128, 2, 16, 128], BF16)
        Xp = sb("Xp", [128, 2, 2, 64], F32)
        CA2 = sb("CA2", [128, 2, 64], F32)
        CB2 = sb("CB2", [128, 2, 64], F32)
        Bt2 = sb("Bt2", [128, 2, 16, 128], BF16)
        ulast = sb("ulast", [128, 2, 16], BF16)
        ypend = sb("ypend", [128, 16, 1], BF16)
        Xinit = sb("Xinit", [128, 2, 2, 64], F32)
        Zhist = sb("Zhist", [128, 16, 2], F32)
        ZsInit = sb("ZsInit", [128, 16, 16, 2], F32)
        ZsOut = sb("ZsOut", [128, 16, 16, 2], F32)
        rstdb = sb("rstdb", [128, NT], F32)
        tA = sb("tA", [128, NT], F32)
        tB = sb("tB", [128, NT], F32)
        tC = sb("tC", [128, NT], F32)
        Zp = sb("Zp", [128, NP + 2], F32)
        Zsp = sb("Zsp", [128, 8, 10], F32)
        ss = sb("ss", [128, 4], F32)
        junk = sb("junk", [128, 4], F32)
        stg = sb("stg", [128, 2, 128], F32)
        tY = sb("tY", [128, 16, 32], F32)
        t1r = sb("t1r", [128, 256], F32)
        t2r = sb("t2r", [128, 256], F32)
        R1f = sb("R1", [128, 18560], F32)
        R2b = sb("R2", [128, 18560], BF16)
        Wr = [sb("wr%d" % i, [128, 4096], BF16) for i in range(3)]
        Df = sb("Dd", [128, 8192], F32)
        PS = es.enter_context(nc.psum_tensor("PS", [128, 8, 512], F32))

        R1b = R1f[:].bitcast(BF16)
        hnT = R1b[:, 0:18560].rearrange("p (k n) -> p k n", k=32)
        ygT = R1b[:, 18560:27840].rearrange("p (k n) -> p k n", k=16)
        ycT = R1b[:, 27840:37120].rearrange("p (k n) -> p k n", k=16)
        OG = R1f[:].rearrange("p (k n) -> p k n", k=32)
        mgT = R2b[:].rearrange("p (k n) -> p k n", k=32)
        xtok = Df[:, 0:4096]
        xtok2 = Df[:, 4096:8192]
        BU = [Df[:, i * 2048:(i + 1) * 2048].rearrange("p (t c e) -> p t c e", t=TB, c=2) for i in range(2)]
        BUP = [Df[:, i * 4096:(i + 1) * 4096].rearrange("p (t c e) -> p t c e", t=32, c=2) for i in range(2)]
        Cm = Df[:, 4096:8192].rearrange("p (e c m) -> p e c m", e=64, c=2)
        hbuf = Df[:].bitcast(BF16)[:, 0:FCH * NT].rearrange("p (k n) -> p k n", k=FCH)

        class RG:
            pass
        rg = RG()
        for n in ("ident ones gcols convw dcol CA CB CA2 CB2 Bt2 ulast ypend Bt Xp Xinit Zhist ZsInit ZsOut rstdb tA tB tC Zp Zsp ss junk stg tY t1r t2r "
                  "hnT yg yc OG mg xtok xtok2 Cm hbuf x1s").split():
            setattr(rg, n, Region())
        rg.wr = [Region() for _ in range(3)]
        rg.BU = [Region() for _ in range(2)]
        rg.pm = [Region() for _ in range(2)]
        rg.pb = [Region() for _ in range(2)]
        rg.pc = Region()
        rg.pt = Region()

        pm = lambda s, w: PS[:, 2 * s:2 * s + 2, 0:w]
        st_ = {"w": 0, "p": 0, "pt": 0}

        def V2(ap, w):
            return ap.rearrange("p (h n) -> p h n", h=2) if w else ap

        def mm_group(wsrc, nkt, rhs_fn, rhs_regs, halves):
            slot = st_["w"]
            st_["w"] = (slot + 1) % 3
            ps = st_["p"]
            st_["p"] = (ps + 1) % 2
            nel = nkt * 128
            bsz = max(b_ for b_ in range(128, 2049, 128) if nel % b_ == 0)
            dma("pool", Wr[slot][:, 0:nel].rearrange("p (a b) -> p a b", b=bsz),
                wsrc.rearrange("p (a b) -> p a b", b=bsz),
                reads=(), writes=(rg.wr[slot],), slot=w_slots[slot])
            nh = len(halves)
            for kt in range(nkt):
                for h, (c0, c1) in enumerate(halves):
                    last = (kt == nkt - 1) and (h == nh - 1)
                    op("pe", lambda: nc.tensor.matmul(PS[:, 2 * ps + h, 0:c1 - c0],
                                                      lhsT=Wr[slot][:, kt * 128:(kt + 1) * 128],
                                                      rhs=rhs_fn(kt)[:, c0:c1],
                                                      start=(kt == 0), stop=(kt == nkt - 1)),
                       reads=[rg.wr[slot]] + list(rhs_regs), writes=[rg.pm[ps]], signal=last)
            return ps

        MH = [(0, H), (H, NT)]
        PH = [(0, HP), (HP, NP)]

        def setup():
            dma("sp", ident[:], ident_d, writes=[rg.ident])
            dma("sp", gcols[:], gcols_d.rearrange("p (a b) -> p a b", a=4), writes=[rg.gcols])
            dma("sp", convw[:], convw_d.rearrange("p (a b) -> p a b", a=3), writes=[rg.convw])
            dma("sp", dcol[:], dcol_d, writes=[rg.dcol])
            op("dve", lambda: nc.vector.memset(ones[:], 1.0), writes=[rg.ones])
            op("dve", lambda: nc.vector.memset(Xp[:], 0.0), writes=[rg.Xp])
            op("dve", lambda: nc.vector.memset(ulast[:], 0.0), writes=[rg.ulast])
            op("dve", lambda: nc.vector.memset(Zhist[:], 0.0), writes=[rg.Zhist])

            rtmp = Region()

            def lam_bar(lre, lim, ldt, tmp, n):
                dtv, ldr, ldi, er, a1, a2 = tmp[:6]
                R = [rtmp]
                op("act", lambda: nc.scalar.activation(out=dtv, in_=ldt, func=AF.Exp), reads=R, writes=R)
                op("dve", lambda: nc.vector.tensor_tensor(out=ldr, in0=lre, in1=dtv, op=ALU.mult), reads=R, writes=R)
                op("dve", lambda: nc.vector.tensor_tensor(out=ldi, in0=lim, in1=dtv, op=ALU.mult), reads=R, writes=R)
                op("act", lambda: nc.scalar.activation(out=er, in_=ldr, func=AF.Exp), reads=R, writes=R)
                TS = lambda o, i, s1, s2, o0, o1=None: op("dve", lambda: (nc.vector.tensor_scalar(out=o, in0=i, scalar1=s1, scalar2=s2, op0=o0, op1=o1)
                                                                         if o1 is not None else
                                                                         nc.vector.tensor_scalar(out=o, in0=i, scalar1=s1, scalar2=None, op0=o0)),
                                                         reads=R, writes=R)
                TT_ = lambda o, a, b, o_: op("dve", lambda: nc.vector.tensor_tensor(out=o, in0=a, in1=b, op=o_), reads=R, writes=R)
                TS(a1, ldi, 1.0 / (2 * PI), None, ALU.mult)
                op("dve", lambda: nc.vector.tensor_copy(out=a2.bitcast(mybir.dt.int32), in_=a1), reads=R, writes=R)
                op("dve", lambda: nc.vector.tensor_copy(out=a1, in_=a2.bitcast(mybir.dt.int32)), reads=R, writes=R)
                op("dve", lambda: nc.vector.scalar_tensor_tensor(out=a1, in0=a1, scalar=-2 * PI, in1=ldi, op0=ALU.mult,
                                                                 op1=ALU.add), reads=R, writes=R)
                TS(dtv, a1, PI, 2 * PI, ALU.is_gt, ALU.mult)
                TT_(a1, a1, dtv, ALU.subtract)
                TS(dtv, a1, -PI, 2 * PI, ALU.is_lt, ALU.mult)
                TT_(a1, a1, dtv, ALU.add)
                TS(a2, a1, PI / 2, None, ALU.add)
                TS(dtv, a2, PI, 2 * PI, ALU.is_gt, ALU.mult)
                TT_(a2, a2, dtv, ALU.subtract)
                op("act", lambda: nc.scalar.activation(out=a1, in_=a1, func=AF.Sin), reads=R, writes=R)
                op("act", lambda: nc.scalar.activation(out=a2, in_=a2, func=AF.Sin), reads=R, writes=R)
                op("dve", lambda: nc.vector.tensor_tensor(out=a2, in0=a2, in1=er, op=ALU.mult), reads=R, writes=R)
                op("dve", lambda: nc.vector.tensor_tensor(out=a1, in0=a1, in1=er, op=ALU.mult), reads=R, writes=R)
                return a2, a1

            t1 = [Df[:, i * 64:(i + 1) * 64] for i in range(12)]
            for i in range(3):
                dma("sp", t1[i], l1_d[i], writes=[rtmp])
            lbr, lbi = lam_bar(t1[0], t1[1], t1[2], t1[3:9], 64)
            R = [rtmp]
            for c in range(2):
                op("dve", lambda: nc.vector.tensor_copy(out=CA[:, c, :], in_=lbr), reads=R, writes=[rg.CA])
            op("dve", lambda: nc.vector.tensor_copy(out=CB[:, 0, :], in_=lbi), reads=R, writes=[rg.CB])
            op("dve", lambda: nc.vector.tensor_scalar(out=CB[:, 1, :], in0=lbi, scalar1=-1.0, scalar2=None,
                                                      op0=ALU.mult), reads=R, writes=[rg.CB])
            q1, q2 = t1[9], t1[10]
            TT1 = lambda o, a, b, o_, w=(rtmp,): op("dve", lambda: nc.vector.tensor_tensor(out=o, in0=a, in1=b, op=o_), reads=R, writes=list(w))
            TT1(q1, lbr, lbr, ALU.mult)
            TT1(q2, lbi, lbi, ALU.mult)
            TT1(q1, q1, q2, ALU.subtract)
            TT1(q2, lbr, lbi, ALU.mult)
            for c in range(2):
                op("dve", lambda: nc.vector.tensor_copy(out=CA2[:, c, :], in_=q1), reads=R, writes=[rg.CA2])
            op("dve", lambda: nc.vector.tensor_scalar(out=CB2[:, 0, :], in0=q2, scalar1=2.0, scalar2=None, op0=ALU.mult),
               reads=R, writes=[rg.CB2])
            op("dve", lambda: nc.vector.tensor_scalar(out=CB2[:, 1, :], in0=q2, scalar1=-2.0, scalar2=None, op0=ALU.mult),
               reads=R, writes=[rg.CB2])
            S.barrier()
            t2 = [R1f[:, i * 2048:(i + 1) * 2048] for i in range(9)] + [Df[:, i * 2048:(i + 1) * 2048] for i in range(4)]
            for i in range(5):
                dma("sp", t2[i], l2_d[i], writes=[rtmp])
            lre, lim, ldt, bre, bim = t2[0:5]
            lbr, lbi = lam_bar(lre, lim, ldt, t2[5:11], 2048)
            nr, m2 = t2[11], t2[12]
            f1, f2 = t2[5], t2[6]
            TT = lambda o, a, b, o_: op("dve", lambda: nc.vector.tensor_tensor(out=o, in0=a, in1=b, op=o_), reads=R, writes=R)
            op("dve", lambda: nc.vector.tensor_scalar(out=nr, in0=lbr, scalar1=-1.0, scalar2=None, op0=ALU.add),
               reads=R, writes=R)
            TT(m2, lre, lre, ALU.mult)
            TT(f1, lim, lim, ALU.mult)
            TT(m2, m2, f1, ALU.add)
            op("dve", lambda: nc.vector.reciprocal(out=m2, in_=m2), reads=R, writes=R)
            TT(f1, nr, lre, ALU.mult)
            TT(f2, lbi, lim, ALU.mult)
            TT(f1, f1, f2, ALU.add)
            TT(f1, f1, m2, ALU.mult)
            TT(f2, lbi, lre, ALU.mult)
            TT(nr, nr, lim, ALU.mult)
            TT(f2, f2, nr, ALU.subtract)
            TT(f2, f2, m2, ALU.mult)
            X1, X2, sc1 = t2[0], t2[1], t2[2]
            TT(nr, f1, bre, ALU.mult)
            TT(m2, f2, bim, ALU.mult)
            TT(X1, nr, m2, ALU.subtract)
            TT(nr, f1, bim, ALU.mult)
            TT(m2, f2, bre, ALU.mult)
            TT(nr, nr, m2, ALU.add)
            op("dve", lambda: nc.vector.tensor_scalar(out=X2, in0=nr, scalar1=-1.0, scalar2=None, op0=ALU.mult),
               reads=R, writes=R)
            flat = lambda t: t.rearrange("p a b -> p (a b)")
            op("dve", lambda: nc.vector.tensor_copy(out=flat(Bt[:, 0]), in_=X1), reads=R, writes=[rg.Bt])
            op("dve", lambda: nc.vector.tensor_copy(out=flat(Bt[:, 1]), in_=X2), reads=R, writes=[rg.Bt])
            TT(nr, lbr, X1, ALU.mult)
            TT(m2, lbi, X2, ALU.mult)
            op("dve", lambda: nc.vector.tensor_tensor(out=flat(Bt2[:, 0]), in0=nr, in1=m2, op=ALU.add), reads=R, writes=[rg.Bt2])
            TT(nr, lbr, X2, ALU.mult)
            TT(m2, lbi, X1, ALU.mult)
            op("dve", lambda: nc.vector.tensor_tensor(out=flat(Bt2[:, 1]), in0=nr, in1=m2, op=ALU.subtract), reads=R,
               writes=[rg.Bt2])
            S.barrier()
            dma("sp", Df[0:32, 0:2048], scv_in, writes=[rg.xtok])
            for ct in range(16):
                op("pe", lambda: nc.tensor.transpose(PS[:, 7, 0:32], Df[0:32, ct * 128:(ct + 1) * 128], ident[0:32, 0:32]),
                   reads=[rg.xtok, rg.ident], writes=[rg.pt])
                op("act", lambda: nc.scalar.activation(out=ZsInit[:, ct].rearrange("p b k -> p (b k)"), in_=PS[:, 7, 0:32],
                                                       func=AF.Copy), reads=[rg.pt], writes=[rg.ZsInit])
            S.barrier()

        def tiles_of(n):
            return [(i, min(128, n - i)) for i in range(0, n, 128)]

        def rstd_from_ss(col):
            op("dve", lambda: nc.vector.tensor_scalar(out=ss[:, col:col + 1], in0=ss[:, col:col + 1], scalar1=1.0 / D,
                                                      scalar2=EPS, op0=ALU.mult, op1=ALU.add), reads=[rg.ss], writes=[rg.ss])
            op("act", lambda: nc.scalar.activation(out=ss[:, col:col + 1], in_=ss[:, col:col + 1], func=AF.Sqrt),
               reads=[rg.ss], writes=[rg.ss])
            op("dve", lambda: nc.vector.reciprocal(out=ss[:, col:col + 1], in_=ss[:, col:col + 1]),
               reads=[rg.ss], writes=[rg.ss])

        def transposes_to_T(src, sz, gi, dstT, dreg, c0, sreg=None):
            for g4 in range(8):
                for j in range(4):
                    kt = g4 * 4 + j
                    op("pe", lambda: nc.tensor.transpose(PS[:, 7, j * 128:j * 128 + sz], src[0:sz, kt * 128:(kt + 1) * 128],
                                                         ident[0:sz, 0:sz]),
                       reads=([sreg] if sreg is not None else [rg.xtok, rg.xtok2]) + [rg.ident], writes=[rg.pt])
                op("dve", lambda: nc.vector.tensor_tensor(
                    out=dstT[:, g4 * 4:g4 * 4 + 4, c0:c0 + sz],
                    in0=PS[:, 7, :].rearrange("p (j n) -> p j n", j=4)[:, :, 0:sz],
                    in1=gcols[:, gi, g4 * 4:g4 * 4 + 4].unsqueeze(2).to_broadcast([128, 4, sz]), op=ALU.mult),
                   reads=[rg.pt, rg.gcols], writes=[dreg])

        XT = [xtok, xtok2]
        rg.XT = [rg.xtok, rg.xtok2]

        def sumsq_rstd(buf, breg, sz, col):
            op("dve", lambda: nc.vector.memset(ss[:, col:col + 1], 0.0), writes=[rg.ss])
            op("act", lambda: nc.scalar.activation(out=junk[0:sz, 0:1].to_broadcast([sz, D]), in_=buf[0:sz, :], func=AF.Square,
                                                   accum_out=ss[0:sz, col:col + 1]),
               reads=[breg], writes=[rg.junk, rg.ss])
            rstd_from_ss(col)

        def prep(xsrc, ntok):
            for ti, (r0, sz) in enumerate(tiles_of(ntok)):
                buf, breg = XT[ti % 2], rg.XT[ti % 2]
                dma("sp", buf[0:sz, :], xsrc[r0:r0 + sz, :], writes=[breg])
                sumsq_rstd(buf, breg, sz, ti % 2)
                op("act", lambda: nc.scalar.activation(out=buf[0:sz, :], in_=buf[0:sz, :], func=AF.Copy,
                                                       scale=ss[0:sz, ti % 2:ti % 2 + 1]),
                   reads=[breg, rg.ss], writes=[breg])
                transposes_to_T(buf, sz, 0, hnT, rg.hnT, r0, breg)

        def ssm_B(uT, ureg, cols, n, sample, bufi, first=False, ubuf=0):
            BUv = st_["BU"][bufi]
            rB = st_["BUreg"][bufi]
            tb = st_["TB"]
            ncl = 128 // tb
            for pas in range(16 // ncl):
                for q in range(4):
                    par = q % 2
                    qh = q // 2
                    pbv = PS[:, 4 + par, :].rearrange("p (j c t) -> p j c t", j=2 * ncl, c=2)
                    for cl in range(ncl):
                        ct = ncl * pas + cl
                        j = 2 * cl + qh
                        for c in range(2):
                            rows = slice(32 * q, 32 * q + 32)
                            if not sample:
                                items = [(Bt, uT[rows, ct, cols:cols + n], pbv[:, j, c, 0:n])]
                                if first:
                                    items.append((Bt2, ulast[rows, ubuf, ct:ct + 1], pbv[:, j, c, 0:1]))
                                    items.append((Bt2, uT[rows, ct, cols:cols + n - 1], pbv[:, j, c, 1:n]))
                                else:
                                    items.append((Bt2, uT[rows, ct, cols - 1:cols + n - 1], pbv[:, j, c, 0:n]))
                            else:
                                items = [(Bt, uT[rows, ct, cols + 8 * b_:cols + 8 * b_ + 8],
                                          pbv[:, j, c, b_:16:2]) for b_ in range(2)]
                            for ii, (bm, rhs, o_) in enumerate(items):
                                st_flag = True if sample else (ii == 0)
                                sp_flag = True if sample else (ii == len(items) - 1)
                                op("pe", lambda: nc.tensor.matmul(o_, lhsT=bm[rows, c, ct, :], rhs=rhs,
                                                                  start=st_flag, stop=sp_flag, tile_position=(32 * q, 0)),
                                   reads=[rg.Bt, rg.Bt2, rg.ulast, ureg], writes=[rg.pb[par]],
                                   signal=(cl == ncl - 1 and c == 1 and ii == len(items) - 1))
                for par in range(2):
                    pbv = PS[:, 4 + par, :].rearrange("p (j c t) -> p j c t", j=2 * ncl, c=2)
                    e0 = 4 * ncl * pas + par
                    op("act", lambda: nc.scalar.activation(
                        out=BUv[:, 0:n, :, e0:e0 + 4 * ncl - 1:2].rearrange("p t c e -> p e c t"),
                        in_=pbv[:, :, :, 0:n], func=AF.Copy), reads=[rg.pb[par]], writes=[rB])

        def ssm_R(n, sample, bufi, prev_ap, inter=None, inter_at=()):
            BUv = st_["BU"][bufi]
            rB = st_["BUreg"][bufi]
            if not sample:
                assert n % 2 == 0
                nsteps = n // 2
                halves = [lambda a, h=h: a[:, h] for h in range(2)]
                get_prev = lambda j: Xp[:] if j == 0 else BUv[:, 2 * (j - 1):2 * j]
                get_cur = lambda j: BUv[:, 2 * j:2 * j + 2]
                ca = CA2[:].unsqueeze(1).to_broadcast([128, 2, 2, 64])
                cb = CB2[:].unsqueeze(1).to_broadcast([128, 2, 2, 64])
                t1 = t1r[:].rearrange("p (b c e) -> p b c e", b=2, c=2)
                t2 = t2r[:].rearrange("p (b c e) -> p b c e", b=2, c=2)
                sw = lambda a: a[:, ::-1, :]
            else:
                nsteps = 8
                halves = [lambda a, h=h: a[:, h] for h in range(2)]
                get_prev = lambda j: Xinit[:] if j == 0 else BUv[:, 2 * (j - 1):2 * j]
                get_cur = lambda j: BUv[:, 2 * j:2 * j + 2]
                ca = CA[:].unsqueeze(1).to_broadcast([128, 2, 2, 64])
                cb = CB[:].unsqueeze(1).to_broadcast([128, 2, 2, 64])
                t1 = t1r[:].rearrange("p (b c e) -> p b c e", b=2, c=2)
                t2 = t2r[:].rearrange("p (b c e) -> p b c e", b=2, c=2)
                sw = lambda a: a[:, ::-1, :]
            RR = [rB, rg.BU[0], rg.BU[1], rg.Xp, rg.Xinit, rg.CA, rg.CB, rg.CA2, rg.CB2]
            for j in range(nsteps):
                prev, cur = get_prev(j), get_cur(j)
                ss_ = (j == 0)
                for hf in halves:
                    op("dve", lambda: nc.vector.tensor_tensor(out=hf(t1), in0=hf(prev), in1=hf(ca), op=ALU.mult),
                       reads=RR, writes=[rg.t1r], self_sync=ss_, signal=False)
                for hf in halves:
                    op("dve", lambda: nc.vector.tensor_tensor(out=hf(t2), in0=sw(hf(prev)), in1=hf(cb), op=ALU.mult),
                       reads=RR, writes=[rg.t2r], self_sync=ss_, signal=False)
                for hf in halves:
                    op("dve", lambda: nc.vector.tensor_tensor(out=hf(t1), in0=hf(t1), in1=hf(t2), op=ALU.add),
                       reads=[rg.t1r, rg.t2r], writes=[rg.t1r], self_sync=ss_, signal=False)
                for hi, hf in enumerate(halves):
                    op("dve", lambda: nc.vector.tensor_tensor(out=hf(cur), in0=hf(cur), in1=hf(t1), op=ALU.add),
                       reads=[rg.t1r, rB], writes=[rB], self_sync=ss_, signal=(hi == 1))
                if inter and j in inter_at:
                    inter.pop(0)()

        def ssm_Y(uT, ureg, cols, n, sample, bufi, hold_last=False):
            BUv = st_["BU"][bufi]
            rB = st_["BUreg"][bufi]
            pcv = PS[:, 6, 0:16 * st_["TB"]].rearrange("p (k t) -> p k t", k=16)
            for ct in range(16):
                for q in range(4):
                    e = 4 * ct + q
                    for c in range(2):
                        op("pe", lambda: nc.tensor.matmul(pcv[32 * q:32 * q + 32, ct, 0:n], lhsT=Cm[:, e, c, :],
                                                          rhs=BUv[:, 0:n, c, e], start=(c == 0), stop=(c == 1),
                                                          tile_position=(0, 32 * q)),
                           reads=[rg.Cm, rB], writes=[rg.pc], signal=(ct == 15 and q == 3 and c == 1))
            if not sample:
                uv = uT[:, :, cols:cols + n]
                tyv = tY[:, :, 0:n]
                pv = pcv[:, :, 0:n]
                dbc = dcol[:].unsqueeze(2).to_broadcast([128, 16, n])
            else:
                uv = uT[:, :, cols:cols + 16].rearrange("p k (b t) -> p k t b", b=2)
                tyv = tY[:, :, 0:16].rearrange("p k (t b) -> p k t b", b=2)
                pv = pcv[:, :, 0:16].rearrange("p k (t b) -> p k t b", b=2)
                dbc = dcol[:].unsqueeze(2).unsqueeze(3).to_broadcast([128, 16, 8, 2])
            op("dve", lambda: nc.vector.tensor_tensor(out=tyv, in0=uv, in1=dbc, op=ALU.mult),
               reads=[ureg, rg.dcol], writes=[rg.tY])
            op("dve", lambda: nc.vector.tensor_tensor(out=tyv, in0=tyv, in1=pv, op=ALU.add),
               reads=[rg.tY, rg.pc], writes=[rg.tY])
            if sample or not hold_last:
                op("act", lambda: nc.scalar.activation(out=uv, in_=tyv, func=AF.Gelu_apprx_tanh),
                   reads=[rg.tY], writes=[ureg])
                return lambda: None
            op("act", lambda: nc.scalar.activation(out=uT[:, :, cols:cols + n - 1], in_=tY[:, :, 0:n - 1],
                                                   func=AF.Gelu_apprx_tanh), reads=[rg.tY], writes=[ureg])
            op("act", lambda: nc.scalar.activation(out=ypend[:], in_=tY[:, :, n - 1:n], func=AF.Gelu_apprx_tanh),
               reads=[rg.tY], writes=[rg.ypend])

            def flush():
                op("act", lambda: nc.scalar.activation(out=uT[:, :, cols + n - 1:cols + n], in_=ypend[:], func=AF.Copy),
                   reads=[rg.ypend], writes=[ureg])
            return flush

        def store_state_T(src_c_aps, dst_drams, row0, nrows):
            for c in range(2):
                op("pe", lambda: nc.tensor.transpose(PS[0:nrows, 7, 0:128], src_c_aps[c], ident[:]),
                   reads=[rg.BU[0], rg.BU[1], rg.Xp, rg.t2r, rg.ident], writes=[rg.pt])
                op("act", lambda: nc.scalar.activation(out=stg[0:nrows, c, :], in_=PS[0:nrows, 7, 0:128], func=AF.Copy,
                                                       scale=(1.0 if c == 0 else -1.0)),
                   reads=[rg.pt], writes=[rg.stg])
                dma("sp", dst_drams[c][row0:row0 + nrows, :], stg[0:nrows, c, :], reads=[rg.stg])

        def ssm_block(uT, ureg, ntok_prompt, blk, main, units):
            tb_ = 32
            st_["TB"] = tb_
            st_["BU"] = [BUP[0], BUP[0]] if main else BUP
            st_["BUreg"] = [rg.BU[0], rg.BU[0]] if main else rg.BU
            subs = [(t0, min(tb_, ntok_prompt - t0)) for t0 in range(0, ntok_prompt, tb_)]
            b0 = st_.get("bufi", 0)
            kb = st_.get("kblk", 0)
            st_["kblk"] = kb + 1
            ub = kb % 2
            op("dve", lambda: nc.vector.tensor_copy(out=ulast[:, 1 - ub, :], in_=uT[:, :, ntok_prompt - 1]),
               reads=[ureg], writes=[rg.ulast])
            nsub = len(subs)
            bf = lambda i: (b0 + i) % 2

            def do_R(i, inter):
                n_ = subs[i][1]
                ssm_R(n_, False, bf(i), None, inter=inter, inter_at=(0, 2, 4, 6, 8, 10))
                op("dve", lambda: nc.vector.tensor_copy(out=Xp[:], in_=st_["BU"][bf(i)][:, n_ - 2:n_]), reads=[st_["BUreg"][bf(i)]],
                   writes=[rg.Xp])

            ssm_B(uT, ureg, subs[0][0], subs[0][1], False, bf(0), first=True, ubuf=ub)
            if main:
                do_R(0, None)
                for i, (t0, n) in enumerate(subs):
                    fl = ssm_Y(uT, ureg, t0, n, False, bf(i), hold_last=True)
                    if i + 1 < nsub:
                        ssm_B(uT, ureg, subs[i + 1][0], subs[i + 1][1], False, bf(i + 1))
                    fl()
                    if i + 1 < nsub:
                        do_R(i + 1, units)
            else:
                if nsub > 1:
                    ssm_B(uT, ureg, subs[1][0], subs[1][1], False, bf(1))
                do_R(0, None)
                for i, (t0, n) in enumerate(subs):
                    if i + 2 < nsub:
                        ssm_B(uT, ureg, subs[i + 2][0], subs[i + 2][1], False, bf(i + 2))
                    if i + 1 < nsub:
                        do_R(i + 1, units)
            bufi = (b0 + len(subs)) % 2
            if main:
                st_["TB"] = 16
                st_["BU"] = [BU[0], BU[0]]
                for r in range(4):
                    seq0 = 8 * blk + 2 * r
                    for c in range(2):
                        dma("sp", stg[:, c, :], sst_in[c][seq0 * 64:(seq0 + 2) * 64, :], writes=[rg.stg])
                    for c in range(2):
                        op("pe", lambda: nc.tensor.transpose(PS[:, 7, c * 128:(c + 1) * 128], stg[:, c, :], ident[:]),
                           reads=[rg.stg, rg.ident], writes=[rg.pt])
                    op("act", lambda: nc.scalar.activation(
                        out=Xinit[:, :, 0, :], in_=PS[:, 7, 0:128].rearrange("p (b e) -> p b e", b=2), func=AF.Copy),
                       reads=[rg.pt], writes=[rg.Xinit])
                    op("act", lambda: nc.scalar.activation(
                        out=Xinit[:, :, 1, :], in_=PS[:, 7, 128:256].rearrange("p (b e) -> p b e", b=2), func=AF.Copy,
                        scale=-1.0), reads=[rg.pt], writes=[rg.Xinit])
                    ssm_B(uT, ureg, NP + 16 * r, 16, True, bufi)
                    ssm_R(16, True, bufi, None, inter=units, inter_at=(0, 2, 4, 6))
                    ssm_Y(uT, ureg, NP + 16 * r, 16, True, bufi)
                    op("dve", lambda: nc.vector.tensor_copy(
                        out=t2r[:].rearrange("p (c b e) -> p c b e", c=2, b=2),
                        in_=st_["BU"][bufi][:, 14:16, :, :].rearrange("p b c e -> p c b e")), reads=[st_["BUreg"][bufi]], writes=[rg.t2r])
                    fin = [t2r[:, c * 128:(c + 1) * 128] for c in range(2)]
                    store_state_T(fin, sst_out, seq0 * 64, 128)
                    bufi = 1 - bufi
                    if units:
                        units.pop(0)()
            st_["bufi"] = bufi
            while units:
                units.pop(0)()

        def conv_unit(ct, blk):
            hn = lambda kt: hnT[:, kt, :]

            def m1():
                ps = mm_group(win[16 + ct], KT, hn, [rg.hnT], MH)
                op("act", lambda: nc.scalar.activation(out=V2(tA[:], 1), in_=pm(ps, H), func=AF.Copy),
                   reads=[rg.pm[ps]], writes=[rg.tA])

            def m2():
                ps = mm_group(win[48 + ct], KT, hn, [rg.hnT], MH)
                op("dve", lambda: nc.vector.tensor_copy(out=Zp[:, 0:2], in_=Zhist[:, ct, :]), reads=[rg.Zhist], writes=[rg.Zp])
                op("dve", lambda: nc.vector.tensor_copy(out=Zsp[:, :, 0:2], in_=ZsInit[:, ct, 8 * blk:8 * blk + 8, :]),
                   reads=[rg.ZsInit], writes=[rg.Zsp])
                op("dve", lambda: nc.vector.tensor_tensor(out=Zp[:, 2:2 + H], in0=PS[:, 2 * ps, 0:H], in1=tA[:, 0:H],
                                                          op=ALU.mult), reads=[rg.pm[ps], rg.tA], writes=[rg.Zp])
                op("dve", lambda: nc.vector.tensor_tensor(out=Zp[:, 2 + H:2 + NP], in0=PS[:, 2 * ps + 1, 0:NP - H],
                                                          in1=tA[:, H:NP], op=ALU.mult),
                   reads=[rg.pm[ps], rg.tA], writes=[rg.Zp])
                op("dve", lambda: nc.vector.tensor_tensor(
                    out=Zsp[:, :, 2:10], in0=PS[:, 2 * ps + 1, NP - H:H].rearrange("p (b t) -> p b t", b=8),
                    in1=tA[:, NP:NT].rearrange("p (b t) -> p b t", b=8), op=ALU.mult),
                   reads=[rg.pm[ps], rg.tA], writes=[rg.Zsp])
                w = lambda k: convw[:, k, ct:ct + 1]
                accp = tB[:, 0:NP]
                accs = tB[:, NP:NT].rearrange("p (b t) -> p b t", b=8)
                RZ = [rg.Zp, rg.Zsp, rg.convw, rg.tB]
                op("dve", lambda: nc.vector.tensor_scalar(out=accp, in0=Zp[:, 2:2 + NP], scalar1=w(2), scalar2=None,
                                                          op0=ALU.mult), reads=RZ, writes=[rg.tB])
                op("dve", lambda: nc.vector.tensor_scalar(out=accs, in0=Zsp[:, :, 2:10], scalar1=w(2), scalar2=None,
                                                          op0=ALU.mult), reads=RZ, writes=[rg.tB])
                for k in (1, 0):
                    sh = 2 - k
                    op("dve", lambda: nc.vector.scalar_tensor_tensor(out=accp, in0=Zp[:, 2 - sh:2 - sh + NP], scalar=w(k),
                                                                     in1=accp, op0=ALU.mult, op1=ALU.add),
                       reads=RZ, writes=[rg.tB])
                    op("dve", lambda: nc.vector.scalar_tensor_tensor(out=accs, in0=Zsp[:, :, 2 - sh:10 - sh], scalar=w(k),
                                                                     in1=accs, op0=ALU.mult, op1=ALU.add),
                       reads=RZ, writes=[rg.tB])
                op("dve", lambda: nc.vector.tensor_copy(out=Zhist[:, ct, :], in_=Zp[:, NP:NP + 2]), reads=[rg.Zp],
                   writes=[rg.Zhist])
                op("dve", lambda: nc.vector.tensor_copy(out=ZsOut[:, ct, 8 * blk:8 * blk + 8, :], in_=Zsp[:, :, 8:10]),
                   reads=[rg.Zsp], writes=[rg.ZsOut])

            def m3():
                ps = mm_group(win[32 + ct], KT, hn, [rg.hnT], MH)
                op("dve", lambda: nc.vector.tensor_tensor(out=V2(ycT[:, ct, :], 1), in0=pm(ps, H), in1=V2(tB[:], 1),
                                                          op=ALU.mult), reads=[rg.pm[ps], rg.tB], writes=[rg.yc])
            return [m1, m2, m3]

        def sumsq_acc(ps, mt, nmt):
            op("act", lambda: nc.scalar.activation(out=V2(tC[:], 1), in_=pm(ps, H), func=AF.Square),
               reads=[rg.pm[ps]], writes=[rg.tC])
            for h in range(2):
                op("pe", lambda: nc.tensor.matmul(PS[:, 4 + h, 0:H], lhsT=ones[:], rhs=tC[:, h * H:(h + 1) * H],
                                                  start=(mt == 0), stop=(mt == nmt - 1)),
                   reads=[rg.ones, rg.tC], writes=[rg.pb[h]], signal=True)

        def rstdb_from_ps():
            op("dve", lambda: nc.vector.tensor_scalar(out=V2(rstdb[:], 1), in0=PS[:, 4:6, 0:H], scalar1=1.0 / D, scalar2=EPS,
                                                      op0=ALU.mult, op1=ALU.add), reads=[rg.pb[0], rg.pb[1]], writes=[rg.rstdb])
            op("act", lambda: nc.scalar.activation(out=rstdb[:], in_=rstdb[:], func=AF.Sqrt),
               reads=[rg.rstdb], writes=[rg.rstdb])
            op("dve", lambda: nc.vector.reciprocal(out=rstdb[:], in_=rstdb[:]), reads=[rg.rstdb], writes=[rg.rstdb])

        def main_block(blk):
            xsrc = xm[blk]
            S.barrier()
            prep(xsrc, NT)
            hn = lambda kt: hnT[:, kt, :]
            for mt in range(16):
                ps = mm_group(win[mt], KT, hn, [rg.hnT], MH)
                op("act", lambda: nc.scalar.activation(out=V2(ygT[:, mt, :], 1), in_=pm(ps, H), func=AF.Copy),
                   reads=[rg.pm[ps]], writes=[rg.yg])
            S.barrier()
            dma("sp", Df[:, 4096:8192], cm_d, writes=[rg.Cm])
            hn = lambda kt: hnT[:, kt, :]
            yg = lambda kt: ygT[:, kt, :]
            yc = lambda kt: ycT[:, kt, :]

            def gbyb_unit(mt):
                def g1():
                    ps = mm_group(win[96 + mt], KT, hn, [rg.hnT], MH)
                    op("act", lambda: nc.scalar.activation(out=V2(tC[:], 1), in_=pm(ps, H), func=AF.Sigmoid),
                       reads=[rg.pm[ps]], writes=[rg.tC])

                def g2():
                    ps = mm_group(wco[mt], 16, yc, [rg.yc], MH)
                    op("dve", lambda: nc.vector.tensor_tensor(out=V2(mgT[:, mt, :], 1), in0=pm(ps, H), in1=V2(tC[:], 1),
                                                              op=ALU.mult), reads=[rg.pm[ps], rg.tC], writes=[rg.mg])
                return [g1, g2]
            units = [m for ct in range(16) for m in conv_unit(ct, blk)] + [m for mt in range(32) for m in gbyb_unit(mt)]
            ssm_block(ygT, rg.yg, NP, blk, True, units)
            for mt in range(32):
                ps = mm_group(win[64 + mt], KT, hn, [rg.hnT], MH)
                op("act", lambda: nc.scalar.activation(out=V2(tA[:], 1), in_=pm(ps, H), func=AF.Sigmoid),
                   reads=[rg.pm[ps]], writes=[rg.tA])
                ps = mm_group(wgv[mt], 16, yg, [rg.yg], MH)
                op("dve", lambda: nc.vector.tensor_tensor(out=V2(tA[:], 1), in0=pm(ps, H), in1=V2(tA[:], 1), op=ALU.mult),
                   reads=[rg.pm[ps], rg.tA], writes=[rg.tA])
                ps = mm_group(wgg[mt], 16, yg, [rg.yg], MH)
                op("act", lambda: nc.scalar.activation(out=V2(tC[:], 1), in_=pm(ps, H), func=AF.Sigmoid),
                   reads=[rg.pm[ps]], writes=[rg.tC])
                op("dve", lambda: nc.vector.tensor_tensor(out=tA[:], in0=tA[:], in1=tC[:], op=ALU.mult),
                   reads=[rg.tA, rg.tC], writes=[rg.tA])
                op("dve", lambda: nc.vector.tensor_tensor(out=mgT[:, mt, :], in0=tA[:], in1=mgT[:, mt, :], op=ALU.add),
                   reads=[rg.tA, rg.mg], writes=[rg.mg])
            S.barrier()
            mg = lambda kt: mgT[:, kt, :]
            for mt in range(32):
                ps = mm_group(wo[mt], KT, mg, [rg.mg], MH)
                op("act", lambda: nc.scalar.activation(out=V2(OG[:, mt, :], 1), in_=pm(ps, H), func=AF.Copy,
                                                       scale=gcols[:, 1, mt:mt + 1]),
                   reads=[rg.pm[ps], rg.gcols], writes=[rg.OG])
                sumsq_acc(ps, mt, 32)
            rstdb_from_ps()
            for mt in range(32):
                op("dve", lambda: nc.vector.tensor_tensor(out=OG[:, mt, :], in0=OG[:, mt, :], in1=rstdb[:], op=ALU.mult),
                   reads=[rg.OG, rg.rstdb], writes=[rg.OG])
            for ti, (r0, sz) in enumerate(tiles_of(NT)):
                buf, breg = XT[ti % 2], rg.XT[ti % 2]
                dma("sp", buf[0:sz, :], xsrc[r0:r0 + sz, :], writes=[breg])
                for g4 in range(8):
                    for j in range(4):
                        kt = g4 * 4 + j
                        op("pe", lambda: nc.tensor.transpose(PS[0:sz, 7, j * 128:(j + 1) * 128], OG[:, kt, r0:r0 + sz], ident[:]),
                           reads=[rg.OG, rg.ident], writes=[rg.pt])
                    op("dve", lambda: nc.vector.tensor_tensor(out=buf[0:sz, g4 * 512:(g4 + 1) * 512],
                                                              in0=buf[0:sz, g4 * 512:(g4 + 1) * 512], in1=PS[0:sz, 7, :],
                                                              op=ALU.add), reads=[breg, rg.pt], writes=[breg])
                dma("sp", x1s[r0:r0 + sz, :], buf[0:sz, :], reads=[breg], writes=[rg.x1s])
                sumsq_rstd(buf, breg, sz, 2 + ti % 2)
                op("act", lambda: nc.scalar.activation(out=buf[0:sz, :], in_=buf[0:sz, :], func=AF.Copy,
                                                       scale=ss[0:sz, 2 + ti % 2:3 + ti % 2]),
                   reads=[breg, rg.ss], writes=[breg])
                transposes_to_T(buf, sz, 2, mgT, rg.mg, r0, breg)
            S.barrier()
            hf = lambda kt: mgT[:, kt, :]
            chunks = [(f0, min(FCH, FT - f0)) for f0 in range(0, FT, FCH)]
            for ci, (f0, nf) in enumerate(chunks):
                for j in range(nf):
                    ps = mm_group(wfg[f0 + j], KT, hf, [rg.mg], MH)
                    op("act", lambda: nc.scalar.activation(out=V2(tA[:], 1), in_=pm(ps, H), func=AF.Silu),
                       reads=[rg.pm[ps]], writes=[rg.tA])
                    ps = mm_group(wfu[f0 + j], KT, hf, [rg.mg], MH)
                    op("dve", lambda: nc.vector.tensor_tensor(out=V2(hbuf[:, j, :], 1), in0=pm(ps, H), in1=V2(tA[:], 1),
                                                              op=ALU.mult), reads=[rg.pm[ps], rg.tA], writes=[rg.hbuf])
                hb = lambda kt: hbuf[:, kt, :]
                lastc = ci == len(chunks) - 1
                for nt_ in range(32):
                    ps = mm_group(wfd[nt_][:, f0 * 128:(f0 + nf) * 128], nf, hb, [rg.hbuf], MH)
                    if ci == 0:
                        op("act", lambda: nc.scalar.activation(out=V2(OG[:, nt_, :], 1), in_=pm(ps, H), func=AF.Copy),
                           reads=[rg.pm[ps]], writes=[rg.OG])
                    else:
                        op("dve", lambda: nc.vector.tensor_tensor(out=V2(OG[:, nt_, :], 1), in0=pm(ps, H),
                                                                  in1=V2(OG[:, nt_, :], 1), op=ALU.add),
                           reads=[rg.pm[ps], rg.OG], writes=[rg.OG])
                    if lastc:
                        op("act", lambda: nc.scalar.activation(out=tC[:], in_=OG[:, nt_, :], func=AF.Square),
                           reads=[rg.OG], writes=[rg.tC])
                        for h in range(2):
                            op("pe", lambda: nc.tensor.matmul(PS[:, 4 + h, 0:H], lhsT=ones[:], rhs=tC[:, h * H:(h + 1) * H],
                                                              start=(nt_ == 0), stop=(nt_ == 31)),
                               reads=[rg.ones, rg.tC], writes=[rg.pb[h]], signal=True)
                        op("act", lambda: nc.scalar.activation(out=OG[:, nt_, :], in_=OG[:, nt_, :], func=AF.Copy,
                                                               scale=gcols[:, 3, nt_:nt_ + 1]),
                           reads=[rg.OG, rg.gcols], writes=[rg.OG])
            rstdb_from_ps()
            for mt in range(32):
                op("dve", lambda: nc.vector.tensor_tensor(out=OG[:, mt, :], in0=OG[:, mt, :], in1=rstdb[:], op=ALU.mult),
                   reads=[rg.OG, rg.rstdb], writes=[rg.OG])
            S.barrier()
            for ti, (r0, sz) in enumerate(tiles_of(NT)):
                buf, breg = XT[ti % 2], rg.XT[ti % 2]
                dma("sp", buf[0:sz, :], x1s[r0:r0 + sz, :], reads=[rg.x1s], writes=[breg])
                for g4 in range(8):
                    for j in range(4):
                        kt = g4 * 4 + j
                        op("pe", lambda: nc.tensor.transpose(PS[0:sz, 7, j * 128:(j + 1) * 128], OG[:, kt, r0:r0 + sz], ident[:]),
                           reads=[rg.OG, rg.ident], writes=[rg.pt])
                    op("dve", lambda: nc.vector.tensor_tensor(out=buf[0:sz, g4 * 512:(g4 + 1) * 512],
                                                              in0=buf[0:sz, g4 * 512:(g4 + 1) * 512], in1=PS[0:sz, 7, :],
                                                              op=ALU.add), reads=[breg, rg.pt], writes=[breg])
                dma("sp", y_out[blk][r0:r0 + sz, :], buf[0:sz, :], reads=[breg])

        def prefix_block(blk):
            S.barrier()
            prep(xp[blk], NP)
            if _DBG_SUB == 1:
                return
            hn = lambda kt: hnT[:, kt, :]
            for mt in range(16 if _DBG_SUB != 2 else 1):
                ps = mm_group(win[mt], KT, hn, [rg.hnT], PH)
                op("act", lambda: nc.scalar.activation(out=ygT[:, mt, 0:NP].rearrange("p (h n) -> p h n", h=2),
                                                       in_=pm(ps, HP), func=AF.Copy), reads=[rg.pm[ps]], writes=[rg.yg])
            if blk == 1:
                for ct in range(16):
                    ps = mm_group(win[16 + ct], KT, hn, [rg.hnT], [(NP - 2, NP)])
                    op("act", lambda: nc.scalar.activation(out=tA[:, 0:2], in_=PS[:, 2 * ps, 0:2], func=AF.Copy),
                       reads=[rg.pm[ps]], writes=[rg.tA])
                    ps = mm_group(win[48 + ct], KT, hn, [rg.hnT], [(NP - 2, NP)])
                    op("dve", lambda: nc.vector.tensor_tensor(out=Zhist[:, ct, :], in0=PS[:, 2 * ps, 0:2], in1=tA[:, 0:2],
                                                              op=ALU.mult), reads=[rg.pm[ps], rg.tA], writes=[rg.Zhist])
            if _DBG_SUB in (2, 3):
                return
            S.barrier()
            ssm_block(ygT, rg.yg, NP, blk, False, [])

        setup()
        if _DBG_STAGE >= 1:
            prefix_block(0)
        if _DBG_STAGE >= 2:
            prefix_block(1)
        if _DBG_STAGE >= 3:
            main_block(0)
        if _DBG_STAGE >= 4:
            main_block(1)
        S.barrier()
        store_state_T([Xp[:, 1, c, :] for c in range(2)], pst_out, 0, 64)
        for k in range(2):
            dma("sp", pcv_out[k].rearrange("(c p) -> p c", p=128), Zhist[:, :, k], reads=[rg.Zhist],
                allow_slow_non_contiguous=True)
        for ct in range(16):
            op("pe", lambda: nc.tensor.transpose(PS[0:32, 7, ct * 128 % 512:ct * 128 % 512 + 128],
                                                 ZsOut[:, ct].rearrange("p b k -> p (b k)"), ident[:]),
               reads=[rg.ZsOut, rg.ident], writes=[rg.pt])
            if ct % 4 == 3:
                g = ct // 4
                op("act", lambda: nc.scalar.activation(out=xtok[0:32, g * 512:(g + 1) * 512], in_=PS[0:32, 7, :], func=AF.Copy),
                   reads=[rg.pt], writes=[rg.xtok])
        dma("sp", scv_out, xtok[0:32, 0:2048], reads=[rg.xtok])
        sp = S.E["sp"]
        for s in S.sp_slots:
            if s.val:
                S._wait(sp, (s, s.val))
        for n in ("pe", "act", "dve"):
            e = S.E[n]
            if e.sem.val:
                S._wait(sp, (e.sem, e.sem.val))
    return nc


def _tile_w(w, nkt, nmt):
    return np.ascontiguousarray(w.reshape(nkt, 128, nmt, 128).transpose(2, 1, 0, 3)).reshape(nmt, 128, nkt * 128)


_NC_CACHE = {}


def kernel(x_prompt, x_sample, state_ssm_re, state_ssm_im, state_conv, meta_tokens, g_pre_mix, w_in,
           ssm_lambda_re, ssm_lambda_im, ssm_log_dt, ssm_b_re, ssm_b_im, ssm_c_re, ssm_c_im, ssm_d,
           w_glu_v, w_glu_g, conv_w, w_conv_out, w_o, g_post_mix, g_pre_ffn, w_ffn_gate, w_ffn_up,
           w_ffn_down, g_post_ffn):
    f32 = np.float32
    A = lambda a: np.asarray(a, dtype=f32)
    x_prompt, x_sample = A(x_prompt), A(x_sample)
    meta = A(meta_tokens)
    shared = {}
    shared["ident"] = np.eye(128, dtype=f32)
    gc = np.stack([A(g)[0].reshape(32, 128).T for g in (g_pre_mix, g_post_mix, g_pre_ffn, g_post_ffn)], axis=1)
    shared["gcols"] = np.ascontiguousarray(gc).reshape(128, 128)
    shared["convw"] = np.ascontiguousarray(A(conv_w)[0].reshape(3, 16, 128).transpose(2, 0, 1)).reshape(128, 48)
    shared["dcol"] = np.ascontiguousarray(A(ssm_d)[0].reshape(16, 128).T)
    lre, lim, ldt = A(ssm_lambda_re)[0], A(ssm_lambda_im)[0], A(ssm_log_dt)[0]
    l1f = lambda a: np.ascontiguousarray(a.reshape(64, 2, 64).transpose(1, 2, 0)).reshape(128, 64)
    ldt_gp = np.broadcast_to(ldt[:, None], (128, 64))
    shared["l1"] = np.stack([l1f(lre), l1f(lim), l1f(ldt_gp)])
    def l2_bcast(a):
        v = a.reshape(16, 4, 2, 64)
        o = np.broadcast_to(v[:, :, None, None, :, :], (16, 4, 2, 16, 2, 64))
        return np.ascontiguousarray(o.transpose(1, 2, 3, 0, 4, 5)).reshape(128, 2048)
    def l2_b(b):
        v = b.reshape(16, 4, 2, 64, 16)
        o = np.zeros((16, 4, 2, 16, 2, 64), f32)
        for g2 in range(2):
            o[:, :, g2, :, g2, :] = v[:, :, g2].transpose(0, 1, 3, 2)
        return np.ascontiguousarray(o.transpose(1, 2, 3, 0, 4, 5)).reshape(128, 2048)
    shared["l2"] = np.stack([l2_bcast(lre), l2_bcast(lim), l2_bcast(ldt_gp), l2_b(A(ssm_b_re)[0]), l2_b(A(ssm_b_im)[0])])
    cm = np.zeros((2, 64, 64, 2, 2, 16), f32)
    for c, carr in enumerate((A(ssm_c_re)[0], A(ssm_c_im)[0])):
        v = carr.reshape(64, 2, 16, 64)
        for g2 in range(2):
            cm[g2, :, :, c, g2, :] = v[:, g2].transpose(2, 0, 1)
    shared["cm"] = cm.reshape(128, 4096)
    shared["win"] = _tile_w(A(w_in)[0], 32, 128)
    shared["wgv"] = _tile_w(A(w_glu_v)[0], 16, 32)
    shared["wgg"] = _tile_w(A(w_glu_g)[0], 16, 32)
    shared["wco"] = _tile_w(A(w_conv_out)[0], 16, 32)
    shared["wo"] = _tile_w(A(w_o)[0], 32, 32)
    shared["wfg"] = _tile_w(A(w_ffn_gate)[0], 32, FT)
    shared["wfu"] = _tile_w(A(w_ffn_up)[0], 32, FT)
    shared["wfd"] = _tile_w(A(w_ffn_down)[0], FT, 32)

    sre, sim, scv = A(state_ssm_re)[0], A(state_ssm_im)[0], A(state_conv)[0]
    in_maps = []
    for c in range(8):
        bq, half = c // 2, c % 2
        full = np.concatenate([meta, x_prompt[bq]], axis=0)
        if half == 0:
            main_p = full[0:1032]
            pre = np.zeros((1032, D), f32)
        else:
            main_p = full[1032:2064]
            pre = full[0:1032]
        xs = x_sample[16 * c:16 * c + 16].reshape(2, 64, D)
        xm_ = np.concatenate([main_p.reshape(2, NP, D), xs], axis=1)
        m = dict(shared)
        m["xm"] = np.ascontiguousarray(xm_)
        m["xp"] = np.ascontiguousarray(pre.reshape(2, NP, D))
        m["sst_re_in"] = np.ascontiguousarray(sre[16 * c:16 * c + 16]).reshape(16 * 64, 128)
        m["sst_im_in"] = np.ascontiguousarray(sim[16 * c:16 * c + 16]).reshape(16 * 64, 128)
        m["scv_in"] = np.ascontiguousarray(scv[16 * c:16 * c + 16]).reshape(32, SW)
        in_maps.append(m)

    if "nc" not in _NC_CACHE:
        _NC_CACHE["nc"] = build_nc()
    nc = _NC_CACHE["nc"]
    res = run_bass_kernel_spmd(nc, in_maps, core_ids=list(range(8)))
    R = res.results

    y_prompt = np.zeros((4, 2048, D), f32)
    y_sample = np.zeros((128, 8, D), f32)
    p_re = np.zeros((1, 4, 128, 64), f32)
    p_im = np.zeros((1, 4, 128, 64), f32)
    p_cv = np.zeros((1, 4, 2, SW), f32)
    s_re = np.zeros((1, 128, 128, 64), f32)
    s_im = np.zeros((1, 128, 128, 64), f32)
    s_cv = np.zeros((1, 128, 2, SW), f32)
    for c in range(8):
        bq, half = c // 2, c % 2
        y = R[c]["y"]
        yp = y[:, 0:NP].reshape(1032, D)
        if half == 0:
            y_prompt[bq, 0:1016] = yp[16:]
        else:
            y_prompt[bq, 1016:2048] = yp
            p_re[0, bq] = R[c]["pst_re"].reshape(128, 64)
            p_im[0, bq] = R[c]["pst_im"].reshape(128, 64)
            p_cv[0, bq] = R[c]["pcv"]
        y_sample[16 * c:16 * c + 16] = y[:, NP:NT].reshape(16, 8, D)
        s_re[0, 16 * c:16 * c + 16] = R[c]["sst_re"].reshape(16, 128, 64)
        s_im[0, 16 * c:16 * c + 16] = R[c]["sst_im"].reshape(16, 128, 64)
        s_cv[0, 16 * c:16 * c + 16] = R[c]["scv"].reshape(16, 2, SW)
    return (y_prompt, y_sample, p_re, p_im, p_cv, s_re, s_im, s_cv)
```

```python
import math
from contextlib import ExitStack

import numpy as np
import concourse.bass as bass
import concourse.mybir as mybir
from concourse.bass_utils import run_bass_kernel_spmd

F32 = mybir.dt.float32
BF16 = mybir.dt.bfloat16
AF = mybir.ActivationFunctionType
ALU = mybir.AluOpType

D = 4096
KT = 32
SW = 2048
DFF = 11008
FT = 86
NT = 580
NP = 516
H = 290
HP = 258
EPS = 1e-6
TB = 16
FCH = 24
PI = math.pi
_DBG_STAGE = 4
_DBG_SUB = 0


class _Sem:
    def __init__(self, h):
        self.h = h
        self.val = 0


class _Eng:
    def __init__(self, name, h, sem):
        self.name = name
        self.h = h
        self.sem = sem
        self.waited = {}


class Region:
    __slots__ = ("w", "r")

    def __init__(self):
        self.w = None
        self.r = {}


class Sync:
    def __init__(self, nc, es):
        self.nc = nc
        mk = lambda n: _Sem(es.enter_context(nc.semaphore(n)))
        self.E = {
            "pe": _Eng("pe", nc.tensor, mk("s_pe")),
            "act": _Eng("act", nc.scalar, mk("s_act")),
            "dve": _Eng("dve", nc.vector, mk("s_dve")),
            "pool": _Eng("pool", nc.gpsimd, mk("s_pool")),
            "sp": _Eng("sp", nc.sync, mk("s_sp")),
        }
        self.sp_slots = [mk("s_d%d" % i) for i in range(12)]
        self.sp_i = 0

    def _wait(self, eng, st, self_sync=True):
        s, v = st
        if eng.waited.get(s, 0) >= v:
            return
        if s is eng.sem and (eng.name == "pe" or not self_sync or v > s.val):
            return
        eng.h.wait_ge(s.h, v)
        eng.waited[s] = v

    def _deps(self, eng, reads, writes, self_sync=True):
        for r in reads:
            if r.w:
                self._wait(eng, r.w, self_sync)
        for r in writes:
            if r.w:
                self._wait(eng, r.w, self_sync)
            for st in r.r.values():
                self._wait(eng, st, self_sync)

    def _mark(self, st, reads, writes):
        for r in writes:
            r.w = st
            r.r = {}
        for r in reads:
            r.r[st[0]] = st

    def op(self, en, fn, reads=(), writes=(), signal=True, self_sync=True):
        eng = self.E[en]
        self._deps(eng, reads, writes, self_sync)
        ins = fn()
        if signal:
            eng.sem.val += 1
            ins.then_inc(eng.sem.h, 1)
            st = (eng.sem, eng.sem.val)
        else:
            st = (eng.sem, eng.sem.val + 1)
        self._mark(st, reads, writes)
        return ins

    def dma(self, qn, out, in_, reads=(), writes=(), slot=None, **kw):
        q = self.E[qn]
        if slot is None:
            slot = self.sp_slots[self.sp_i]
            self.sp_i = (self.sp_i + 1) % len(self.sp_slots)
        if slot.val:
            self._wait(q, (slot, slot.val))
        self._deps(q, reads, writes)
        ins = q.h.dma_start(out=out, in_=in_, **kw)
        slot.val += 16
        ins.then_inc(slot.h, 16)
        st = (slot, slot.val)
        self._mark(st, reads, writes)
        return ins

    def barrier(self, extra_slots=()):
        names = ["pe", "act", "dve", "sp"]
        for a in names:
            ea = self.E[a]
            for b in names:
                if a == b:
                    continue
                eb = self.E[b]
                if eb.sem.val:
                    self._wait(ea, (eb.sem, eb.sem.val))
            for s in list(self.sp_slots) + list(extra_slots):
                if s.val:
                    self._wait(ea, (s, s.val))


def build_nc():
    nc = bass.Bass("TRN2", target_bir_lowering=False)
    dt_in = lambda n, s: nc.dram_tensor(n, s, F32, kind="ExternalInput").ap()
    dt_out = lambda n, s: nc.dram_tensor(n, s, F32, kind="ExternalOutput").ap()

    xm = dt_in("xm", [2, NT, D])
    xp = dt_in("xp", [2, NP, D])
    sst_in = [dt_in("sst_re_in", [16 * 64, 128]), dt_in("sst_im_in", [16 * 64, 128])]
    scv_in = dt_in("scv_in", [32, SW])
    ident_d = dt_in("ident", [128, 128])
    gcols_d = dt_in("gcols", [128, 4 * 32])
    convw_d = dt_in("convw", [128, 48])
    dcol_d = dt_in("dcol", [128, 16])
    l1_d = dt_in("l1", [3, 128, 64])
    l2_d = dt_in("l2", [5, 128, 2048])
    cm_d = dt_in("cm", [128, 4096])
    win = dt_in("win", [128, 128, 4096])
    wgv = dt_in("wgv", [32, 128, 2048])
    wgg = dt_in("wgg", [32, 128, 2048])
    wco = dt_in("wco", [32, 128, 2048])
    wo = dt_in("wo", [32, 128, 4096])
    wfg = dt_in("wfg", [FT, 128, 4096])
    wfu = dt_in("wfu", [FT, 128, 4096])
    wfd = dt_in("wfd", [32, 128, FT * 128])

    y_out = dt_out("y", [2, NT, D])
    pst_out = [dt_out("pst_re", [64, 128]), dt_out("pst_im", [64, 128])]
    pcv_out = dt_out("pcv", [2, SW])
    sst_out = [dt_out("sst_re", [16 * 64, 128]), dt_out("sst_im", [16 * 64, 128])]
    scv_out = dt_out("scv", [32, SW])
    x1s = nc.dram_tensor("x1s", [NT, D], F32).ap()

    with ExitStack() as es:
        sb = lambda n, s, d: es.enter_context(nc.sbuf_tensor("sb_" + n, s, d))
        S = Sync(nc, es)
        op, dma = S.op, S.dma
        w_slots = [_Sem(es.enter_context(nc.semaphore("s_w%d" % i))) for i in range(3)]

        ident = sb("ident", [128, 128], F32)
        ones = sb("ones", [128, 128], F32)
        gcols = sb("gcols", [128, 4, 32], F32)
        convw = sb("convw", [128, 3, 16], F32)
        dcol = sb("dcol", [128, 16], F32)
        CA = sb("CA", [128, 2, 64], F32)
        CB = sb("CB", [128, 2, 64], F32)
        Bt = sb("Bt", [128, 2, 16, 128], BF16)
        Xp = sb("Xp", [128, 2, 2, 64], F32)
        CA2 = sb("CA2", [128, 2, 64], F32)
        CB2 = sb("CB2", [128, 2, 64], F32)
        Bt2 = sb("Bt2", [128, 2, 16, 128], BF16)
        ulast = sb("ulast", [128, 2, 16], BF16)
        ypend = sb("ypend", [128, 16, 1], BF16)
        Xinit = sb("Xinit", [128, 2, 2, 64], F32)
        Zhist = sb("Zhist", [128, 16, 2], F32)
        ZsInit = sb("ZsInit", [128, 16, 16, 2], F32)
        ZsOut = sb("ZsOut", [128, 16, 16, 2], F32)
        rstdb = sb("rstdb", [128, NT], F32)
        tA = sb("tA", [128, NT], F32)
        tB = sb("tB", [128, NT], F32)
        tC = sb("tC", [128, NT], F32)
        Zp = sb("Zp", [128, NP + 2], F32)
        Zsp = sb("Zsp", [128, 8, 10], F32)
        ss = sb("ss", [128, 4], F32)
        junk = sb("junk", [128, 4], F32)
        stg = sb("stg", [128, 2, 128], F32)
        tY = sb("tY", [128, 16, 32], F32)
        t1r = sb("t1r", [128, 256], F32)
        t2r = sb("t2r", [128, 256], F32)
        R1f = sb("R1", [128, 18560], F32)
        R2b = sb("R2", [128, 18560], BF16)
        Wr = [sb("wr%d" % i, [128, 4096], BF16) for i in range(3)]
        Df = sb("Dd", [128, 8192], F32)
        PS = es.enter_context(nc.psum_tensor("PS", [128, 8, 512], F32))

        R1b = R1f[:].bitcast(BF16)
        hnT = R1b[:, 0:18560].rearrange("p (k n) -> p k n", k=32)
        ygT = R1b[:, 18560:27840].rearrange("p (k n) -> p k n", k=16)
        ycT = R1b[:, 27840:37120].rearrange("p (k n) -> p k n", k=16)
        OG = R1f[:].rearrange("p (k n) -> p k n", k=32)
        mgT = R2b[:].rearrange("p (k n) -> p k n", k=32)
        xtok = Df[:, 0:4096]
        xtok2 = Df[:, 4096:8192]
        BU = [Df[:, i * 2048:(i + 1) * 2048].rearrange("p (t c e) -> p t c e", t=TB, c=2) for i in range(2)]
        BUP = [Df[:, i * 4096:(i + 1) * 4096].rearrange("p (t c e) -> p t c e", t=32, c=2) for i in range(2)]
        Cm = Df[:, 4096:8192].rearrange("p (e c m) -> p e c m", e=64, c=2)
        hbuf = Df[:].bitcast(BF16)[:, 0:FCH * NT].rearrange("p (k n) -> p k n", k=FCH)

        class RG:
            pass
        rg = RG()
        for n in ("ident ones gcols convw dcol CA CB CA2 CB2 Bt2 ulast ypend Bt Xp Xinit Zhist ZsInit ZsOut rstdb tA tB tC Zp Zsp ss junk stg tY t1r t2r "
                  "hnT yg yc OG mg xtok xtok2 Cm hbuf x1s").split():
            setattr(rg, n, Region())
        rg.wr = [Region() for _ in range(3)]
        rg.BU = [Region() for _ in range(2)]
        rg.pm = [Region() for _ in range(2)]
        rg.pb = [Region() for _ in range(2)]
        rg.pc = Region()
        rg.pt = Region()

        pm = lambda s, w: PS[:, 2 * s:2 * s + 2, 0:w]
        st_ = {"w": 0, "p": 0, "pt": 0}

        def V2(ap, w):
            return ap.rearrange("p (h n) -> p h n", h=2) if w else ap

        def mm_group(wsrc, nkt, rhs_fn, rhs_regs, halves):
            slot = st_["w"]
            st_["w"] = (slot + 1) % 3
            ps = st_["p"]
            st_["p"] = (ps + 1) % 2
            nel = nkt * 128
            bsz = max(b_ for b_ in range(128, 2049, 128) if nel % b_ == 0)
            dma("pool", Wr[slot][:, 0:nel].rearrange("p (a b) -> p a b", b=bsz),
                wsrc.rearrange("p (a b) -> p a b", b=bsz),
                reads=(), writes=(rg.wr[slot],), slot=w_slots[slot])
            nh = len(halves)
            for kt in range(nkt):
                for h, (c0, c1) in enumerate(halves):
                    last = (kt == nkt - 1) and (h == nh - 1)
                    op("pe", lambda: nc.tensor.matmul(PS[:, 2 * ps + h, 0:c1 - c0],
                                                      lhsT=Wr[slot][:, kt * 128:(kt + 1) * 128],
                                                      rhs=rhs_fn(kt)[:, c0:c1],
                                                      start=(kt == 0), stop=(kt == nkt - 1)),
                       reads=[rg.wr[slot]] + list(rhs_regs), writes=[rg.pm[ps]], signal=last)
            return ps

        MH = [(0, H), (H, NT)]
        PH = [(0, HP), (HP, NP)]

        def setup():
            dma("sp", ident[:], ident_d, writes=[rg.ident])
            dma("sp", gcols[:], gcols_d.rearrange("p (a b) -> p a b", a=4), writes=[rg.gcols])
            dma("sp", convw[:], convw_d.rearrange("p (a b) -> p a b", a=3), writes=[rg.convw])
            dma("sp", dcol[:], dcol_d, writes=[rg.dcol])
            op("dve", lambda: nc.vector.memset(ones[:], 1.0), writes=[rg.ones])
            op("dve", lambda: nc.vector.memset(Xp[:], 0.0), writes=[rg.Xp])
            op("dve", lambda: nc.vector.memset(ulast[:], 0.0), writes=[rg.ulast])
            op("dve", lambda: nc.vector.memset(Zhist[:], 0.0), writes=[rg.Zhist])

            rtmp = Region()

            def lam_bar(lre, lim, ldt, tmp, n):
                dtv, ldr, ldi, er, a1, a2 = tmp[:6]
                R = [rtmp]
                op("act", lambda: nc.scalar.activation(out=dtv, in_=ldt, func=AF.Exp), reads=R, writes=R)
                op("dve", lambda: nc.vector.tensor_tensor(out=ldr, in0=lre, in1=dtv, op=ALU.mult), reads=R, writes=R)
                op("dve", lambda: nc.vector.tensor_tensor(out=ldi, in0=lim, in1=dtv, op=ALU.mult), reads=R, writes=R)
                op("act", lambda: nc.scalar.activation(out=er, in_=ldr, func=AF.Exp), reads=R, writes=R)
                TS = lambda o, i, s1, s2, o0, o1=None: op("dve", lambda: (nc.vector.tensor_scalar(out=o, in0=i, scalar1=s1, scalar2=s2, op0=o0, op1=o1)
                                                                         if o1 is not None else
                                                                         nc.vector.tensor_scalar(out=o, in0=i, scalar1=s1, scalar2=None, op0=o0)),
                                                         reads=R, writes=R)
                TT_ = lambda o, a, b, o_: op("dve", lambda: nc.vector.tensor_tensor(out=o, in0=a, in1=b, op=o_), reads=R, writes=R)
                TS(a1, ldi, 1.0 / (2 * PI), None, ALU.mult)
                op("dve", lambda: nc.vector.tensor_copy(out=a2.bitcast(mybir.dt.int32), in_=a1), reads=R, writes=R)
                op("dve", lambda: nc.vector.tensor_copy(out=a1, in_=a2.bitcast(mybir.dt.int32)), reads=R, writes=R)
                op("dve", lambda: nc.vector.scalar_tensor_tensor(out=a1, in0=a1, scalar=-2 * PI, in1=ldi, op0=ALU.mult,
                                                                 op1=ALU.add), reads=R, writes=R)
                TS(dtv, a1, PI, 2 * PI, ALU.is_gt, ALU.mult)
                TT_(a1, a1, dtv, ALU.subtract)
                TS(dtv, a1, -PI, 2 * PI, ALU.is_lt, ALU.mult)
                TT_(a1, a1, dtv, ALU.add)
                TS(a2, a1, PI / 2, None, ALU.add)
                TS(dtv, a2, PI, 2 * PI, ALU.is_gt, ALU.mult)
                TT_(a2, a2, dtv, ALU.subtract)
                op("act", lambda: nc.scalar.activation(out=a1, in_=a1, func=AF.Sin), reads=R, writes=R)
                op("act", lambda: nc.scalar.activation(out=a2, in_=a2, func=AF.Sin), reads=R, writes=R)
                op("dve", lambda: nc.vector.tensor_tensor(out=a2, in0=a2, in1=er, op=ALU.mult), reads=R, writes=R)
                op("dve", lambda: nc.vector.tensor_tensor(out=a1, in0=a1, in1=er, op=ALU.mult), reads=R, writes=R)
                return a2, a1

            t1 = [Df[:, i * 64:(i + 1) * 64] for i in range(12)]
            for i in range(3):
                dma("sp", t1[i], l1_d[i], writes=[rtmp])
            lbr, lbi = lam_bar(t1[0], t1[1], t1[2], t1[3:9], 64)
            R = [rtmp]
            for c in range(2):
                op("dve", lambda: nc.vector.tensor_copy(out=CA[:, c, :], in_=lbr), reads=R, writes=[rg.CA])
            op("dve", lambda: nc.vector.tensor_copy(out=CB[:, 0, :], in_=lbi), reads=R, writes=[rg.CB])
            op("dve", lambda: nc.vector.tensor_scalar(out=CB[:, 1, :], in0=lbi, scalar1=-1.0, scalar2=None,
                                                      op0=ALU.mult), reads=R, writes=[rg.CB])
            q1, q2 = t1[9], t1[10]
            TT1 = lambda o, a, b, o_, w=(rtmp,): op("dve", lambda: nc.vector.tensor_tensor(out=o, in0=a, in1=b, op=o_), reads=R, writes=list(w))
            TT1(q1, lbr, lbr, ALU.mult)
            TT1(q2, lbi, lbi, ALU.mult)
            TT1(q1, q1, q2, ALU.subtract)
            TT1(q2, lbr, lbi, ALU.mult)
            for c in range(2):
                op("dve", lambda: nc.vector.tensor_copy(out=CA2[:, c, :], in_=q1), reads=R, writes=[rg.CA2])
            op("dve", lambda: nc.vector.tensor_scalar(out=CB2[:, 0, :], in0=q2, scalar1=2.0, scalar2=None, op0=ALU.mult),
               reads=R, writes=[rg.CB2])
            op("dve", lambda: nc.vector.tensor_scalar(out=CB2[:, 1, :], in0=q2, scalar1=-2.0, scalar2=None, op0=ALU.mult),
               reads=R, writes=[rg.CB2])
            S.barrier()
            t2 = [R1f[:, i * 2048:(i + 1) * 2048] for i in range(9)] + [Df[:, i * 2048:(i + 1) * 2048] for i in range(4)]
            for i in range(5):
                dma("sp", t2[i], l2_d[i], writes=[rtmp])
            lre, lim, ldt, bre, bim = t2[0:5]
            lbr, lbi = lam_bar(lre, lim, ldt, t2[5:11], 2048)
            nr, m2 = t2[11], t2[12]
            f1, f2 = t2[5], t2[6]
            TT = lambda o, a, b, o_: op("dve", lambda: nc.vector.tensor_tensor(out=o, in0=a, in1=b, op=o_), reads=R, writes=R)
            op("dve", lambda: nc.vector.tensor_scalar(out=nr, in0=lbr, scalar1=-1.0, scalar2=None, op0=ALU.add),
               reads=R, writes=R)
            TT(m2, lre, lre, ALU.mult)
            TT(f1, lim, lim, ALU.mult)
            TT(m2, m2, f1, ALU.add)
            op("dve", lambda: nc.vector.reciprocal(out=m2, in_=m2), reads=R, writes=R)
            TT(f1, nr, lre, ALU.mult)
            TT(f2, lbi, lim, ALU.mult)
            TT(f1, f1, f2, ALU.add)
            TT(f1, f1, m2, ALU.mult)
            TT(f2, lbi, lre, ALU.mult)
            TT(nr, nr, lim, ALU.mult)
            TT(f2, f2, nr, ALU.subtract)
            TT(f2, f2, m2, ALU.mult)
            X1, X2, sc1 = t2[0], t2[1], t2[2]
            TT(nr, f1, bre, ALU.mult)
            TT(m2, f2, bim, ALU.mult)
            TT(X1, nr, m2, ALU.subtract)
            TT(nr, f1, bim, ALU.mult)
            TT(m2, f2, bre, ALU.mult)
            TT(nr, nr, m2, ALU.add)
            op("dve", lambda: nc.vector.tensor_scalar(out=X2, in0=nr, scalar1=-1.0, scalar2=None, op0=ALU.mult),
               reads=R, writes=R)
            flat = lambda t: t.rearrange("p a b -> p (a b)")
            op("dve", lambda: nc.vector.tensor_copy(out=flat(Bt[:, 0]), in_=X1), reads=R, writes=[rg.Bt])
            op("dve", lambda: nc.vector.tensor_copy(out=flat(Bt[:, 1]), in_=X2), reads=R, writes=[rg.Bt])
            TT(nr, lbr, X1, ALU.mult)
            TT(m2, lbi, X2, ALU.mult)
            op("dve", lambda: nc.vector.tensor_tensor(out=flat(Bt2[:, 0]), in0=nr, in1=m2, op=ALU.add), reads=R, writes=[rg.Bt2])
            TT(nr, lbr, X2, ALU.mult)
            TT(m2, lbi, X1, ALU.mult)
            op("dve", lambda: nc.vector.tensor_tensor(out=flat(Bt2[:, 1]), in0=nr, in1=m2, op=ALU.subtract), reads=R,
               writes=[rg.Bt2])
            S.barrier()
            dma("sp", Df[0:32, 0:2048], scv_in, writes=[rg.xtok])
            for ct in range(16):
                op("pe", lambda: nc.tensor.transpose(PS[:, 7, 0:32], Df[0:32, ct * 128:(ct + 1) * 128], ident[0:32, 0:32]),
                   reads=[rg.xtok, rg.ident], writes=[rg.pt])
                op("act", lambda: nc.scalar.activation(out=ZsInit[:, ct].rearrange("p b k -> p (b k)"), in_=PS[:, 7, 0:32],
                                                       func=AF.Copy), reads=[rg.pt], writes=[rg.ZsInit])
            S.barrier()

        def tiles_of(n):
            return [(i, min(128, n - i)) for i in range(0, n, 128)]

        def rstd_from_ss(col):
            op("dve", lambda: nc.vector.tensor_scalar(out=ss[:, col:col + 1], in0=ss[:, col:col + 1], scalar1=1.0 / D,
                                                      scalar2=EPS, op0=ALU.mult, op1=ALU.add), reads=[rg.ss], writes=[rg.ss])
            op("act", lambda: nc.scalar.activation(out=ss[:, col:col + 1], in_=ss[:, col:col + 1], func=AF.Sqrt),
               reads=[rg.ss], writes=[rg.ss])
            op("dve", lambda: nc.vector.reciprocal(out=ss[:, col:col + 1], in_=ss[:, col:col + 1]),
               reads=[rg.ss], writes=[rg.ss])

        def transposes_to_T(src, sz, gi, dstT, dreg, c0, sreg=None):
            for g4 in range(8):
                for j in range(4):
                    kt = g4 * 4 + j
                    op("pe", lambda: nc.tensor.transpose(PS[:, 7, j * 128:j * 128 + sz], src[0:sz, kt * 128:(kt + 1) * 128],
                                                         ident[0:sz, 0:sz]),
                       reads=([sreg] if sreg is not None else [rg.xtok, rg.xtok2]) + [rg.ident], writes=[rg.pt])
                op("dve", lambda: nc.vector.tensor_tensor(
                    out=dstT[:, g4 * 4:g4 * 4 + 4, c0:c0 + sz],
                    in0=PS[:, 7, :].rearrange("p (j n) -> p j n", j=4)[:, :, 0:sz],
                    in1=gcols[:, gi, g4 * 4:g4 * 4 + 4].unsqueeze(2).to_broadcast([128, 4, sz]), op=ALU.mult),
                   reads=[rg.pt, rg.gcols], writes=[dreg])

        XT = [xtok, xtok2]
        rg.XT = [rg.xtok, rg.xtok2]

        def sumsq_rstd(buf, breg, sz, col):
            op("dve", lambda: nc.vector.memset(ss[:, col:col + 1], 0.0), writes=[rg.ss])
            op("act", lambda: nc.scalar.activation(out=junk[0:sz, 0:1].to_broadcast([sz, D]), in_=buf[0:sz, :], func=AF.Square,
                                                   accum_out=ss[0:sz, col:col + 1]),
               reads=[breg], writes=[rg.junk, rg.ss])
            rstd_from_ss(col)

        R2f = R2b[:].bitcast(F32)
        XTP = [R2f[:, 0:4096], R2f[:, 4096:8192]]
        rg.XTP = [Region(), Region()]

        def prep_tiles(xsrc, ntok, bufs, bregs):
            def mk(ti, r0, sz):
                def run():
                    buf, breg = bufs[ti % 2], bregs[ti % 2]
                    dma("sp", buf[0:sz, :], xsrc[r0:r0 + sz, :], writes=[breg])
                    sumsq_rstd(buf, breg, sz, ti % 2)
                    op("act", lambda: nc.scalar.activation(out=buf[0:sz, :], in_=buf[0:sz, :], func=AF.Copy,
                                                           scale=ss[0:sz, ti % 2:ti % 2 + 1]),
                       reads=[breg, rg.ss], writes=[breg])
                    transposes_to_T(buf, sz, 0, hnT, rg.hnT, r0, breg)
                return run
            return [mk(ti, r0, sz) for ti, (r0, sz) in enumerate(tiles_of(ntok))]

        def prep(xsrc, ntok):
            for f in prep_tiles(xsrc, ntok, XT, rg.XT):
                f()

        def uproj_groups(dstT, dreg, halves, hw):
            hn = lambda kt: hnT[:, kt, :]

            def mk(mt):
                def run():
                    ps = mm_group(win[mt], KT, hn, [rg.hnT], halves)
                    op("act", lambda: nc.scalar.activation(
                        out=dstT[:, mt, 0:2 * hw].rearrange("p (h n) -> p h n", h=2), in_=pm(ps, hw), func=AF.Copy),
                       reads=[rg.pm[ps]], writes=[dreg])
                return run
            return [mk(mt) for mt in range(16)]

        def convhist_groups():
            hn = lambda kt: hnT[:, kt, :]
            out = []
            for ct in range(16):
                def a(ct=ct):
                    ps = mm_group(win[16 + ct], KT, hn, [rg.hnT], [(NP - 2, NP)])
                    op("act", lambda: nc.scalar.activation(out=tA[:, 0:2], in_=PS[:, 2 * ps, 0:2], func=AF.Copy),
                       reads=[rg.pm[ps]], writes=[rg.tA])

                def b(ct=ct):
                    ps = mm_group(win[48 + ct], KT, hn, [rg.hnT], [(NP - 2, NP)])
                    op("dve", lambda: nc.vector.tensor_tensor(out=Zhist[:, ct, :], in0=PS[:, 2 * ps, 0:2], in1=tA[:, 0:2],
                                                              op=ALU.mult), reads=[rg.pm[ps], rg.tA], writes=[rg.Zhist])
                out += [a, b]
            return out

        def ssm_B(uT, ureg, cols, n, sample, bufi, first=False, ubuf=0):
            BUv = st_["BU"][bufi]
            rB = st_["BUreg"][bufi]
            tb = st_["TB"]
            ncl = 128 // tb
            for pas in range(16 // ncl):
                for q in range(4):
                    par = q % 2
                    qh = q // 2
                    pbv = PS[:, 4 + par, :].rearrange("p (j c t) -> p j c t", j=2 * ncl, c=2)
                    for cl in range(ncl):
                        ct = ncl * pas + cl
                        j = 2 * cl + qh
                        for c in range(2):
                            rows = slice(32 * q, 32 * q + 32)
                            if not sample:
                                items = [(Bt, uT[rows, ct, cols:cols + n], pbv[:, j, c, 0:n])]
                                if first:
                                    items.append((Bt2, ulast[rows, ubuf, ct:ct + 1], pbv[:, j, c, 0:1]))
                                    items.append((Bt2, uT[rows, ct, cols:cols + n - 1], pbv[:, j, c, 1:n]))
                                else:
                                    items.append((Bt2, uT[rows, ct, cols - 1:cols + n - 1], pbv[:, j, c, 0:n]))
                            else:
                                items = [(Bt, uT[rows, ct, cols + 8 * b_:cols + 8 * b_ + 8],
                                          pbv[:, j, c, b_:16:2]) for b_ in range(2)]
                            for ii, (bm, rhs, o_) in enumerate(items):
                                st_flag = True if sample else (ii == 0)
                                sp_flag = True if sample else (ii == len(items) - 1)
                                op("pe", lambda: nc.tensor.matmul(o_, lhsT=bm[rows, c, ct, :], rhs=rhs,
                                                                  start=st_flag, stop=sp_flag, tile_position=(32 * q, 0)),
                                   reads=[rg.Bt, rg.Bt2, rg.ulast, ureg], writes=[rg.pb[par]],
                                   signal=(cl == ncl - 1 and c == 1 and ii == len(items) - 1))
                for par in range(2):
                    pbv = PS[:, 4 + par, :].rearrange("p (j c t) -> p j c t", j=2 * ncl, c=2)
                    e0 = 4 * ncl * pas + par
                    op("act", lambda: nc.scalar.activation(
                        out=BUv[:, 0:n, :, e0:e0 + 4 * ncl - 1:2].rearrange("p t c e -> p e c t"),
                        in_=pbv[:, :, :, 0:n], func=AF.Copy), reads=[rg.pb[par]], writes=[rB])

        def ssm_R(n, sample, bufi, prev_ap, inter=None, inter_at=()):
            BUv = st_["BU"][bufi]
            rB = st_["BUreg"][bufi]
            if not sample:
                assert n % 2 == 0
                nsteps = n // 2
                halves = [lambda a, h=h: a[:, h] for h in range(2)]
                get_prev = lambda j: Xp[:] if j == 0 else BUv[:, 2 * (j - 1):2 * j]
                get_cur = lambda j: BUv[:, 2 * j:2 * j + 2]
                ca = CA2[:].unsqueeze(1).to_broadcast([128, 2, 2, 64])
                cb = CB2[:].unsqueeze(1).to_broadcast([128, 2, 2, 64])
                t1 = t1r[:].rearrange("p (b c e) -> p b c e", b=2, c=2)
                t2 = t2r[:].rearrange("p (b c e) -> p b c e", b=2, c=2)
                sw = lambda a: a[:, ::-1, :]
            else:
                nsteps = 8
                halves = [lambda a, h=h: a[:, h] for h in range(2)]
                get_prev = lambda j: Xinit[:] if j == 0 else BUv[:, 2 * (j - 1):2 * j]
                get_cur = lambda j: BUv[:, 2 * j:2 * j + 2]
                ca = CA[:].unsqueeze(1).to_broadcast([128, 2, 2, 64])
                cb = CB[:].unsqueeze(1).to_broadcast([128, 2, 2, 64])
                t1 = t1r[:].rearrange("p (b c e) -> p b c e", b=2, c=2)
                t2 = t2r[:].rearrange("p (b c e) -> p b c e", b=2, c=2)
                sw = lambda a: a[:, ::-1, :]
            RR = [rB, rg.BU[0], rg.BU[1], rg.Xp, rg.Xinit, rg.CA, rg.CB, rg.CA2, rg.CB2]
            for j in range(nsteps):
                prev, cur = get_prev(j), get_cur(j)
                ss_ = (j == 0)
                for hf in halves:
                    op("dve", lambda: nc.vector.tensor_tensor(out=hf(t1), in0=hf(prev), in1=hf(ca), op=ALU.mult),
                       reads=RR, writes=[rg.t1r], self_sync=ss_, signal=False)
                for hf in halves:
                    op("dve", lambda: nc.vector.tensor_tensor(out=hf(t2), in0=sw(hf(prev)), in1=hf(cb), op=ALU.mult),
                       reads=RR, writes=[rg.t2r], self_sync=ss_, signal=False)
                for hf in halves:
                    op("dve", lambda: nc.vector.tensor_tensor(out=hf(t1), in0=hf(t1), in1=hf(t2), op=ALU.add),
                       reads=[rg.t1r, rg.t2r], writes=[rg.t1r], self_sync=ss_, signal=False)
                for hi, hf in enumerate(halves):
                    op("dve", lambda: nc.vector.tensor_tensor(out=hf(cur), in0=hf(cur), in1=hf(t1), op=ALU.add),
                       reads=[rg.t1r, rB], writes=[rB], self_sync=ss_, signal=(hi == 1))
                if inter and j in inter_at:
                    inter.pop(0)()

        def ssm_Y(uT, ureg, cols, n, sample, bufi, hold_last=False):
            BUv = st_["BU"][bufi]
            rB = st_["BUreg"][bufi]
            pcv = PS[:, 6, 0:16 * st_["TB"]].rearrange("p (k t) -> p k t", k=16)
            for ct in range(16):
                for q in range(4):
                    e = 4 * ct + q
                    for c in range(2):
                        op("pe", lambda: nc.tensor.matmul(pcv[32 * q:32 * q + 32, ct, 0:n], lhsT=Cm[:, e, c, :],
                                                          rhs=BUv[:, 0:n, c, e], start=(c == 0), stop=(c == 1),
                                                          tile_position=(0, 32 * q)),
                           reads=[rg.Cm, rB], writes=[rg.pc], signal=(ct == 15 and q == 3 and c == 1))
            if not sample:
                uv = uT[:, :, cols:cols + n]
                tyv = tY[:, :, 0:n]
                pv = pcv[:, :, 0:n]
                dbc = dcol[:].unsqueeze(2).to_broadcast([128, 16, n])
            else:
                uv = uT[:, :, cols:cols + 16].rearrange("p k (b t) -> p k t b", b=2)
                tyv = tY[:, :, 0:16].rearrange("p k (t b) -> p k t b", b=2)
                pv = pcv[:, :, 0:16].rearrange("p k (t b) -> p k t b", b=2)
                dbc = dcol[:].unsqueeze(2).unsqueeze(3).to_broadcast([128, 16, 8, 2])
            op("dve", lambda: nc.vector.tensor_tensor(out=tyv, in0=uv, in1=dbc, op=ALU.mult),
               reads=[ureg, rg.dcol], writes=[rg.tY])
            op("dve", lambda: nc.vector.tensor_tensor(out=tyv, in0=tyv, in1=pv, op=ALU.add),
               reads=[rg.tY, rg.pc], writes=[rg.tY])
            if sample or not hold_last:
                op("act", lambda: nc.scalar.activation(out=uv, in_=tyv, func=AF.Gelu_apprx_tanh),
                   reads=[rg.tY], writes=[ureg])
                return lambda: None
            op("act", lambda: nc.scalar.activation(out=uT[:, :, cols:cols + n - 1], in_=tY[:, :, 0:n - 1],
                                                   func=AF.Gelu_apprx_tanh), reads=[rg.tY], writes=[ureg])
            op("act", lambda: nc.scalar.activation(out=ypend[:], in_=tY[:, :, n - 1:n], func=AF.Gelu_apprx_tanh),
               reads=[rg.tY], writes=[rg.ypend])

            def flush():
                op("act", lambda: nc.scalar.activation(out=uT[:, :, cols + n - 1:cols + n], in_=ypend[:], func=AF.Copy),
                   reads=[rg.ypend], writes=[ureg])
            return flush

        def store_state_T(src_c_aps, dst_drams, row0, nrows):
            for c in range(2):
                op("pe", lambda: nc.tensor.transpose(PS[0:nrows, 7, 0:128], src_c_aps[c], ident[:]),
                   reads=[rg.BU[0], rg.BU[1], rg.Xp, rg.t2r, rg.ident], writes=[rg.pt])
                op("act", lambda: nc.scalar.activation(out=stg[0:nrows, c, :], in_=PS[0:nrows, 7, 0:128], func=AF.Copy,
                                                       scale=(1.0 if c == 0 else -1.0)),
                   reads=[rg.pt], writes=[rg.stg])
                dma("sp", dst_drams[c][row0:row0 + nrows, :], stg[0:nrows, c, :], reads=[rg.stg])

        def ssm_block(uT, ureg, ntok_prompt, blk, main, units):
            tb_ = 32
            st_["TB"] = tb_
            st_["BU"] = [BUP[0], BUP[0]] if main else BUP
            st_["BUreg"] = [rg.BU[0], rg.BU[0]] if main else rg.BU
            subs = [(t0, min(tb_, ntok_prompt - t0)) for t0 in range(0, ntok_prompt, tb_)]
            b0 = st_.get("bufi", 0)
            kb = st_.get("kblk", 0)
            st_["kblk"] = kb + 1
            ub = kb % 2
            op("dve", lambda: nc.vector.tensor_copy(out=ulast[:, 1 - ub, :], in_=uT[:, :, ntok_prompt - 1]),
               reads=[ureg], writes=[rg.ulast])
            nsub = len(subs)
            bf = lambda i: (b0 + i) % 2

            def do_R(i, inter):
                n_ = subs[i][1]
                ssm_R(n_, False, bf(i), None, inter=inter, inter_at=(0, 2, 4, 6, 8, 10))
                op("dve", lambda: nc.vector.tensor_copy(out=Xp[:], in_=st_["BU"][bf(i)][:, n_ - 2:n_]), reads=[st_["BUreg"][bf(i)]],
                   writes=[rg.Xp])

            ssm_B(uT, ureg, subs[0][0], subs[0][1], False, bf(0), first=True, ubuf=ub)
            if main:
                do_R(0, None)
                for i, (t0, n) in enumerate(subs):
                    fl = ssm_Y(uT, ureg, t0, n, False, bf(i), hold_last=True)
                    if i + 1 < nsub:
                        ssm_B(uT, ureg, subs[i + 1][0], subs[i + 1][1], False, bf(i + 1))
                    fl()
                    if i + 1 < nsub:
                        do_R(i + 1, units)
            else:
                if nsub > 1:
                    ssm_B(uT, ureg, subs[1][0], subs[1][1], False, bf(1))
                do_R(0, None)
                for i, (t0, n) in enumerate(subs):
                    if i + 2 < nsub:
                        ssm_B(uT, ureg, subs[i + 2][0], subs[i + 2][1], False, bf(i + 2))
                    if i + 1 < nsub:
                        do_R(i + 1, units)
            bufi = (b0 + len(subs)) % 2
            if main:
                st_["TB"] = 16
                st_["BU"] = [BU[0], BU[0]]
                for r in range(4):
                    seq0 = 8 * blk + 2 * r
                    for c in range(2):
                        dma("sp", stg[:, c, :], sst_in[c][seq0 * 64:(seq0 + 2) * 64, :], writes=[rg.stg])
                    for c in range(2):
                        op("pe", lambda: nc.tensor.transpose(PS[:, 7, c * 128:(c + 1) * 128], stg[:, c, :], ident[:]),
                           reads=[rg.stg, rg.ident], writes=[rg.pt])
                    op("act", lambda: nc.scalar.activation(
                        out=Xinit[:, :, 0, :], in_=PS[:, 7, 0:128].rearrange("p (b e) -> p b e", b=2), func=AF.Copy),
                       reads=[rg.pt], writes=[rg.Xinit])
                    op("act", lambda: nc.scalar.activation(
                        out=Xinit[:, :, 1, :], in_=PS[:, 7, 128:256].rearrange("p (b e) -> p b e", b=2), func=AF.Copy,
                        scale=-1.0), reads=[rg.pt], writes=[rg.Xinit])
                    ssm_B(uT, ureg, NP + 16 * r, 16, True, bufi)
                    ssm_R(16, True, bufi, None, inter=units, inter_at=(0, 2, 4, 6))
                    ssm_Y(uT, ureg, NP + 16 * r, 16, True, bufi)
                    op("dve", lambda: nc.vector.tensor_copy(
                        out=t2r[:].rearrange("p (c b e) -> p c b e", c=2, b=2),
                        in_=st_["BU"][bufi][:, 14:16, :, :].rearrange("p b c e -> p c b e")), reads=[st_["BUreg"][bufi]], writes=[rg.t2r])
                    fin = [t2r[:, c * 128:(c + 1) * 128] for c in range(2)]
                    store_state_T(fin, sst_out, seq0 * 64, 128)
                    bufi = 1 - bufi
                    if units:
                        units.pop(0)()
            st_["bufi"] = bufi
            while units:
                units.pop(0)()

        def conv_unit(ct, blk):
            hn = lambda kt: hnT[:, kt, :]

            def m1():
                ps = mm_group(win[16 + ct], KT, hn, [rg.hnT], MH)
                op("act", lambda: nc.scalar.activation(out=V2(tA[:], 1), in_=pm(ps, H), func=AF.Copy),
                   reads=[rg.pm[ps]], writes=[rg.tA])

            def m2():
                ps = mm_group(win[48 + ct], KT, hn, [rg.hnT], MH)
                op("dve", lambda: nc.vector.tensor_copy(out=Zp[:, 0:2], in_=Zhist[:, ct, :]), reads=[rg.Zhist], writes=[rg.Zp])
                op("dve", lambda: nc.vector.tensor_copy(out=Zsp[:, :, 0:2], in_=ZsInit[:, ct, 8 * blk:8 * blk + 8, :]),
                   reads=[rg.ZsInit], writes=[rg.Zsp])
                op("dve", lambda: nc.vector.tensor_tensor(out=Zp[:, 2:2 + H], in0=PS[:, 2 * ps, 0:H], in1=tA[:, 0:H],
                                                          op=ALU.mult), reads=[rg.pm[ps], rg.tA], writes=[rg.Zp])
                op("dve", lambda: nc.vector.tensor_tensor(out=Zp[:, 2 + H:2 + NP], in0=PS[:, 2 * ps + 1, 0:NP - H],
                                                          in1=tA[:, H:NP], op=ALU.mult),
                   reads=[rg.pm[ps], rg.tA], writes=[rg.Zp])
                op("dve", lambda: nc.vector.tensor_tensor(
                    out=Zsp[:, :, 2:10], in0=PS[:, 2 * ps + 1, NP - H:H].rearrange("p (b t) -> p b t", b=8),
                    in1=tA[:, NP:NT].rearrange("p (b t) -> p b t", b=8), op=ALU.mult),
                   reads=[rg.pm[ps], rg.tA], writes=[rg.Zsp])
                w = lambda k: convw[:, k, ct:ct + 1]
                accp = tB[:, 0:NP]
                accs = tB[:, NP:NT].rearrange("p (b t) -> p b t", b=8)
                RZ = [rg.Zp, rg.Zsp, rg.convw, rg.tB]
                op("dve", lambda: nc.vector.tensor_scalar(out=accp, in0=Zp[:, 2:2 + NP], scalar1=w(2), scalar2=None,
                                                          op0=ALU.mult), reads=RZ, writes=[rg.tB])
                op("dve", lambda: nc.vector.tensor_scalar(out=accs, in0=Zsp[:, :, 2:10], scalar1=w(2), scalar2=None,
                                                          op0=ALU.mult), reads=RZ, writes=[rg.tB])
                for k in (1, 0):
                    sh = 2 - k
                    op("dve", lambda: nc.vector.scalar_tensor_tensor(out=accp, in0=Zp[:, 2 - sh:2 - sh + NP], scalar=w(k),
                                                                     in1=accp, op0=ALU.mult, op1=ALU.add),
                       reads=RZ, writes=[rg.tB])
                    op("dve", lambda: nc.vector.scalar_tensor_tensor(out=accs, in0=Zsp[:, :, 2 - sh:10 - sh], scalar=w(k),
                                                                     in1=accs, op0=ALU.mult, op1=ALU.add),
                       reads=RZ, writes=[rg.tB])
                op("dve", lambda: nc.vector.tensor_copy(out=Zhist[:, ct, :], in_=Zp[:, NP:NP + 2]), reads=[rg.Zp],
                   writes=[rg.Zhist])
                op("dve", lambda: nc.vector.tensor_copy(out=ZsOut[:, ct, 8 * blk:8 * blk + 8, :], in_=Zsp[:, :, 8:10]),
                   reads=[rg.Zsp], writes=[rg.ZsOut])

            def m3():
                ps = mm_group(win[32 + ct], KT, hn, [rg.hnT], MH)
                op("dve", lambda: nc.vector.tensor_tensor(out=V2(ycT[:, ct, :], 1), in0=pm(ps, H), in1=V2(tB[:], 1),
                                                          op=ALU.mult), reads=[rg.pm[ps], rg.tB], writes=[rg.yc])
            return [m1, m2, m3]

        def sumsq_acc(ps, mt, nmt):
            op("act", lambda: nc.scalar.activation(out=V2(tC[:], 1), in_=pm(ps, H), func=AF.Square),
               reads=[rg.pm[ps]], writes=[rg.tC])
            for h in range(2):
                op("pe", lambda: nc.tensor.matmul(PS[:, 4 + h, 0:H], lhsT=ones[:], rhs=tC[:, h * H:(h + 1) * H],
                                                  start=(mt == 0), stop=(mt == nmt - 1)),
                   reads=[rg.ones, rg.tC], writes=[rg.pb[h]], signal=True)

        def rstdb_from_ps():
            op("dve", lambda: nc.vector.tensor_scalar(out=V2(rstdb[:], 1), in0=PS[:, 4:6, 0:H], scalar1=1.0 / D, scalar2=EPS,
                                                      op0=ALU.mult, op1=ALU.add), reads=[rg.pb[0], rg.pb[1]], writes=[rg.rstdb])
            op("act", lambda: nc.scalar.activation(out=rstdb[:], in_=rstdb[:], func=AF.Sqrt),
               reads=[rg.rstdb], writes=[rg.rstdb])
            op("dve", lambda: nc.vector.reciprocal(out=rstdb[:], in_=rstdb[:]), reads=[rg.rstdb], writes=[rg.rstdb])

        def main_block(blk, skip_front=False):
            xsrc = xm[blk]
            S.barrier()
            hn = lambda kt: hnT[:, kt, :]
            if not skip_front:
                prep(xsrc, NT)
                for f in uproj_groups(ygT, rg.yg, MH, H):
                    f()
                S.barrier()
            dma("sp", Df[:, 4096:8192], cm_d, writes=[rg.Cm])
            hn = lambda kt: hnT[:, kt, :]
            yg = lambda kt: ygT[:, kt, :]
            yc = lambda kt: ycT[:, kt, :]

            def gbyb_unit(mt):
                def g1():
                    ps = mm_group(win[96 + mt], KT, hn, [rg.hnT], MH)
                    op("act", lambda: nc.scalar.activation(out=V2(tC[:], 1), in_=pm(ps, H), func=AF.Sigmoid),
                       reads=[rg.pm[ps]], writes=[rg.tC])

                def g2():
                    ps = mm_group(wco[mt], 16, yc, [rg.yc], MH)
                    op("dve", lambda: nc.vector.tensor_tensor(out=V2(mgT[:, mt, :], 1), in0=pm(ps, H), in1=V2(tC[:], 1),
                                                              op=ALU.mult), reads=[rg.pm[ps], rg.tC], writes=[rg.mg])
                return [g1, g2]
            units = [m for ct in range(16) for m in conv_unit(ct, blk)] + [m for mt in range(32) for m in gbyb_unit(mt)]
            ssm_block(ygT, rg.yg, NP, blk, True, units)
            for mt in range(32):
                ps = mm_group(win[64 + mt], KT, hn, [rg.hnT], MH)
                op("act", lambda: nc.scalar.activation(out=V2(tA[:], 1), in_=pm(ps, H), func=AF.Sigmoid),
                   reads=[rg.pm[ps]], writes=[rg.tA])
                ps = mm_group(wgv[mt], 16, yg, [rg.yg], MH)
                op("dve", lambda: nc.vector.tensor_tensor(out=V2(tA[:], 1), in0=pm(ps, H), in1=V2(tA[:], 1), op=ALU.mult),
                   reads=[rg.pm[ps], rg.tA], writes=[rg.tA])
                ps = mm_group(wgg[mt], 16, yg, [rg.yg], MH)
                op("act", lambda: nc.scalar.activation(out=V2(tC[:], 1), in_=pm(ps, H), func=AF.Sigmoid),
                   reads=[rg.pm[ps]], writes=[rg.tC])
                op("dve", lambda: nc.vector.tensor_tensor(out=tA[:], in0=tA[:], in1=tC[:], op=ALU.mult),
                   reads=[rg.tA, rg.tC], writes=[rg.tA])
                op("dve", lambda: nc.vector.tensor_tensor(out=mgT[:, mt, :], in0=tA[:], in1=mgT[:, mt, :], op=ALU.add),
                   reads=[rg.tA, rg.mg], writes=[rg.mg])
            S.barrier()
            mg = lambda kt: mgT[:, kt, :]
            for mt in range(32):
                ps = mm_group(wo[mt], KT, mg, [rg.mg], MH)
                op("act", lambda: nc.scalar.activation(out=V2(OG[:, mt, :], 1), in_=pm(ps, H), func=AF.Copy,
                                                       scale=gcols[:, 1, mt:mt + 1]),
                   reads=[rg.pm[ps], rg.gcols], writes=[rg.OG])
                sumsq_acc(ps, mt, 32)
            rstdb_from_ps()
            for mt in range(32):
                op("dve", lambda: nc.vector.tensor_tensor(out=OG[:, mt, :], in0=OG[:, mt, :], in1=rstdb[:], op=ALU.mult),
                   reads=[rg.OG, rg.rstdb], writes=[rg.OG])
            for ti, (r0, sz) in enumerate(tiles_of(NT)):
                buf, breg = XT[ti % 2], rg.XT[ti % 2]
                dma("sp", buf[0:sz, :], xsrc[r0:r0 + sz, :], writes=[breg])
                for g4 in range(8):
                    for j in range(4):
                        kt = g4 * 4 + j
                        op("pe", lambda: nc.tensor.transpose(PS[0:sz, 7, j * 128:(j + 1) * 128], OG[:, kt, r0:r0 + sz], ident[:]),
                           reads=[rg.OG, rg.ident], writes=[rg.pt])
                    op("dve", lambda: nc.vector.tensor_tensor(out=buf[0:sz, g4 * 512:(g4 + 1) * 512],
                                                              in0=buf[0:sz, g4 * 512:(g4 + 1) * 512], in1=PS[0:sz, 7, :],
                                                              op=ALU.add), reads=[breg, rg.pt], writes=[breg])
                dma("sp", x1s[r0:r0 + sz, :], buf[0:sz, :], reads=[breg], writes=[rg.x1s])
                sumsq_rstd(buf, breg, sz, 2 + ti % 2)
                op("act", lambda: nc.scalar.activation(out=buf[0:sz, :], in_=buf[0:sz, :], func=AF.Copy,
                                                       scale=ss[0:sz, 2 + ti % 2:3 + ti % 2]),
                   reads=[breg, rg.ss], writes=[breg])
                transposes_to_T(buf, sz, 2, mgT, rg.mg, r0, breg)
            S.barrier()
            hf = lambda kt: mgT[:, kt, :]
            chunks = [(f0, min(FCH, FT - f0)) for f0 in range(0, FT, FCH)]
            for ci, (f0, nf) in enumerate(chunks):
                for j in range(nf):
                    ps = mm_group(wfg[f0 + j], KT, hf, [rg.mg], MH)
                    op("act", lambda: nc.scalar.activation(out=V2(tA[:], 1), in_=pm(ps, H), func=AF.Silu),
                       reads=[rg.pm[ps]], writes=[rg.tA])
                    ps = mm_group(wfu[f0 + j], KT, hf, [rg.mg], MH)
                    op("dve", lambda: nc.vector.tensor_tensor(out=V2(hbuf[:, j, :], 1), in0=pm(ps, H), in1=V2(tA[:], 1),
                                                              op=ALU.mult), reads=[rg.pm[ps], rg.tA], writes=[rg.hbuf])
                hb = lambda kt: hbuf[:, kt, :]
                lastc = ci == len(chunks) - 1
                for nt_ in range(32):
                    ps = mm_group(wfd[nt_][:, f0 * 128:(f0 + nf) * 128], nf, hb, [rg.hbuf], MH)
                    if ci == 0:
                        op("act", lambda: nc.scalar.activation(out=V2(OG[:, nt_, :], 1), in_=pm(ps, H), func=AF.Copy),
                           reads=[rg.pm[ps]], writes=[rg.OG])
                    else:
                        op("dve", lambda: nc.vector.tensor_tensor(out=V2(OG[:, nt_, :], 1), in0=pm(ps, H),
                                                                  in1=V2(OG[:, nt_, :], 1), op=ALU.add),
                           reads=[rg.pm[ps], rg.OG], writes=[rg.OG])
                    if lastc:
                        op("act", lambda: nc.scalar.activation(out=tC[:], in_=OG[:, nt_, :], func=AF.Square),
                           reads=[rg.OG], writes=[rg.tC])
                        for h in range(2):
                            op("pe", lambda: nc.tensor.matmul(PS[:, 4 + h, 0:H], lhsT=ones[:], rhs=tC[:, h * H:(h + 1) * H],
                                                              start=(nt_ == 0), stop=(nt_ == 31)),
                               reads=[rg.ones, rg.tC], writes=[rg.pb[h]], signal=True)
                        op("act", lambda: nc.scalar.activation(out=OG[:, nt_, :], in_=OG[:, nt_, :], func=AF.Copy,
                                                               scale=gcols[:, 3, nt_:nt_ + 1]),
                           reads=[rg.OG, rg.gcols], writes=[rg.OG])
            rstdb_from_ps()
            for mt in range(32):
                op("dve", lambda: nc.vector.tensor_tensor(out=OG[:, mt, :], in0=OG[:, mt, :], in1=rstdb[:], op=ALU.mult),
                   reads=[rg.OG, rg.rstdb], writes=[rg.OG])
            S.barrier()
            for ti, (r0, sz) in enumerate(tiles_of(NT)):
                buf, breg = XT[ti % 2], rg.XT[ti % 2]
                dma("sp", buf[0:sz, :], x1s[r0:r0 + sz, :], reads=[rg.x1s], writes=[breg])
                for g4 in range(8):
                    for j in range(4):
                        kt = g4 * 4 + j
                        op("pe", lambda: nc.tensor.transpose(PS[0:sz, 7, j * 128:(j + 1) * 128], OG[:, kt, r0:r0 + sz], ident[:]),
                           reads=[rg.OG, rg.ident], writes=[rg.pt])
                    op("dve", lambda: nc.vector.tensor_tensor(out=buf[0:sz, g4 * 512:(g4 + 1) * 512],
                                                              in0=buf[0:sz, g4 * 512:(g4 + 1) * 512], in1=PS[0:sz, 7, :],
                                                              op=ALU.add), reads=[breg, rg.pt], writes=[breg])
                dma("sp", y_out[blk][r0:r0 + sz, :], buf[0:sz, :], reads=[breg])

        def prefix_block(blk):
            S.barrier()
            prep(xp[blk], NP)
            if _DBG_SUB == 1:
                return
            hn = lambda kt: hnT[:, kt, :]
            for mt in range(16 if _DBG_SUB != 2 else 1):
                ps = mm_group(win[mt], KT, hn, [rg.hnT], PH)
                op("act", lambda: nc.scalar.activation(out=ygT[:, mt, 0:NP].rearrange("p (h n) -> p h n", h=2),
                                                       in_=pm(ps, HP), func=AF.Copy), reads=[rg.pm[ps]], writes=[rg.yg])
            if blk == 1:
                for ct in range(16):
                    ps = mm_group(win[16 + ct], KT, hn, [rg.hnT], [(NP - 2, NP)])
                    op("act", lambda: nc.scalar.activation(out=tA[:, 0:2], in_=PS[:, 2 * ps, 0:2], func=AF.Copy),
                       reads=[rg.pm[ps]], writes=[rg.tA])
                    ps = mm_group(win[48 + ct], KT, hn, [rg.hnT], [(NP - 2, NP)])
                    op("dve", lambda: nc.vector.tensor_tensor(out=Zhist[:, ct, :], in0=PS[:, 2 * ps, 0:2], in1=tA[:, 0:2],
                                                              op=ALU.mult), reads=[rg.pm[ps], rg.tA], writes=[rg.Zhist])
            if _DBG_SUB in (2, 3):
                return
            S.barrier()
            ssm_block(ygT, rg.yg, NP, blk, False, [])

        setup()
        if _DBG_STAGE >= 1:
            S.barrier()
            for f in prep_tiles(xp[0], NP, XTP, rg.XTP):
                f()
            for f in uproj_groups(ygT, rg.yg, PH, HP):
                f()
            S.barrier()
            microA = prep_tiles(xp[1], NP, XTP, rg.XTP) + uproj_groups(ycT, rg.yc, PH, HP) + convhist_groups()
            ssm_block(ygT, rg.yg, NP, 0, False, microA)
        if _DBG_STAGE >= 2:
            microB = prep_tiles(xm[0], NT, XTP, rg.XTP) + uproj_groups(ygT, rg.yg, MH, H)
            ssm_block(ycT, rg.yc, NP, 1, False, microB)
        if _DBG_STAGE >= 3:
            main_block(0, skip_front=True)
        if _DBG_STAGE >= 4:
            main_block(1)
        S.barrier()
        store_state_T([Xp[:, 1, c, :] for c in range(2)], pst_out, 0, 64)
        for k in range(2):
            dma("sp", pcv_out[k].rearrange("(c p) -> p c", p=128), Zhist[:, :, k], reads=[rg.Zhist],
                allow_slow_non_contiguous=True)
        for ct in range(16):
            op("pe", lambda: nc.tensor.transpose(PS[0:32, 7, ct * 128 % 512:ct * 128 % 512 + 128],
                                                 ZsOut[:, ct].rearrange("p b k -> p (b k)"), ident[:]),
               reads=[rg.ZsOut, rg.ident], writes=[rg.pt])
            if ct % 4 == 3:
                g = ct // 4
                op("act", lambda: nc.scalar.activation(out=xtok[0:32, g * 512:(g + 1) * 512], in_=PS[0:32, 7, :], func=AF.Copy),
                   reads=[rg.pt], writes=[rg.xtok])
        dma("sp", scv_out, xtok[0:32, 0:2048], reads=[rg.xtok])
        sp = S.E["sp"]
        for s in S.sp_slots:
            if s.val:
                S._wait(sp, (s, s.val))
        for n in ("pe", "act", "dve"):
            e = S.E[n]
            if e.sem.val:
                S._wait(sp, (e.sem, e.sem.val))
    return nc


def _tile_w(w, nkt, nmt):
    return np.ascontiguousarray(w.reshape(nkt, 128, nmt, 128).transpose(2, 1, 0, 3)).reshape(nmt, 128, nkt * 128)


_NC_CACHE = {}


def kernel(x_prompt, x_sample, state_ssm_re, state_ssm_im, state_conv, meta_tokens, g_pre_mix, w_in,
           ssm_lambda_re, ssm_lambda_im, ssm_log_dt, ssm_b_re, ssm_b_im, ssm_c_re, ssm_c_im, ssm_d,
           w_glu_v, w_glu_g, conv_w, w_conv_out, w_o, g_post_mix, g_pre_ffn, w_ffn_gate, w_ffn_up,
           w_ffn_down, g_post_ffn):
    f32 = np.float32
    A = lambda a: np.asarray(a, dtype=f32)
    x_prompt, x_sample = A(x_prompt), A(x_sample)
    meta = A(meta_tokens)
    shared = {}
    shared["ident"] = np.eye(128, dtype=f32)
    gc = np.stack([A(g)[0].reshape(32, 128).T for g in (g_pre_mix, g_post_mix, g_pre_ffn, g_post_ffn)], axis=1)
    shared["gcols"] = np.ascontiguousarray(gc).reshape(128, 128)
    shared["convw"] = np.ascontiguousarray(A(conv_w)[0].reshape(3, 16, 128).transpose(2, 0, 1)).reshape(128, 48)
    shared["dcol"] = np.ascontiguousarray(A(ssm_d)[0].reshape(16, 128).T)
    lre, lim, ldt = A(ssm_lambda_re)[0], A(ssm_lambda_im)[0], A(ssm_log_dt)[0]
    l1f = lambda a: np.ascontiguousarray(a.reshape(64, 2, 64).transpose(1, 2, 0)).reshape(128, 64)
    ldt_gp = np.broadcast_to(ldt[:, None], (128, 64))
    shared["l1"] = np.stack([l1f(lre), l1f(lim), l1f(ldt_gp)])
    def l2_bcast(a):
        v = a.reshape(16, 4, 2, 64)
        o = np.broadcast_to(v[:, :, None, None, :, :], (16, 4, 2, 16, 2, 64))
        return np.ascontiguousarray(o.transpose(1, 2, 3, 0, 4, 5)).reshape(128, 2048)
    def l2_b(b):
        v = b.reshape(16, 4, 2, 64, 16)
        o = np.zeros((16, 4, 2, 16, 2, 64), f32)
        for g2 in range(2):
            o[:, :, g2, :, g2, :] = v[:, :, g2].transpose(0, 1, 3, 2)
        return np.ascontiguousarray(o.transpose(1, 2, 3, 0, 4, 5)).reshape(128, 2048)
    shared["l2"] = np.stack([l2_bcast(lre), l2_bcast(lim), l2_bcast(ldt_gp), l2_b(A(ssm_b_re)[0]), l2_b(A(ssm_b_im)[0])])
    cm = np.zeros((2, 64, 64, 2, 2, 16), f32)
    for c, carr in enumerate((A(ssm_c_re)[0], A(ssm_c_im)[0])):
        v = carr.reshape(64, 2, 16, 64)
        for g2 in range(2):
            cm[g2, :, :, c, g2, :] = v[:, g2].transpose(2, 0, 1)
    shared["cm"] = cm.reshape(128, 4096)
    shared["win"] = _tile_w(A(w_in)[0], 32, 128)
    shared["wgv"] = _tile_w(A(w_glu_v)[0], 16, 32)
    shared["wgg"] = _tile_w(A(w_glu_g)[0], 16, 32)
    shared["wco"] = _tile_w(A(w_conv_out)[0], 16, 32)
    shared["wo"] = _tile_w(A(w_o)[0], 32, 32)
    shared["wfg"] = _tile_w(A(w_ffn_gate)[0], 32, FT)
    shared["wfu"] = _tile_w(A(w_ffn_up)[0], 32, FT)
    shared["wfd"] = _tile_w(A(w_ffn_down)[0], FT, 32)

    sre, sim, scv = A(state_ssm_re)[0], A(state_ssm_im)[0], A(state_conv)[0]
    in_maps = []
    for c in range(8):
        bq, half = c // 2, c % 2
        full = np.concatenate([meta, x_prompt[bq]], axis=0)
        if half == 0:
            main_p = full[0:1032]
            pre = np.zeros((1032, D), f32)
        else:
            main_p = full[1032:2064]
            pre = full[0:1032]
        xs = x_sample[16 * c:16 * c + 16].reshape(2, 64, D)
        xm_ = np.concatenate([main_p.reshape(2, NP, D), xs], axis=1)
        m = dict(shared)
        m["xm"] = np.ascontiguousarray(xm_)
        m["xp"] = np.ascontiguousarray(pre.reshape(2, NP, D))
        m["sst_re_in"] = np.ascontiguousarray(sre[16 * c:16 * c + 16]).reshape(16 * 64, 128)
        m["sst_im_in"] = np.ascontiguousarray(sim[16 * c:16 * c + 16]).reshape(16 * 64, 128)
        m["scv_in"] = np.ascontiguousarray(scv[16 * c:16 * c + 16]).reshape(32, SW)
        in_maps.append(m)

    if "nc" not in _NC_CACHE:
        _NC_CACHE["nc"] = build_nc()
    nc = _NC_CACHE["nc"]
    res = run_bass_kernel_spmd(nc, in_maps, core_ids=list(range(8)))
    R = res.results

    y_prompt = np.zeros((4, 2048, D), f32)
    y_sample = np.zeros((128, 8, D), f32)
    p_re = np.zeros((1, 4, 128, 64), f32)
    p_im = np.zeros((1, 4, 128, 64), f32)
    p_cv = np.zeros((1, 4, 2, SW), f32)
    s_re = np.zeros((1, 128, 128, 64), f32)
    s_im = np.zeros((1, 128, 128, 64), f32)
    s_cv = np.zeros((1, 128, 2, SW), f32)
    for c in range(8):
        bq, half = c // 2, c % 2
        y = R[c]["y"]
        yp = y[:, 0:NP].reshape(1032, D)
        if half == 0:
            y_prompt[bq, 0:1016] = yp[16:]
        else:
            y_prompt[bq, 1016:2048] = yp
            p_re[0, bq] = R[c]["pst_re"].reshape(128, 64)
            p_im[0, bq] = R[c]["pst_im"].reshape(128, 64)
            p_cv[0, bq] = R[c]["pcv"]
        y_sample[16 * c:16 * c + 16] = y[:, NP:NT].reshape(16, 8, D)
        s_re[0, 16 * c:16 * c + 16] = R[c]["sst_re"].reshape(16, 128, 64)
        s_im[0, 16 * c:16 * c + 16] = R[c]["sst_im"].reshape(16, 128, 64)
        s_cv[0, 16 * c:16 * c + 16] = R[c]["scv"].reshape(16, 2, SW)
    return (y_prompt, y_sample, p_re, p_im, p_cv, s_re, s_im, s_cv)
```
